# Optimizing a Trainium2 kernel written in Bass

```python
import math
import jax, jax.numpy as jnp
from jax import lax
import numpy as np

D_MODEL = 2048
BATCH = 4
SEQ = 4096
DEPTH = 4

DIFF_HEADS = 8
DIFF_HD = 64
DIFF_VD = 2 * DIFF_HD
DIFF_W = DIFF_HEADS * DIFF_VD
Q_BLOCK = 128
ROPE_THETA = 10000.0
LRU_W = 512
LRU_BLOCKS = 8
LRU_BD = LRU_W // LRU_BLOCKS
CONV_W = 4
LRU_C = 8.0
GLA_HEADS = 4
GLA_DK = 64
GLA_DV = 128
GLA_KW = GLA_HEADS * GLA_DK
GLA_VW = GLA_HEADS * GLA_DV
GLA_RANK = 16
GLA_NORMALIZER = 16.0
GLA_CHUNK = 64
D_MIX = DIFF_W + LRU_W + GLA_VW
IN_SIZES = (DIFF_HEADS * 2 * DIFF_HD, DIFF_HEADS * 2 * DIFF_HD, DIFF_W,
            LRU_W, LRU_W,
            GLA_KW, GLA_KW, GLA_VW, GLA_VW, GLA_RANK)
N_IN = sum(IN_SIZES)
D_FF = -(-8 * D_MODEL // (3 * 256)) * 256
NORM_EPS = 1e-6

kernel_name = 'hybrid_diffattn_rglru_gla_parallel_heads'


def rms_norm(x, w, eps=NORM_EPS):
    x32 = x.astype(jnp.float32)
    y = x32 * lax.rsqrt(jnp.mean(x32 * x32, axis=-1, keepdims=True) + eps)
    return (y * w.astype(jnp.float32)).astype(x.dtype)


def rope(x, pos):
    d = x.shape[-1]
    inv = ROPE_THETA ** (-jnp.arange(0, d, 2, dtype=jnp.float32) / d)
    ang = pos.astype(jnp.float32)[:, None] * inv[None, :]
    bshape = (pos.shape[0],) + (1,) * (x.ndim - 3) + (d,)
    cos = jnp.concatenate([jnp.cos(ang), jnp.cos(ang)], -1).reshape(bshape)
    sin = jnp.concatenate([jnp.sin(ang), jnp.sin(ang)], -1).reshape(bshape)
    x32 = x.astype(jnp.float32)
    x1, x2 = jnp.split(x32, 2, axis=-1)
    rot = jnp.concatenate([-x2, x1], -1)
    return (x32 * cos + rot * sin).astype(x.dtype)


def diff_attention(q, k, v, lam, subln_w, lam_init):
    B, S, H, _, d = q.shape
    nb = S // Q_BLOCK
    qh = jnp.transpose(q, (0, 2, 3, 1, 4)).astype(jnp.float32) * (d ** -0.5)
    kh = jnp.transpose(k, (0, 2, 3, 1, 4)).astype(jnp.float32)
    vh = jnp.transpose(v, (0, 2, 1, 3)).astype(jnp.float32)
    qb = qh.reshape(B, H, 2, nb, Q_BLOCK, d).transpose(3, 0, 1, 2, 4, 5)
    kpos = jnp.arange(S)

    def block(args):
        q_blk, i = args
        s = jnp.einsum('bhmqd,bhmkd->bhmqk', q_blk, kh)
        qpos = i * Q_BLOCK + jnp.arange(Q_BLOCK)
        s = jnp.where(kpos[None, :] <= qpos[:, None], s, -jnp.inf)
        p = jax.nn.softmax(s, axis=-1)
        a = p[:, :, 0] - lam * p[:, :, 1]
        return jnp.einsum('bhqk,bhkv->bhqv', a, vh)

    o = lax.map(block, (qb, jnp.arange(nb)))
    o = o.transpose(1, 0, 3, 2, 4).reshape(B, S, H, 2 * d)
    o = rms_norm(o, subln_w) * (1.0 - lam_init)
    return o.reshape(B, S, H * 2 * d).astype(v.dtype)


def rglru_branch(xg, xr, conv_w, conv_b, w_a, b_a, w_x, b_x, lru_lambda):
    B, S, C = xr.shape
    gate = jax.nn.gelu(xg)
    xc = lax.conv_general_dilated(xr, conv_w[:, None, :], window_strides=(1,),
                                  padding=((CONV_W - 1, 0),),
                                  dimension_numbers=('NWC', 'WIO', 'NWC'),
                                  feature_group_count=C) + conv_b
    xb = xc.reshape(B, S, LRU_BLOCKS, LRU_BD)
    r = jax.nn.sigmoid(jnp.einsum('bsnc,ncd->bsnd', xb, w_a).reshape(B, S, C) + b_a)
    i = jax.nn.sigmoid(jnp.einsum('bsnc,ncd->bsnd', xb, w_x).reshape(B, S, C) + b_x)
    log_a = -LRU_C * r.astype(jnp.float32) * jax.nn.softplus(-lru_lambda.astype(jnp.float32))
    a = jnp.exp(log_a)
    u = jnp.sqrt(-jnp.expm1(2.0 * log_a)) * (i * xc).astype(jnp.float32)

    def combine(left, right):
        a1, b1 = left
        a2, b2 = right
        return a1 * a2, a2 * b1 + b2

    _, h = lax.associative_scan(combine, (a, u), axis=1)
    return h.astype(xr.dtype) * gate


def gla_branch(q, k, v, g_out, lr, w_gup, b_g, norm_w):
    B, S, _ = q.shape
    H, dk, dv, C = GLA_HEADS, GLA_DK, GLA_DV, GLA_CHUNK
    N = S // C
    gk = jax.nn.log_sigmoid((lr @ w_gup + b_g).astype(jnp.float32)) / GLA_NORMALIZER

    def heads(t, d):
        return t.astype(jnp.float32).reshape(B, N, C, H, d).transpose(0, 3, 1, 2, 4)

    qh = heads(q, dk) * (dk ** -0.5)
    kh = heads(k, dk)
    vh = heads(v, dv)
    bcum = jnp.cumsum(heads(gk, dk), axis=3)
    q_e = qh * jnp.exp(bcum)
    k_e = kh * jnp.exp(-bcum)
    causal = jnp.tril(jnp.ones((C, C), dtype=bool))
    att = jnp.where(causal, jnp.einsum('bhncd,bhnjd->bhncj', q_e, k_e), 0.0)
    o_intra = jnp.einsum('bhncj,bhnjv->bhncv', att, vh)
    b_last = bcum[:, :, :, -1:, :]
    kv = jnp.einsum('bhncd,bhncv->bhndv', kh * jnp.exp(b_last - bcum), vh)
    decay = jnp.exp(b_last[:, :, :, 0, :])

    def step(state, inp):
        dec, kv_n = inp
        return dec[..., None] * state + kv_n, state

    _, s_prev = lax.scan(step, jnp.zeros((B, H, dk, dv), jnp.float32),
                         (decay.transpose(2, 0, 1, 3), kv.transpose(2, 0, 1, 3, 4)))
    o_inter = jnp.einsum('bhncd,nbhdv->bhncv', q_e, s_prev)
    o = (o_intra + o_inter).transpose(0, 2, 3, 1, 4).reshape(B, S, H, dv)
    o = rms_norm(o, norm_w).reshape(B, S, H * dv)
    return (o * jax.nn.silu(g_out.astype(jnp.float32))).astype(q.dtype)


def setup_inputs(seed: int = 0) -> dict:
    key = jax.random.key(seed)
    ks = jax.random.split(key, 32)
    f32 = jnp.float32
    L = DEPTH

    def nrm(k, shape, scale):
        return jax.random.normal(k, shape, f32) * scale

    def gain(k, shape):
        return 1.0 + 0.01 * jax.random.normal(k, shape, f32)

    a_pow_c = jax.random.uniform(ks[18], (L, LRU_W), f32, 0.9, 0.999)
    log_a = jnp.log(a_pow_c) / LRU_C
    lru_lambda = log_a - jnp.log(-jnp.expm1(log_a))
    return {
        'x': jax.random.normal(ks[0], (BATCH, SEQ, D_MODEL), f32),
        'pre_mix_norm': gain(ks[1], (L, D_MODEL)),
        'post_mix_norm': gain(ks[2], (L, D_MODEL)),
        'pre_ffn_norm': gain(ks[3], (L, D_MODEL)),
        'post_ffn_norm': gain(ks[4], (L, D_MODEL)),
        'w_in': nrm(ks[5], (L, D_MODEL, N_IN), D_MODEL ** -0.5),
        'w_out': nrm(ks[6], (L, D_MIX, D_MODEL), D_MIX ** -0.5),
        'lambda_q1': nrm(ks[7], (L, DIFF_HD), 0.1),
        'lambda_k1': nrm(ks[8], (L, DIFF_HD), 0.1),
        'lambda_q2': nrm(ks[9], (L, DIFF_HD), 0.1),
        'lambda_k2': nrm(ks[10], (L, DIFF_HD), 0.1),
        'diff_subln': gain(ks[11], (L, DIFF_VD)),
        'conv_w': nrm(ks[12], (L, CONV_W, LRU_W), CONV_W ** -0.5),
        'conv_b': nrm(ks[13], (L, LRU_W), 0.01),
        'w_rgate': nrm(ks[14], (L, LRU_BLOCKS, LRU_BD, LRU_BD), LRU_BD ** -0.5),
        'b_rgate': nrm(ks[15], (L, LRU_W), 0.01),
        'w_igate': nrm(ks[16], (L, LRU_BLOCKS, LRU_BD, LRU_BD), LRU_BD ** -0.5),
        'b_igate': nrm(ks[17], (L, LRU_W), 0.01),
        'lru_lambda': lru_lambda,
        'w_gla_gate_up': nrm(ks[19], (L, GLA_RANK, GLA_KW), GLA_RANK ** -0.5),
        'b_gla_gate': nrm(ks[20], (L, GLA_KW), 0.01),
        'gla_norm': gain(ks[21], (L, GLA_DV)),
        'w_ffn_gate': nrm(ks[22], (L, D_MODEL, D_FF), D_MODEL ** -0.5),
        'w_ffn_up': nrm(ks[23], (L, D_MODEL, D_FF), D_MODEL ** -0.5),
        'w_ffn_down': nrm(ks[24], (L, D_FF, D_MODEL), D_FF ** -0.5),
    }


def reference(x, pre_mix_norm, post_mix_norm, pre_ffn_norm, post_ffn_norm, w_in, w_out,
              lambda_q1, lambda_k1, lambda_q2, lambda_k2, diff_subln,
              conv_w, conv_b, w_rgate, b_rgate, w_igate, b_igate, lru_lambda,
              w_gla_gate_up, b_gla_gate, gla_norm,
              w_ffn_gate, w_ffn_up, w_ffn_down):
    B, S, _ = x.shape
    pos = jnp.arange(S, dtype=jnp.int32)
    split_points = np.cumsum(IN_SIZES)[:-1].tolist()
    f32 = jnp.float32
    for l in range(DEPTH):
        lam_init = 0.8 - 0.6 * math.exp(-0.3 * l)
        h = rms_norm(x, pre_mix_norm[l])
        proj = jnp.einsum('bsd,dn->bsn', h, w_in[l])
        (q_d, k_d, v_d, lru_g, lru_x, g_q, g_k, g_v, g_o, g_lr) = jnp.split(proj, split_points, axis=-1)
        q_d = rope(q_d.reshape(B, S, DIFF_HEADS, 2, DIFF_HD), pos)
        k_d = rope(k_d.reshape(B, S, DIFF_HEADS, 2, DIFF_HD), pos)
        v_d = v_d.reshape(B, S, DIFF_HEADS, DIFF_VD)
        lam = (jnp.exp(jnp.sum(lambda_q1[l].astype(f32) * lambda_k1[l].astype(f32)))
               - jnp.exp(jnp.sum(lambda_q2[l].astype(f32) * lambda_k2[l].astype(f32)))
               + lam_init)
        y_attn = diff_attention(q_d, k_d, v_d, lam, diff_subln[l], lam_init).astype(x.dtype)
        y_lru = rglru_branch(lru_g, lru_x, conv_w[l], conv_b[l], w_rgate[l], b_rgate[l],
                             w_igate[l], b_igate[l], lru_lambda[l]).astype(x.dtype)
        y_gla = gla_branch(g_q, g_k, g_v, g_o, g_lr, w_gla_gate_up[l], b_gla_gate[l],
                           gla_norm[l]).astype(x.dtype)
        y_cat = jnp.concatenate([y_attn, y_lru, y_gla], axis=-1)
        mix = jnp.einsum('bsm,md->bsd', y_cat, w_out[l])
        x = x + rms_norm(mix, post_mix_norm[l])
        h = rms_norm(x, pre_ffn_norm[l])
        hid = jax.nn.silu(jnp.einsum('bsd,df->bsf', h, w_ffn_gate[l])) * jnp.einsum('bsd,df->bsf', h, w_ffn_up[l])
        ff = jnp.einsum('bsf,fd->bsd', hid, w_ffn_down[l])
        x = x + rms_norm(ff, post_ffn_norm[l])
    return x
```

```python
import math
import numpy as np
import ml_dtypes
import concourse.bass as bass
import concourse.mybir as mybir
from concourse.bass_utils import run_bass_kernel_spmd

F32 = mybir.dt.float32
BF16 = mybir.dt.bfloat16
AF = mybir.ActivationFunctionType
ALU = mybir.AluOpType

D = 2048
B = 4
S = 4096
L = 4
T = 2048
NCORES = 8
HC = 2944
NHC = HC // 128
DFF = 5632
NFF = DFF // 128
KC = D // 128
EPS = 1e-6
RG = [[0, 1], [2, 3], [4, 5], [6, 7]]

R_Q, R_K, R_V, R_LG, R_LX, R_GQ, R_GK, R_GV, R_GO, R_GLR = 0, 512, 1024, 1536, 1792, 2048, 2176, 2304, 2560, 2816
YH = 1024
Y_ATT, Y_LRU, Y_GLA = 0, 512, 768


class Buf:
    __slots__ = ("name", "last_w", "readers")

    def __init__(self, name):
        self.name = name
        self.last_w = None
        self.readers = []


class _Op:
    __slots__ = ("q", "fn", "deps", "kind", "sig", "sem", "val", "slot_prev")

    def __init__(self, q, fn, deps, kind):
        self.q = q
        self.fn = fn
        self.deps = deps
        self.kind = kind
        self.sig = False
        self.sem = None
        self.val = 0
        self.slot_prev = None


QUEUES = ("pe", "act", "dve", "pool", "sp")
EPOCH = 30000


class Sched:
    def __init__(self, nc):
        self.nc = nc
        self.ops = []
        self.nslots = {"sp": 16, "act": 6, "pool": 8, "pe": 2, "dve": 2}

    def op(self, q, fn, reads=(), writes=(), kind="c"):
        idx = len(self.ops)
        deps = set()
        for b in reads:
            if b.last_w is not None:
                deps.add(b.last_w)
        for b in writes:
            if b.last_w is not None:
                deps.add(b.last_w)
            deps.update(b.readers)
        for b in reads:
            b.readers.append(idx)
        for b in writes:
            b.last_w = idx
            b.readers = []
        deps.discard(idx)
        self.ops.append(_Op(q, fn, deps, kind))
        return idx

    def dma(self, q, out, in_, reads=(), writes=(), **kw):
        def fn(e):
            o = out(e) if callable(out) else out
            i = in_(e) if callable(in_) else in_
            return e.dma_start(out=o, in_=i, **kw)
        return self.op(q, fn, reads, writes, kind="d")

    def emit(self):
        nc = self.nc
        ops = self.ops
        for o in ops:
            best = {}
            keep = set()
            for d in o.deps:
                p = ops[d]
                if p.kind == "c":
                    if p.q == "pe" and o.q == "pe":
                        continue
                    if p.q not in best or best[p.q] < d:
                        best[p.q] = d
                else:
                    keep.add(d)
            o.deps = keep | set(best.values())
            for d in o.deps:
                ops[d].sig = True
        sems = {}

        def get_sem(name):
            if name not in sems:
                sems[name] = nc.alloc_semaphore(name)
            return sems[name]

        ccount = {q: 0 for q in QUEUES}
        dcount = {q: 0 for q in QUEUES}
        xcount = 0
        slot_tot = {}
        slot_last = {}
        for i, o in enumerate(ops):
            if o.kind == "c":
                if not o.sig:
                    continue
                ccount[o.q] += 1
                ep = ccount[o.q] // EPOCH
                o.sem = get_sem(f"c_{o.q}_{ep}")
                o.val = ccount[o.q] - ep * EPOCH + (1 if ep > 0 else 0)
                if ep > 0:
                    o.val = ccount[o.q] - ep * EPOCH + 1
            elif o.kind == "d":
                k = dcount[o.q] % self.nslots[o.q]
                dcount[o.q] += 1
                name = f"d_{o.q}_{k}"
                o.sem = get_sem(name)
                slot_tot[name] = slot_tot.get(name, 0) + 16
                o.val = slot_tot[name]
                o.slot_prev = slot_last.get(name)
                slot_last[name] = i
                o.sig = True
            else:
                k = xcount % 4
                xcount += 1
                name = f"x_{k}"
                o.sem = get_sem(name)
                slot_tot[name] = slot_tot.get(name, 0) + 1
                o.val = slot_tot[name]
                o.slot_prev = slot_last.get(name)
                slot_last[name] = i
                o.sig = True
        per_q = {q: [] for q in QUEUES}
        for i, o in enumerate(ops):
            per_q[o.q].append(i)
        final_waits = [(o.sem, o.val) for o in (ops[i] for i in slot_last.values())]
        self.n_inst = len(ops)

        def run_queue(q, e):
            known = {}
            for i in per_q[q]:
                o = ops[i]
                deps = set(o.deps)
                if o.slot_prev is not None:
                    deps.add(o.slot_prev)
                need = {}
                for d in deps:
                    p = ops[d]
                    key = id(p.sem)
                    if key not in need or need[key][1] < p.val:
                        need[key] = (p.sem, p.val)
                for key, (sem, val) in need.items():
                    if known.get(key, 0) >= val:
                        continue
                    e.wait_ge(sem, val)
                    known[key] = val
                ins = o.fn(e)
                if o.sig:
                    if o.kind == "d":
                        ins.then_inc(o.sem, 16)
                    elif o.kind == "x":
                        ins.then_inc(o.sem)
                    else:
                        ins.then_inc(o.sem, 1)
            if q == "sp":
                for sem, val in final_waits:
                    e.wait_ge(sem, val)

        with nc.Block() as block:
            @block.tensor
            def _(e):
                run_queue("pe", e)

            @block.scalar
            def _(e):
                run_queue("act", e)

            @block.vector
            def _(e):
                run_queue("dve", e)

            @block.gpsimd
            def _(e):
                run_queue("pool", e)

            @block.sync
            def _(e):
                run_queue("sp", e)


class Tile:
    __slots__ = ("ap", "buf")

    def __init__(self, ap, buf):
        self.ap = ap
        self.buf = buf


class Arena:
    def __init__(self, base_ap, nwords):
        self.base = base_ap
        self.nwords = nwords
        self.top = 0
        self.live = []
        self.retired = []
        self.marks = []

    def alloc(self, name, free_shape, dtype):
        n = 1
        for s in free_shape:
            n *= s
        words = (n + 1) // 2 if dtype == BF16 else n
        words = (words + 7) // 8 * 8
        start = self.top
        end = start + words
        assert end <= self.nwords, f"SBUF arena overflow allocating {name}: {end*4} > {self.nwords*4}"
        self.top = end
        buf = Buf(name)
        for (s0, e0, b0) in self.retired:
            if s0 < end and start < e0:
                if b0.last_w is not None:
                    buf.readers.append(b0.last_w)
                buf.readers.extend(b0.readers)
        self.live.append((start, end, buf))
        ap = self.base[:, start:end]
        if dtype == BF16:
            ap = ap.bitcast(BF16)[:, 0:n]
        else:
            ap = ap[:, 0:n]
        if len(free_shape) == 2:
            ap = ap.rearrange("p (a b) -> p a b", a=free_shape[0])
        elif len(free_shape) == 3:
            ap = ap.rearrange("p (a b c) -> p a b c", a=free_shape[0], b=free_shape[1])
        return Tile(ap, buf)

    def mark(self):
        self.marks.append((self.top, len(self.live)))

    def release(self):
        top, nl = self.marks.pop()
        self.retired.extend(self.live[nl:])
        del self.live[nl:]
        self.top = top


class Ctx:
    pass


def dense(cx, name, w_ap, n_oc, n_kg, KG, A, Tn, evac, scale=None, banks=(2, 3, 4, 5), wq="sp", oc_order=None):
    S, ar = cx.S, cx.arena
    ar.mark()
    wst = [ar.alloc(f"{name}_wst{j}", [KG, 128], F32) for j in range(2)]
    wb = [ar.alloc(f"{name}_wb{j}", [KG, 128], BF16) for j in range(2)]
    steps = [(oc, kg) for oc in (oc_order if oc_order is not None else range(n_oc)) for kg in range(n_kg)]
    ntg = Tn // 512

    def load(i):
        oc, kg = steps[i]
        t = wst[i % 2]
        S.dma(wq, t.ap, w_ap[oc, kg].rearrange("p (k n) -> p k n", k=KG), writes=[t.buf])

    def cast(i):
        oc, kg = steps[i]
        src, dst = wst[i % 2], wb[i % 2]
        if scale is None:
            S.op("act", lambda e, d=dst, s=src: e.copy(out=d.ap, in_=s.ap), reads=[src.buf], writes=[dst.buf])
        else:
            sc = scale.ap[:, kg * KG:(kg + 1) * KG].unsqueeze(2).broadcast_to([128, KG, 128])
            S.op("dve", lambda e, d=dst, s=src, sc=sc: e.tensor_tensor(out=d.ap, in0=s.ap, in1=sc, op=ALU.mult),
                 reads=[src.buf, scale.buf], writes=[dst.buf])

    load(0)
    cast(0)
    if len(steps) > 1:
        load(1)
    bi = 0
    cur_bank = {}
    for i, (oc, kg) in enumerate(steps):
        if i + 1 < len(steps):
            cast(i + 1)
        if i + 2 < len(steps):
            load(i + 2)
        w = wb[i % 2]
        for tg in range(ntg):
            if kg == 0:
                cur_bank[tg] = banks[bi % len(banks)]
                bi += 1
            bk = cur_bank[tg]
            ps = cx.psum[bk]
            for kc in range(KG):
                kk = kg * KG + kc
                S.op("pe", lambda e, ps=ps, w=w, kc=kc, kk=kk, tg=tg, st=(kk == 0), sp=(kk == n_kg * KG - 1):
                     e.matmul(ps.ap, lhsT=w.ap[:, kc, :], rhs=A.ap[:, kk, tg * 512:(tg + 1) * 512], start=st, stop=sp),
                     reads=[w.buf, A.buf], writes=[ps.buf])
            if kg == n_kg - 1:
                evac(oc, tg, bk)
    ar.release()


def dbuf(cx, name, key=0):
    k = (name, key)
    if k not in cx.dbufs:
        cx.dbufs[k] = Buf(f"{name}_{key}")
    return cx.dbufs[k]


def load_x_pass(cx, x_dram, t0, xg):
    xv = x_dram.rearrange("(k p) t -> p k t", p=128)
    for g in range(4):
        cx.S.dma("sp", xg[g].ap, xv[:, 4 * g:4 * g + 4, t0:t0 + 1024], reads=[dbuf(cx, "x", x_dram.tensor.name)], writes=[xg[g].buf])


def rms_stats(cx, src_fn, rstd, c0, bank):
    S = cx.S
    ps = cx.psum[bank]
    for kc in range(KC):
        sqt = cx.sq[kc % 2]
        ap, buf = src_fn(kc)
        S.op("act", lambda e, o=sqt, a=ap: e.activation(out=o.ap, in_=a, func=AF.Square), reads=[buf], writes=[sqt.buf])
        S.op("pe", lambda e, o=sqt, kc=kc: e.matmul(ps.ap, lhsT=cx.ones.ap, rhs=o.ap, start=(kc == 0), stop=(kc == KC - 1)),
             reads=[sqt.buf, cx.ones.buf], writes=[ps.buf])
    r = rstd.ap[:, c0:c0 + 512]
    S.op("act", lambda e: e.activation(out=r, in_=ps.ap, func=AF.Sqrt, scale=1.0 / D, bias=cx.epsc.ap), reads=[ps.buf, cx.epsc.buf], writes=[rstd.buf])
    S.op("dve", lambda e: e.reciprocal(out=r, in_=r), reads=[rstd.buf], writes=[rstd.buf])


def norm_to_bf16(cx, x_dram, hT, rstd, Tn, tbase):
    S, ar = cx.S, cx.arena
    for ps_ in range(Tn // 1024):
        ar.mark()
        xg = [ar.alloc(f"xg{g}", [4, 1024], F32) for g in range(4)]
        load_x_pass(cx, x_dram, tbase + ps_ * 1024, xg)
        for tg in range(2):
            c0 = ps_ * 1024 + tg * 512
            rms_stats(cx, lambda kc, tg=tg: (xg[kc // 4].ap[:, kc % 4, tg * 512:(tg + 1) * 512], xg[kc // 4].buf), rstd, c0, tg)
            rb = rstd.ap[:, c0:c0 + 512].unsqueeze(1).broadcast_to([128, 4, 512])
            for g in range(4):
                S.op("dve", lambda e, g=g, tg=tg, c0=c0, rb=rb: e.tensor_tensor(
                    out=hT.ap[:, 4 * g:4 * g + 4, c0:c0 + 512], in0=xg[g].ap[:, :, tg * 512:(tg + 1) * 512], in1=rb, op=ALU.mult),
                    reads=[xg[g].buf, rstd.buf], writes=[hT.buf])
        ar.release()


def phase_A(cx, l, x_dram, mine_dst, send_dst, oc_order=None, post_store=None):
    S, ar = cx.S, cx.arena
    ar.mark()
    hT = ar.alloc("hT", [KC, T], BF16)
    gcol = ar.alloc("gpre", [KC], F32)
    rstd = ar.alloc("rstdA", [T], F32)
    S.dma("sp", gcol.ap, cx.d_pre_mix[l], writes=[gcol.buf])
    norm_to_bf16(cx, x_dram, hT, rstd, T, 0)
    ob = [ar.alloc(f"obA{j}", [512], BF16) for j in range(4)]
    cnt = [0]

    def evac(oc, tg, bk):
        o = ob[cnt[0] % 4]
        ps = cx.psum[bk]
        if cnt[0] % 2 == 0:
            S.op("act", lambda e: e.copy(out=o.ap, in_=ps.ap), reads=[ps.buf], writes=[o.buf])
        else:
            S.op("dve", lambda e: e.tensor_copy(out=o.ap, in_=ps.ap), reads=[ps.buf], writes=[o.buf])
        cnt[0] += 1
        dst, db = mine_dst(oc, tg) if oc < NHC else send_dst(oc - NHC, tg)
        S.dma("sp", dst, o.ap, reads=[o.buf], writes=[db])
        if post_store is not None and tg == T // 512 - 1:
            post_store(oc)

    dense(cx, "inproj", cx.d_w_in[l], 2 * NHC, 1, KC, hT, T, evac, scale=gcol, banks=(2, 3, 4, 5, 6, 7), oc_order=oc_order)
    ar.release()


def make_ctx(nc, stack):
    cx = Ctx()
    cx.nc = nc
    cx.S = Sched(nc)
    cx.dbufs = {}
    NW = 49152
    base = stack.enter_context(nc.sbuf_tensor("arena", [128, NW], F32))
    cx.arena = Arena(base, NW)
    cx.psum = []
    for i in range(8):
        t = stack.enter_context(nc.psum_tensor(f"psb{i}", [128, 512], F32))
        cx.psum.append(Tile(t[:, :], Buf(f"psum{i}")))
    cx.ones = cx.arena.alloc("ones", [128], F32)
    cx.epsc = cx.arena.alloc("epsc", [1], F32)
    cx.sq = [cx.arena.alloc(f"sq{j}", [512], F32) for j in range(2)]
    cx.S.op("pool", lambda e: e.memset(cx.ones.ap, 1.0), writes=[cx.ones.buf])
    cx.S.op("pool", lambda e: e.memset(cx.epsc.ap, EPS), writes=[cx.epsc.buf])
    return cx


def tile_w(w, KG):
    K, N = w.shape
    n_oc, n_kc = N // 128, K // 128
    n_kg = n_kc // KG
    return np.ascontiguousarray(w.reshape(n_kg, KG, 128, n_oc, 128).transpose(3, 0, 2, 1, 4)).reshape(n_oc, n_kg, 128, KG * 128)


def half_cols(h):
    idx = []
    idx += list(range(0 + 512 * h, 0 + 512 * h + 512))
    idx += list(range(1024 + 512 * h, 1024 + 512 * h + 512))
    idx += list(range(2048 + 512 * h, 2048 + 512 * h + 512))
    idx += list(range(3072 + 256 * h, 3072 + 256 * h + 256))
    idx += list(range(3584 + 256 * h, 3584 + 256 * h + 256))
    idx += list(range(4096 + 128 * h, 4096 + 128 * h + 128))
    idx += list(range(4352 + 128 * h, 4352 + 128 * h + 128))
    idx += list(range(4608 + 256 * h, 4608 + 256 * h + 256))
    idx += list(range(5120 + 256 * h, 5120 + 256 * h + 256))
    idx += list(range(5632, 5648))
    return idx


def w_in_layout(w, r):
    out = np.zeros((D, 2 * HC), np.float32)
    out[:, 0:2832] = w[:, half_cols(r)]
    out[:, HC:HC + 2832] = w[:, half_cols(1 - r)]
    return tile_w(out, KC)


def gcol_layout(g):
    return np.ascontiguousarray(g.reshape(-1, 128).T)


def proj_post_residual(cx, name, w_ap, n_kg, KG, A, g_dram_ap, x_in, x_out, t0, h_out=None):
    S, ar = cx.S, cx.arena
    Tn = 1024
    ar.mark()
    rstd = ar.alloc(f"{name}_rstd", [Tn], F32)
    gcol = ar.alloc(f"{name}_g", [KC], F32)
    S.dma("sp", gcol.ap, g_dram_ap, writes=[gcol.buf])
    mo = [ar.alloc(f"{name}_mo{j}", [512], F32) for j in range(3)]
    cnt = [0]
    spill = cx.d_spill

    def evac(oc, tg, bk):
        o = mo[cnt[0] % 3]
        cnt[0] += 1
        ps = cx.psum[bk]
        S.op("act", lambda e: e.copy(out=o.ap, in_=ps.ap), reads=[ps.buf], writes=[o.buf])
        sqt = cx.sq[cnt[0] % 2]
        S.op("dve", lambda e: e.tensor_tensor(out=sqt.ap, in0=ps.ap, in1=o.ap, op=ALU.mult), reads=[ps.buf, o.buf], writes=[sqt.buf])
        acc = cx.psum[tg]
        S.op("pe", lambda e: e.matmul(acc.ap, lhsT=cx.ones.ap, rhs=sqt.ap, start=(oc == 0), stop=(oc == KC - 1)),
             reads=[sqt.buf, cx.ones.buf], writes=[acc.buf])
        S.dma("sp", spill[oc * 128:(oc + 1) * 128, tg * 512:(tg + 1) * 512], o.ap, reads=[o.buf], writes=[dbuf(cx, "spill", oc)])

    dense(cx, name, w_ap, KC, n_kg, KG, A, Tn, evac, banks=(2, 3, 4, 5, 6, 7))
    for tg in range(2):
        r = rstd.ap[:, tg * 512:(tg + 1) * 512]
        acc = cx.psum[tg]
        S.op("act", lambda e, r=r, acc=acc: e.activation(out=r, in_=acc.ap, func=AF.Sqrt, scale=1.0 / D, bias=cx.epsc.ap),
             reads=[acc.buf, cx.epsc.buf], writes=[rstd.buf])
        S.op("dve", lambda e, r=r: e.reciprocal(out=r, in_=r), reads=[rstd.buf], writes=[rstd.buf])
    xp = None
    if h_out is not None:
        xp = ar.alloc(f"{name}_xp", [KC, Tn], F32)
    mt = [ar.alloc(f"{name}_mt{j}", [Tn], F32) for j in range(2)]
    xt = [ar.alloc(f"{name}_xt{j}", [Tn], F32) for j in range(2)]
    xo = [ar.alloc(f"{name}_xo{j}", [Tn], F32) for j in range(2)] if xp is None else None
    for kc in range(KC):
        m, x = mt[kc % 2], xt[kc % 2]
        S.dma("sp", m.ap, spill[kc * 128:(kc + 1) * 128, :], reads=[dbuf(cx, "spill", kc)], writes=[m.buf])
        S.dma("sp", x.ap, x_in[kc * 128:(kc + 1) * 128, t0:t0 + Tn], reads=[dbuf(cx, "x", x_in.tensor.name)], writes=[x.buf])
        S.op("pool", lambda e, m=m: e.tensor_tensor(out=m.ap, in0=m.ap, in1=rstd.ap, op=ALU.mult), reads=[m.buf, rstd.buf], writes=[m.buf])
        if xp is not None:
            dst_ap, dst_buf = xp.ap[:, kc, :], xp.buf
        else:
            dst_ap, dst_buf = xo[kc % 2].ap, xo[kc % 2].buf
        S.op("dve", lambda e, m=m, x=x, kc=kc, d=dst_ap: e.scalar_tensor_tensor(out=d, in0=m.ap, scalar=gcol.ap[:, kc:kc + 1], in1=x.ap, op0=ALU.mult, op1=ALU.add),
             reads=[m.buf, x.buf, gcol.buf], writes=[dst_buf])
        S.dma("sp", x_out[kc * 128:(kc + 1) * 128, t0:t0 + Tn], dst_ap, reads=[dst_buf], writes=[dbuf(cx, "x", x_out.tensor.name)])
    if h_out is not None:
        rstd2 = ar.alloc(f"{name}_rstd2", [Tn], F32)
        for tg in range(2):
            rms_stats(cx, lambda kc, tg=tg: (xp.ap[:, kc, tg * 512:(tg + 1) * 512], xp.buf), rstd2, tg * 512, tg)
            rb = rstd2.ap[:, tg * 512:(tg + 1) * 512].unsqueeze(1).broadcast_to([128, 4, 512])
            for g in range(4):
                S.op("dve", lambda e, g=g, tg=tg, rb=rb: e.tensor_tensor(
                    out=h_out.ap[:, 4 * g:4 * g + 4, tg * 512:(tg + 1) * 512], in0=xp.ap[:, 4 * g:4 * g + 4, tg * 512:(tg + 1) * 512], in1=rb, op=ALU.mult),
                    reads=[xp.buf, rstd2.buf], writes=[h_out.buf])
    ar.release()


def ffn_hidden(cx, l, h2T, hidT):
    S, ar = cx.S, cx.arena
    ar.mark()
    gcol = ar.alloc("gffn", [KC], F32)
    S.dma("sp", gcol.ap, cx.d_pre_ffn[l], writes=[gcol.buf])
    sg = [ar.alloc(f"sg{j}", [512], F32) for j in range(4)]
    state = {}
    cnt = [0]

    def evac(oc2, tg, bk):
        ps = cx.psum[bk]
        if oc2 % 2 == 0:
            s = sg[cnt[0] % 4]
            cnt[0] += 1
            state[tg] = s
            S.op("act", lambda e: e.activation(out=s.ap, in_=ps.ap, func=AF.Silu), reads=[ps.buf], writes=[s.buf])
        else:
            s = state[tg]
            oc = oc2 // 2
            S.op("dve", lambda e: e.tensor_tensor(out=hidT.ap[:, oc, tg * 512:(tg + 1) * 512], in0=ps.ap, in1=s.ap, op=ALU.mult),
                 reads=[ps.buf, s.buf], writes=[hidT.buf])

    dense(cx, "gu", cx.d_w_gu[l], 2 * NFF, 1, KC, h2T, 1024, evac, scale=gcol, banks=(2, 3, 4, 5, 6, 7))
    ar.release()


def phase_C(cx, l, x_in, x1_dram, x_out, y_load):
    S, ar = cx.S, cx.arena
    for p in range(2):
        t0 = p * 1024
        ar.mark()
        h2T = ar.alloc("h2T", [KC, 1024], BF16)
        ar.mark()
        yT = ar.alloc("ycatT", [KC, 1024], BF16)
        for kc in range(KC):
            y_load(kc, t0, yT)
        proj_post_residual(cx, "op", cx.d_w_out[l], 1, KC, yT, cx.d_post_mix[l], x_in, x1_dram, t0, h_out=h2T)
        ar.release()
        hidT = ar.alloc("hidT", [NFF, 1024], BF16)
        ffn_hidden(cx, l, h2T, hidT)
        proj_post_residual(cx, "dn", cx.d_w_down[l], 2, 22, hidT, cx.d_post_ffn[l], x1_dram, x_out, t0)
        ar.release()


def ycat_half_cols(h):
    return list(range(512 * h, 512 * h + 512)) + list(range(1024 + 256 * h, 1024 + 256 * h + 256)) + list(range(1536 + 256 * h, 1536 + 256 * h + 256))


def w_out_layout(w, r):
    rows = ycat_half_cols(r) + ycat_half_cols(1 - r)
    return tile_w(np.ascontiguousarray(w[rows, :]), KC)


def w_gu_layout(wg, wu):
    tg, tu = tile_w(wg, KC), tile_w(wu, KC)
    out = np.empty((2 * NFF,) + tg.shape[1:], np.float32)
    out[0::2] = tg
    out[1::2] = tu
    return out


def ln_rstd(cx, dst, src_ps, n_feat, eps_t):
    S = cx.S
    S.op("act", lambda e: e.activation(out=dst.ap, in_=src_ps.ap, func=AF.Ln, scale=1.0 / n_feat, bias=eps_t.ap),
         reads=[src_ps.buf, eps_t.buf], writes=[dst.buf])
    S.op("act", lambda e: e.activation(out=dst.ap, in_=dst.ap, func=AF.Exp, scale=-0.5), reads=[dst.buf], writes=[dst.buf])


def load_bcast_vec(cx, name, dram_vec_ap, n):
    t = cx.arena.alloc(name, [n], F32)
    cx.S.dma("sp", t.ap, dram_vec_ap.partition_broadcast(128), writes=[t.buf])
    return t


def attn_alloc(cx):
    ar = cx.arena
    at = {}
    at["raw"] = {nm: ar.alloc(f"a_{nm}", [S_LEN], BF16) for nm in ("q", "k", "v")}
    at["KR"] = ar.alloc("KR", [S_LEN], BF16)
    at["sets"] = [{"QR": ar.alloc(f"QR{j}", [S_LEN], BF16), "KRm": [ar.alloc(f"KRm{j}_{m}", [S_LEN], BF16) for m in range(2)],
                   "Vt": ar.alloc(f"Vtok{j}", [32, 128], BF16)} for j in range(2)]
    at["t1"] = [ar.alloc(f"rt1_{j}", [512], F32) for j in range(2)]
    at["t2"] = [ar.alloc(f"rt2_{j}", [512], F32) for j in range(2)]
    at["pt"] = [ar.alloc(f"pt{j}", [512], BF16) for j in range(6)]
    at["sacc"] = [ar.alloc(f"sacc{j}", [512], F32) for j in range(2)]
    at["rs"] = [ar.alloc(f"rs{j}", [512], F32) for j in range(2)]
    at["tn"] = [ar.alloc(f"tn{j}", [512], F32) for j in range(2)]
    at["o_t"] = ar.alloc("o_t", [512], F32)
    at["sq_t"] = ar.alloc("sq_t", [512], F32)
    at["rstd_t"] = ar.alloc("rstd_t", [512], F32)
    at["yb"] = [ar.alloc(f"yb{j}", [512], BF16) for j in range(2)]
    return at


def attn_prep(cx, hd, cs, at, st):
    S = cx.S
    raw, KR, QR, KRm, Vt = at["raw"], at["KR"], st["QR"], st["KRm"], st["Vt"]
    for nm, r0 in (("q", R_Q), ("k", R_K), ("v", R_V)):
        t = raw[nm]
        for hf in range(2):
            src, sb = cx.seq_src(r0 + hd * 128, 128, hf * 2048, 2048)
            S.dma("sp", t.ap[:, hf * 2048:(hf + 1) * 2048], src, reads=[sb], writes=[t.buf])
    yield
    ps = cx.psum[3]
    i = 0
    for src, dst in ((raw["q"], QR), (raw["k"], KR)):
        for tg in range(8):
            sl = slice(tg * 512, (tg + 1) * 512)
            a, b2 = at["t1"][i % 2], at["t2"][i % 2]
            i += 1
            S.op("pe", lambda e, src=src, sl=sl: e.matmul(ps.ap, lhsT=cs["rm"].ap, rhs=src.ap[:, sl], start=True, stop=True),
                 reads=[cs["rm"].buf, src.buf], writes=[ps.buf])
            S.op("dve", lambda e, a=a, src=src, sl=sl: e.tensor_tensor(out=a.ap, in0=src.ap[:, sl], in1=cs["cos"].ap[:, sl], op=ALU.mult),
                 reads=[src.buf, cs["cos"].buf], writes=[a.buf])
            S.op("dve", lambda e, b2=b2, sl=sl: e.tensor_tensor(out=b2.ap, in0=ps.ap, in1=cs["sin"].ap[:, sl], op=ALU.mult),
                 reads=[ps.buf, cs["sin"].buf], writes=[b2.buf])
            S.op("pool", lambda e, a=a, b2=b2, dst=dst, sl=sl: e.tensor_tensor(out=dst.ap[:, sl], in0=a.ap, in1=b2.ap, op=ALU.add),
                 reads=[a.buf, b2.buf], writes=[dst.buf])
            if dst is KR:
                for m in range(2):
                    S.op("pool", lambda e, m=m, sl=sl: e.tensor_scalar(out=KRm[m].ap[:, sl], in0=KR.ap[:, sl], scalar1=cs["hm"][m].ap, scalar2=None, op0=ALU.mult),
                         reads=[KR.buf, cs["hm"][m].buf], writes=[KRm[m].buf])
            yield
    psb = ps.ap.bitcast(BF16)
    for g in range(4):
        for j in range(8):
            tt = g * 8 + j
            S.op("pe", lambda e, j=j, tt=tt: e.transpose(psb[:, j * 128:(j + 1) * 128], raw["v"].ap[:, tt * 128:(tt + 1) * 128], cs["ident"].ap),
                 reads=[raw["v"].buf, cs["ident"].buf], writes=[ps.buf])
        S.op("act", lambda e, g=g: e.copy(out=Vt.ap[:, g * 8:(g + 1) * 8, :], in_=psb.rearrange("p (a b) -> p a b", a=8)),
             reads=[ps.buf], writes=[Vt.buf])
        yield


def attn_main(cx, hd, cs, at, st, bg=None, bg_per_qg=3):
    S = cx.S
    QR, KRm, Vt = st["QR"], st["KRm"], st["Vt"]
    pt, rs, tn, o_t, sq_t, rstd_t, yb, sacc = at["pt"], at["rs"], at["tn"], at["o_t"], at["sq_t"], at["rstd_t"], at["yb"], at["sacc"]
    for qg in range(8):
        blocks = [(kt, m) for kt in range(4 * qg + 4) for m in range(2)]
        last_kt = 4 * qg + 3

        def geom(kt):
            j = kt - 4 * qg
            q0 = max(j, 0) * 128
            return j, q0, 512 - q0

        STB = (0, 1, 2, 6, 7)
        LA = 4

        def ST(bi):
            kt, m = blocks[bi]
            j, q0, N = geom(kt)
            ps = cx.psum[STB[bi % 5]]
            S.op("pe", lambda e, qg=qg: e.matmul(ps.ap[:, 0:N], lhsT=KRm[m].ap[:, kt * 128:(kt + 1) * 128], rhs=QR.ap[:, qg * 512 + q0:(qg + 1) * 512], start=True, stop=True),
                 reads=[KRm[m].buf, QR.buf], writes=[ps.buf])

        for b0 in range(min(LA, len(blocks))):
            ST(b0)
        for bi, (kt, m) in enumerate(blocks):
            if bi + LA < len(blocks):
                ST(bi + LA)
            j, q0, N = geom(kt)
            ps = cx.psum[STB[bi % 5]]
            p = pt[bi % 6]
            S.op("act", lambda e, ps=ps, p=p, N=N: e.activation(out=p.ap[:, 0:N], in_=ps.ap[:, 0:N], func=AF.Exp, scale=0.125), reads=[ps.buf], writes=[p.buf])
            if j >= 0:
                S.op("pool", lambda e, p=p: e.tensor_tensor(out=p.ap[:, 0:128], in0=p.ap[:, 0:128], in1=cs["tri"].ap, op=ALU.mult),
                     reads=[p.buf, cs["tri"].buf], writes=[p.buf])
            po = cx.psum[4 + m]
            S.op("pe", lambda e, po=po, p=p, kt=kt, q0=q0, N=N, last_kt=last_kt: e.matmul(po.ap[:, q0:512], lhsT=Vt.ap[:, kt, :], rhs=p.ap[:, 0:N], start=(kt == 0), stop=(kt == last_kt)),
                 reads=[Vt.buf, p.buf], writes=[po.buf])
            sa = sacc[m]
            eng = "dve" if m == 0 else "pool"
            if kt == 0:
                S.op(eng, lambda e, sa=sa, p=p: e.tensor_copy(out=sa.ap, in_=p.ap), reads=[p.buf], writes=[sa.buf])
            else:
                S.op(eng, lambda e, sa=sa, p=p, q0=q0, N=N: e.tensor_tensor(out=sa.ap[:, q0:512], in0=sa.ap[:, q0:512], in1=p.ap[:, 0:N], op=ALU.add),
                     reads=[p.buf, sa.buf], writes=[sa.buf])
        for m in range(2):
            S.op("pe", lambda e, m=m: e.matmul(cx.psum[6 + m].ap, lhsT=cx.ones.ap, rhs=sacc[m].ap, start=True, stop=True),
                 reads=[cx.ones.buf, sacc[m].buf], writes=[cx.psum[6 + m].buf])
        for m in range(2):
            S.op("dve", lambda e, m=m: e.reciprocal(out=rs[m].ap, in_=cx.psum[6 + m].ap), reads=[cx.psum[6 + m].buf], writes=[rs[m].buf])
            S.op("dve", lambda e, m=m: e.tensor_tensor(out=tn[m].ap, in0=cx.psum[4 + m].ap, in1=rs[m].ap, op=ALU.mult),
                 reads=[cx.psum[4 + m].buf, rs[m].buf], writes=[tn[m].buf])
        S.op("dve", lambda e: e.scalar_tensor_tensor(out=o_t.ap, in0=tn[1].ap, scalar=cs["neg_lam"].ap, in1=tn[0].ap, op0=ALU.mult, op1=ALU.add),
             reads=[tn[0].buf, tn[1].buf, cs["neg_lam"].buf], writes=[o_t.buf])
        S.op("pool", lambda e: e.tensor_tensor(out=sq_t.ap, in0=o_t.ap, in1=o_t.ap, op=ALU.mult), reads=[o_t.buf], writes=[sq_t.buf])
        pn = cx.psum[0]
        S.op("pe", lambda e: e.matmul(pn.ap, lhsT=cx.ones.ap, rhs=sq_t.ap, start=True, stop=True), reads=[cx.ones.buf, sq_t.buf], writes=[pn.buf])
        ln_rstd(cx, rstd_t, pn, 128, cx.epsc)
        y = yb[qg % 2]
        S.op("dve", lambda e, y=y: e.scalar_tensor_tensor(out=y.ap, in0=o_t.ap, scalar=cs["subw"].ap, in1=rstd_t.ap, op0=ALU.mult, op1=ALU.mult),
             reads=[o_t.buf, cs["subw"].buf, rstd_t.buf], writes=[y.buf])
        dst, db = cx.y_dst(Y_ATT + hd * 128, qg * 512, 512)
        S.dma("sp", dst, y.ap, reads=[y.buf], writes=[db])
        if bg is not None:
            for _ in range(bg_per_qg):
                next(bg, None)
    if bg is not None:
        for _ in bg:
            pass


def attention_all(cx, l, cs, after_attn=None):
    S, ar = cx.S, cx.arena
    ar.mark()
    cos = ar.alloc("cos", [S_LEN], F32)
    sin = ar.alloc("sin", [S_LEN], F32)
    for t, d in ((cos, cx.d_cos), (sin, cx.d_sin)):
        for hf in range(2):
            S.dma("sp", t.ap[:, hf * 2048:(hf + 1) * 2048], d[:, hf * 2048:(hf + 1) * 2048], writes=[t.buf])
    cs["cos"], cs["sin"] = cos, sin
    at = attn_alloc(cx)
    for _ in attn_prep(cx, 0, cs, at, at["sets"][0]):
        pass
    for hd in range(4):
        bg = attn_prep(cx, hd + 1, cs, at, at["sets"][(hd + 1) % 2]) if hd < 3 else None
        attn_main(cx, hd, cs, at, at["sets"][hd % 2], bg)
    ar.release()


S_LEN = S

PM_CW, PM_CB, PM_BA, PM_BX, PM_LAM, PM_SUBLN, PM_GNORM, PM_BG = 0, 8, 10, 12, 14, 16, 17, 18
PM_BD = 32
PM_WG = 544
PM_LV = 672
NPM = 928
CB_RM, CB_ID, CB_TRI, CB_ONES, CB_GM = 0, 128, 256, 384, 512


def mixer_consts(cx, l):
    S, ar = cx.S, cx.arena
    cs = {}
    pm = ar.alloc("pm", [NPM], F32)
    S.dma("sp", pm.ap, cx.d_pm[l], writes=[pm.buf])
    cbf = ar.alloc("cbf", [640], BF16)
    S.dma("sp", cbf.ap, cx.d_cbf, writes=[cbf.buf])
    cs["pm"] = pm
    for nm, off in (("rm", CB_RM), ("ident", CB_ID), ("tri", CB_TRI), ("ones_bf", CB_ONES), ("gmask", CB_GM)):
        cs[nm] = Tile(cbf.ap[:, off:off + 128], cbf.buf)
    sm = ar.alloc("smallc", [16], F32)
    cs["sm"] = sm
    onec = ar.alloc("onec", [1], F32)
    S.op("pool", lambda e: e.memset(onec.ap, 1.0), writes=[onec.buf])
    cs["one"] = onec
    lam_init = 0.8 - 0.6 * math.exp(-0.3 * l)
    tmp = ar.alloc("lamtmp", [128], F32)
    for k in range(2):
        a = pm.ap[:, PM_LV + 128 * k:PM_LV + 128 * k + 64]
        b2 = pm.ap[:, PM_LV + 128 * k + 64:PM_LV + 128 * k + 128]
        S.op("dve", lambda e, a=a, b2=b2, k=k: e.tensor_tensor(out=tmp.ap[:, 64 * k:64 * k + 64], in0=a, in1=b2, op=ALU.mult), reads=[pm.buf], writes=[tmp.buf])
        S.op("dve", lambda e, k=k: e.reduce_sum(out=sm.ap[:, k:k + 1], in_=tmp.ap[:, 64 * k:64 * k + 64], axis=mybir.AxisListType.X), reads=[tmp.buf], writes=[sm.buf])
    S.op("act", lambda e: e.activation(out=sm.ap[:, 0:2], in_=sm.ap[:, 0:2], func=AF.Exp), reads=[sm.buf], writes=[sm.buf])
    S.op("dve", lambda e: e.tensor_tensor(out=sm.ap[:, 2:3], in0=sm.ap[:, 1:2], in1=sm.ap[:, 0:1], op=ALU.subtract), reads=[sm.buf], writes=[sm.buf])
    S.op("dve", lambda e: e.tensor_scalar(out=sm.ap[:, 2:3], in0=sm.ap[:, 2:3], scalar1=-lam_init, scalar2=None, op0=ALU.add), reads=[sm.buf], writes=[sm.buf])
    cs["neg_lam"] = Tile(sm.ap[:, 2:3], sm.buf)
    S.op("dve", lambda e: e.tensor_scalar(out=sm.ap[:, 3:4], in0=pm.ap[:, PM_SUBLN:PM_SUBLN + 1], scalar1=1.0 - lam_init, scalar2=None, op0=ALU.mult), reads=[pm.buf], writes=[sm.buf])
    cs["subw"] = Tile(sm.ap[:, 3:4], sm.buf)
    S.op("act", lambda e: e.activation(out=sm.ap[:, 4:6], in_=pm.ap[:, PM_LAM:PM_LAM + 2], func=AF.Exp, scale=-1.0), reads=[pm.buf], writes=[sm.buf])
    S.op("act", lambda e: e.activation(out=sm.ap[:, 4:6], in_=sm.ap[:, 4:6], func=AF.Ln, bias=onec.ap), reads=[sm.buf, onec.buf], writes=[sm.buf])
    S.op("dve", lambda e: e.tensor_scalar(out=sm.ap[:, 6:8], in0=sm.ap[:, 4:6], scalar1=-16.0, scalar2=None, op0=ALU.mult), reads=[sm.buf], writes=[sm.buf])
    S.op("dve", lambda e: e.tensor_scalar(out=sm.ap[:, 4:6], in0=sm.ap[:, 4:6], scalar1=-8.0, scalar2=None, op0=ALU.mult), reads=[sm.buf], writes=[sm.buf])
    S.op("dve", lambda e: e.tensor_scalar(out=sm.ap[:, 8:9], in0=pm.ap[:, PM_BG:PM_BG + 1], scalar1=-1.0, scalar2=None, op0=ALU.mult), reads=[pm.buf], writes=[sm.buf])
    S.op("pool", lambda e: e.memset(sm.ap[0:64, 9:10], 1.0), writes=[sm.buf])
    S.op("pool", lambda e: e.memset(sm.ap[64:128, 9:10], 0.0), writes=[sm.buf])
    S.op("pool", lambda e: e.memset(sm.ap[0:64, 10:11], 0.0), writes=[sm.buf])
    S.op("pool", lambda e: e.memset(sm.ap[64:128, 10:11], 1.0), writes=[sm.buf])
    cs["hm"] = [Tile(sm.ap[:, 9:10], sm.buf), Tile(sm.ap[:, 10:11], sm.buf)]
    return cs


def lru_chunk(cx, l, c, cs):
    S, ar = cx.S, cx.arena
    pm, sm = cs["pm"], cs["sm"]
    BL = 1024
    xgb = [ar.alloc(f"l{c}_xg{j}", [BL], BF16) for j in range(2)]
    xrb = [ar.alloc(f"l{c}_xr{j}", [BL + 8], BF16) for j in range(2)]
    names = ("w1", "gate", "xc", "r", "i", "a", "m", "u", "hs")
    f = {nm: ar.alloc(f"l{c}_{nm}", [BL], F32) for nm in names}
    hprev = ar.alloc(f"l{c}_hprev", [1], F32)
    yb = [ar.alloc(f"l{c}_yb{j}", [BL], BF16) for j in range(2)]
    bda = pm.ap[:, PM_BD + c * 128:PM_BD + (c + 1) * 128]
    bdx = pm.ap[:, PM_BD + (2 + c) * 128:PM_BD + (3 + c) * 128]
    col = lambda off: pm.ap[:, off:off + 1]
    for blk in range(S_LEN // BL):
        t0 = blk * BL
        xg, xr = xgb[blk % 2], xrb[blk % 2]
        src, sb = cx.seq_src(R_LG + c * 128, 128, t0, BL)
        S.dma("sp", xg.ap, src, reads=[sb], writes=[xg.buf])
        if blk == 0:
            S.op("pool", lambda e, xr=xr: e.memset(xr.ap[:, 0:3], 0.0), writes=[xr.buf])
            src, sb = cx.seq_src(R_LX + c * 128, 128, 0, BL)
            S.dma("sp", xr.ap[:, 3:3 + BL], src, reads=[sb], writes=[xr.buf])
        else:
            src, sb = cx.seq_src(R_LX + c * 128, 128, t0 - 3, BL + 3)
            S.dma("sp", xr.ap[:, 0:3 + BL], src, reads=[sb], writes=[xr.buf])
        w1, gate, xc, r, i_, a, m, u, hs = (f[n] for n in names)
        yield
        S.op("act", lambda e, xg=xg: e.activation(out=w1.ap, in_=xg.ap, func=AF.Square), reads=[xg.buf], writes=[w1.buf])
        S.op("dve", lambda e: e.tensor_scalar(out=w1.ap, in0=w1.ap, scalar1=0.044715, scalar2=1.0, op0=ALU.mult, op1=ALU.add), reads=[w1.buf], writes=[w1.buf])
        S.op("dve", lambda e, xg=xg: e.tensor_tensor(out=w1.ap, in0=w1.ap, in1=xg.ap, op=ALU.mult), reads=[w1.buf, xg.buf], writes=[w1.buf])
        S.op("act", lambda e: e.activation(out=w1.ap, in_=w1.ap, func=AF.Sigmoid, scale=1.5957691216057308), reads=[w1.buf], writes=[w1.buf])
        S.op("pool", lambda e, xg=xg: e.tensor_tensor(out=gate.ap, in0=w1.ap, in1=xg.ap, op=ALU.mult), reads=[w1.buf, xg.buf], writes=[gate.buf])
        yield
        S.op("dve", lambda e, xr=xr: e.tensor_scalar(out=xc.ap, in0=xr.ap[:, 3:3 + BL], scalar1=col(PM_CW + 4 * c + 3), scalar2=col(PM_CB + c), op0=ALU.mult, op1=ALU.add),
             reads=[xr.buf, pm.buf], writes=[xc.buf])
        for j in (2, 1, 0):
            S.op("dve", lambda e, xr=xr, j=j: e.scalar_tensor_tensor(out=xc.ap, in0=xr.ap[:, j:j + BL], scalar=col(PM_CW + 4 * c + j), in1=xc.ap, op0=ALU.mult, op1=ALU.add),
                 reads=[xr.buf, pm.buf, xc.buf], writes=[xc.buf])
        yield
        for tg in range(BL // 512):
            sl = slice(tg * 512, (tg + 1) * 512)
            for (bd, bias_off, dst, bk) in ((bda, PM_BA + c, r, 4 * c + tg), (bdx, PM_BX + c, i_, 4 * c + 2 + tg)):
                ps = cx.psum[bk]
                S.op("pe", lambda e, ps=ps, bd=bd, sl=sl: e.matmul(ps.ap, lhsT=bd, rhs=xc.ap[:, sl], start=True, stop=True), reads=[pm.buf, xc.buf], writes=[ps.buf])
                S.op("act", lambda e, ps=ps, dst=dst, sl=sl, bo=bias_off: e.activation(out=dst.ap[:, sl], in_=ps.ap, func=AF.Sigmoid, bias=col(bo)),
                     reads=[ps.buf, pm.buf], writes=[dst.buf])
        yield
        S.op("act", lambda e: e.activation(out=a.ap, in_=r.ap, func=AF.Exp, scale=sm.ap[:, 4 + c:5 + c]), reads=[r.buf, sm.buf], writes=[a.buf])
        S.op("act", lambda e: e.activation(out=m.ap, in_=r.ap, func=AF.Exp, scale=sm.ap[:, 6 + c:7 + c]), reads=[r.buf, sm.buf], writes=[m.buf])
        S.op("act", lambda e: e.activation(out=m.ap, in_=m.ap, func=AF.Sqrt, scale=-1.0, bias=cs["one"].ap), reads=[m.buf, cs["one"].buf], writes=[m.buf])
        S.op("pool", lambda e: e.tensor_tensor(out=u.ap, in0=m.ap, in1=i_.ap, op=ALU.mult), reads=[m.buf, i_.buf], writes=[u.buf])
        S.op("pool", lambda e: e.tensor_tensor(out=u.ap, in0=u.ap, in1=xc.ap, op=ALU.mult), reads=[u.buf, xc.buf], writes=[u.buf])
        yield
        init = 0.0 if blk == 0 else hprev.ap
        S.op("dve", lambda e, init=init: e.tensor_tensor_scan(out=hs.ap, data0=a.ap, data1=u.ap, initial=init, op0=ALU.mult, op1=ALU.add),
             reads=[a.buf, u.buf, hprev.buf], writes=[hs.buf])
        S.op("pool", lambda e: e.tensor_copy(out=hprev.ap, in_=hs.ap[:, BL - 1:BL]), reads=[hs.buf], writes=[hprev.buf])
        y = yb[blk % 2]
        S.op("pool", lambda e, y=y: e.tensor_tensor(out=y.ap, in0=hs.ap, in1=gate.ap, op=ALU.mult), reads=[hs.buf, gate.buf], writes=[y.buf])
        dst, db = cx.y_dst(Y_LRU + c * 128, t0, BL)
        S.dma("sp", dst, y.ap, reads=[y.buf], writes=[db])
        yield


def gla_heads(cx, l, cs):
    S, ar = cx.S, cx.arena
    pm, sm, hm = cs["pm"], cs["sm"], cs["hm"]
    ar.mark()
    qe = ar.alloc("g_qe", [S_LEN], BF16)
    qeh = [ar.alloc(f"g_qe{h}", [S_LEN], BF16) for h in range(2)]
    keh = [ar.alloc(f"g_ke{h}", [S_LEN], BF16) for h in range(2)]
    dec = ar.alloc("g_dec", [64], F32)
    KDh = [ar.alloc(f"g_KDt{h}", [32, 128], BF16) for h in range(2)]
    Vt = [ar.alloc(f"g_Vt{h}", [32, 128], BF16) for h in range(2)]
    ar.mark()
    kd = ar.alloc("g_kd", [S_LEN], BF16)
    ar.mark()
    gq = ar.alloc("g_q", [S_LEN], BF16)
    gk = ar.alloc("g_k", [S_LEN], BF16)
    for t, r0 in ((gq, R_GQ), (gk, R_GK)):
        for hf in range(2):
            src, sb = cx.seq_src(r0, 128, hf * 2048, 2048)
            S.dma("sp", t.ap[:, hf * 2048:(hf + 1) * 2048], src, reads=[sb], writes=[t.buf])
    lrb = [ar.alloc(f"g_lr{j}", [512], BF16) for j in range(2)]
    lrf = [ar.alloc(f"g_lrf{j}", [512], F32) for j in range(2)]
    Bt = ar.alloc("g_Bt", [S_LEN], F32)
    Bc = ar.alloc("g_Bc", [S_LEN], F32)
    E = ar.alloc("g_E", [S_LEN], F32)
    cm = E
    S.op("pool", lambda e: e.memset(cm.ap, 1.0), writes=[cm.buf])
    S.op("pool", lambda e: e.memset(cm.ap.rearrange("p (n c) -> p n c", c=64)[:, :, 0:1], 0.0), writes=[cm.buf])
    wg = pm.ap[0:16, PM_WG:PM_WG + 128]
    for tg in range(8):
        sl = slice(tg * 512, (tg + 1) * 512)
        a, b2 = lrb[tg % 2], lrf[tg % 2]
        src, sb = cx.seq_src(R_GLR, 16, tg * 512, 512)
        S.dma("sp", a.ap[0:16, :], src, reads=[sb], writes=[a.buf])
        S.op("pool", lambda e, a=a, b2=b2: e.tensor_copy(out=b2.ap[0:16, :], in_=a.ap[0:16, :]), reads=[a.buf], writes=[b2.buf])
        ps = cx.psum[tg % 4]
        S.op("pe", lambda e, ps=ps, b2=b2: e.matmul(ps.ap, lhsT=wg, rhs=b2.ap[0:16, :], start=True, stop=True), reads=[pm.buf, b2.buf], writes=[ps.buf])
        S.op("act", lambda e, ps=ps, sl=sl: e.activation(out=Bt.ap[:, sl], in_=ps.ap, func=AF.Exp, scale=-1.0, bias=sm.ap[:, 8:9]), reads=[ps.buf, sm.buf], writes=[Bt.buf])
    S.op("act", lambda e: e.activation(out=Bt.ap, in_=Bt.ap, func=AF.Ln, bias=cs["one"].ap), reads=[Bt.buf, cs["one"].buf], writes=[Bt.buf])
    S.op("dve", lambda e: e.tensor_tensor_scan(out=Bc.ap, data0=cm.ap, data1=Bt.ap, initial=0.0, op0=ALU.mult, op1=ALU.add), reads=[cm.buf, Bt.buf], writes=[Bc.buf])
    S.op("act", lambda e: e.activation(out=E.ap, in_=Bc.ap, func=AF.Exp, scale=-1.0 / 16.0), reads=[Bc.buf], writes=[E.buf])
    S.op("dve", lambda e: e.scalar_tensor_tensor(out=qe.ap, in0=gq.ap, scalar=0.125, in1=E.ap, op0=ALU.mult, op1=ALU.mult), reads=[gq.buf, E.buf], writes=[qe.buf])
    for h in range(2):
        S.op("pool", lambda e, h=h: e.tensor_scalar(out=qeh[h].ap, in0=qe.ap, scalar1=hm[h].ap, scalar2=None, op0=ALU.mult), reads=[qe.buf, hm[h].buf], writes=[qeh[h].buf])
    S.op("act", lambda e: e.activation(out=E.ap, in_=Bc.ap, func=AF.Exp, scale=1.0 / 16.0), reads=[Bc.buf], writes=[E.buf])
    for h in range(2):
        S.op("dve", lambda e, h=h: e.scalar_tensor_tensor(out=keh[h].ap, in0=gk.ap, scalar=hm[h].ap, in1=E.ap, op0=ALU.mult, op1=ALU.mult),
             reads=[gk.buf, hm[h].buf, E.buf], writes=[keh[h].buf])
    Bc3 = Bc.ap.rearrange("p (n c) -> p n c", c=64)
    S.op("dve", lambda e: e.tensor_tensor(out=E.ap.rearrange("p (n c) -> p n c", c=64), in0=Bc3, in1=Bc3[:, :, 63:64].broadcast_to([128, 64, 64]), op=ALU.subtract),
         reads=[Bc.buf], writes=[E.buf])
    S.op("act", lambda e: e.activation(out=E.ap, in_=E.ap, func=AF.Exp, scale=1.0 / 16.0), reads=[E.buf], writes=[E.buf])
    S.op("dve", lambda e: e.tensor_tensor(out=kd.ap, in0=gk.ap, in1=E.ap, op=ALU.mult), reads=[gk.buf, E.buf], writes=[kd.buf])
    S.op("act", lambda e: e.activation(out=dec.ap, in_=Bc3[:, :, 63], func=AF.Exp, scale=-1.0 / 16.0), reads=[Bc.buf], writes=[dec.buf])
    ar.release()
    gv = ar.alloc("g_v", [S_LEN], BF16)
    k = 0
    for (srcT, dsts, row0) in ((kd, KDh, None), (gv, [Vt[0]], R_GV), (gv, [Vt[1]], R_GV + 128)):
        if row0 is not None:
            for hf in range(2):
                src, sb = cx.seq_src(row0, 128, hf * 2048, 2048)
                S.dma("sp", gv.ap[:, hf * 2048:(hf + 1) * 2048], src, reads=[sb], writes=[gv.buf])
        for g in range(4):
            ps = cx.psum[4 + k % 2]
            k += 1
            psb = ps.ap.bitcast(BF16)
            for j in range(8):
                tt = g * 8 + j
                S.op("pe", lambda e, psb=psb, j=j, tt=tt, srcT=srcT: e.transpose(psb[:, j * 128:(j + 1) * 128], srcT.ap[:, tt * 128:(tt + 1) * 128], cs["ident"].ap),
                     reads=[srcT.buf, cs["ident"].buf], writes=[ps.buf])
            pv = psb.rearrange("p (a b) -> p a b", a=8)
            if row0 is None:
                for h in range(2):
                    S.op("dve", lambda e, pv=pv, g=g, h=h: e.tensor_scalar(out=KDh[h].ap[:, g * 8:(g + 1) * 8, :], in0=pv, scalar1=hm[h].ap, scalar2=None, op0=ALU.mult),
                         reads=[ps.buf, hm[h].buf], writes=[KDh[h].buf])
            else:
                S.op("act", lambda e, pv=pv, g=g, d=dsts[0]: e.copy(out=d.ap[:, g * 8:(g + 1) * 8, :], in_=pv), reads=[ps.buf], writes=[dsts[0].buf])
    ar.release()
    if globals().get("GLA_STOP", 9) <= 1.5:
        ar.release()
        return
    KV = ar.alloc("g_KV", [64, 128], F32)
    Sst = ar.alloc("g_Sst", [65, 128], F32)
    Spb = ar.alloc("g_Spb", [64, 128], BF16)
    for g in range(16):
        for h2 in range(2):
            ps = cx.psum[(2 * g + h2) % 4]
            hs_ = slice(h2 * 64, (h2 + 1) * 64)
            for q in range(4):
                n = 4 * g + q
                tt, hf = n // 2, n % 2
                S.op("pe", lambda e, ps=ps, h2=h2, q=q, tt=tt, hf=hf: e.matmul(
                    ps.ap[:, q * 128:(q + 1) * 128], lhsT=KDh[hf].ap[:, tt, :], rhs=Vt[h2].ap[:, tt, :], start=True, stop=True),
                    reads=[KDh[hf].buf, Vt[h2].buf], writes=[ps.buf])
            if h2 == 0:
                S.op("act", lambda e, ps=ps, g=g, hs_=hs_: e.copy(out=KV.ap[hs_, 4 * g:4 * g + 4, :], in_=ps.ap[hs_, :].rearrange("p (a b) -> p a b", a=4)), reads=[ps.buf], writes=[KV.buf])
            else:
                S.op("dve", lambda e, ps=ps, g=g, hs_=hs_: e.tensor_copy(out=KV.ap[hs_, 4 * g:4 * g + 4, :], in_=ps.ap[hs_, :].rearrange("p (a b) -> p a b", a=4)), reads=[ps.buf], writes=[KV.buf])
    if globals().get("GLA_STOP", 9) <= 1.7:
        ar.release()
        return
    S.op("pool", lambda e: e.memset(Sst.ap[:, 0, :], 0.0), writes=[Sst.buf])
    for n in range(63):
        S.op("dve", lambda e, n=n: e.scalar_tensor_tensor(out=Sst.ap[:, n + 1, :], in0=Sst.ap[:, n, :], scalar=dec.ap[:, n:n + 1], in1=KV.ap[:, n, :], op0=ALU.mult, op1=ALU.add),
             reads=[Sst.buf, dec.buf, KV.buf], writes=[Sst.buf])
    S.op("pool", lambda e: e.tensor_copy(out=Spb.ap, in_=Sst.ap[:, 0:64, :]), reads=[Sst.buf], writes=[Spb.buf])
    if globals().get("GLA_STOP", 9) <= 2:
        ar.release()
        return
    at4 = [ar.alloc(f"g_at{j}", [512], BF16) for j in range(2)]
    gob = [ar.alloc(f"g_go{j}", [512], BF16) for j in range(2)]
    o_t = ar.alloc("g_o", [512], F32)
    sq_t = ar.alloc("g_sq", [512], F32)
    rstd_t = ar.alloc("g_rstd", [512], F32)
    sil = ar.alloc("g_sil", [512], F32)
    yb = [ar.alloc(f"g_yb{j}", [512], BF16) for j in range(2)]
    it = 0
    for h2 in range(2):
        for g4 in range(8):
            pa, po, pn = cx.psum[it % 2], cx.psum[2 + it % 2], cx.psum[4 + it % 2]
            at, go, y = at4[it % 2], gob[it % 2], yb[it % 2]
            it += 1
            src, sb = cx.seq_src(R_GO + h2 * 128, 128, g4 * 512, 512)
            S.dma("sp", go.ap, src, reads=[sb], writes=[go.buf])
            for j in range(4):
                tt = 4 * g4 + j
                ts = slice(tt * 128, (tt + 1) * 128)
                S.op("pe", lambda e, pa=pa, j=j, ts=ts, h2=h2: e.matmul(pa.ap[:, j * 128:(j + 1) * 128], lhsT=keh[h2].ap[:, ts], rhs=qe.ap[:, ts], start=True, stop=True),
                     reads=[keh[h2].buf, qe.buf], writes=[pa.buf])
            S.op("dve", lambda e, pa=pa, at=at: e.tensor_tensor(out=at.ap.rearrange("p (a b) -> p a b", a=4), in0=pa.ap.rearrange("p (a b) -> p a b", a=4),
                                                                 in1=cs["gmask"].ap.unsqueeze(1).broadcast_to([128, 4, 128]), op=ALU.mult),
                 reads=[pa.buf, cs["gmask"].buf], writes=[at.buf])
            for j in range(4):
                tt = 4 * g4 + j
                S.op("pe", lambda e, po=po, j=j, tt=tt, at=at, h2=h2: e.matmul(po.ap[:, j * 128:(j + 1) * 128], lhsT=Vt[h2].ap[:, tt, :], rhs=at.ap[:, j * 128:(j + 1) * 128], start=True, stop=False),
                     reads=[Vt[h2].buf, at.buf], writes=[po.buf])
                for hf in range(2):
                    n = 2 * tt + hf
                    S.op("pe", lambda e, po=po, j=j, hf=hf, n=n, h2=h2: e.matmul(po.ap[:, j * 128 + hf * 64:j * 128 + hf * 64 + 64], lhsT=Spb.ap[:, n, :], rhs=qeh[h2].ap[:, n * 64:(n + 1) * 64],
                                                                            start=False, stop=(hf == 1)),
                         reads=[Spb.buf, qeh[h2].buf], writes=[po.buf])
            S.op("act", lambda e, po=po: e.copy(out=o_t.ap, in_=po.ap), reads=[po.buf], writes=[o_t.buf])
            S.op("pool", lambda e: e.tensor_tensor(out=sq_t.ap, in0=o_t.ap, in1=o_t.ap, op=ALU.mult), reads=[o_t.buf], writes=[sq_t.buf])
            S.op("pe", lambda e, pn=pn: e.matmul(pn.ap, lhsT=cx.ones.ap, rhs=sq_t.ap, start=True, stop=True), reads=[cx.ones.buf, sq_t.buf], writes=[pn.buf])
            ln_rstd(cx, rstd_t, pn, 128, cx.epsc)
            S.op("act", lambda e, go=go: e.activation(out=sil.ap, in_=go.ap, func=AF.Silu), reads=[go.buf], writes=[sil.buf])
            S.op("dve", lambda e: e.scalar_tensor_tensor(out=o_t.ap, in0=o_t.ap, scalar=pm.ap[:, PM_GNORM:PM_GNORM + 1], in1=rstd_t.ap, op0=ALU.mult, op1=ALU.mult),
                 reads=[o_t.buf, pm.buf, rstd_t.buf], writes=[o_t.buf])
            S.op("dve", lambda e, y=y: e.tensor_tensor(out=y.ap, in0=o_t.ap, in1=sil.ap, op=ALU.mult), reads=[o_t.buf, sil.buf], writes=[y.buf])
            dst, db = cx.y_dst(Y_GLA + h2 * 128, g4 * 512, 512)
            S.dma("sp", dst, y.ap, reads=[y.buf], writes=[db])
    ar.release()


def phase_B(cx, l, after_attn=None):
    S, ar = cx.S, cx.arena
    ar.mark()
    cs = mixer_consts(cx, l)
    attention_all(cx, l, cs)
    if after_attn is not None:
        after_attn()
    ar.mark()
    alive = [lru_chunk(cx, l, c, cs) for c in range(2)]
    while alive:
        for g in list(alive):
            try:
                next(g)
            except StopIteration:
                alive.remove(g)
    ar.release()
    gla_heads(cx, l, cs)
    ar.release()


def host_consts():
    pos = np.arange(S, dtype=np.float32)
    j = np.arange(128) % 64
    inv = (10000.0 ** (-(2.0 * (j % 32)).astype(np.float32) / 64.0)).astype(np.float32)
    ang = inv[:, None] * pos[None, :]
    cos = np.cos(ang).astype(np.float32)
    sin = np.sin(ang).astype(np.float32)
    cb = np.zeros((128, 640), np.float32)
    for m in range(128):
        jj = m % 64
        base = m - jj
        if jj < 32:
            cb[base + jj + 32, CB_RM + m] = -1.0
        else:
            cb[base + jj - 32, CB_RM + m] = 1.0
    cb[:, CB_ID:CB_ID + 128] = np.eye(128)
    kk = np.arange(128)
    cb[:, CB_TRI:CB_TRI + 128] = (kk[:, None] <= kk[None, :])
    cb[:, CB_ONES:CB_ONES + 128] = 1.0
    cb[:, CB_GM:CB_GM + 128] = (kk[:, None] <= kk[None, :]) & ((kk[:, None] // 64) == (kk[None, :] // 64))
    return cos, sin, cb.astype(ml_dtypes.bfloat16)


def pm_layout(P, l, r):
    pm = np.zeros((128, NPM), np.float32)
    for c in range(2):
        ch = slice(256 * r + 128 * c, 256 * r + 128 * (c + 1))
        pm[:, PM_CW + 4 * c:PM_CW + 4 * c + 4] = P['conv_w'][l][:, ch].T
        pm[:, PM_CB + c] = P['conv_b'][l][ch]
        pm[:, PM_BA + c] = P['b_rgate'][l][ch]
        pm[:, PM_BX + c] = P['b_igate'][l][ch]
        pm[:, PM_LAM + c] = P['lru_lambda'][l][ch]
        for k2, wn in ((0, 'w_rgate'), (2, 'w_igate')):
            for bb in range(2):
                blk = 4 * r + 2 * c + bb
                pm[64 * bb:64 * bb + 64, PM_BD + (k2 + c) * 128 + 64 * bb:PM_BD + (k2 + c) * 128 + 64 * bb + 64] = P[wn][l][blk]
    pm[:, PM_SUBLN] = P['diff_subln'][l]
    pm[:, PM_GNORM] = P['gla_norm'][l]
    pm[:, PM_BG] = P['b_gla_gate'][l][128 * r:128 * r + 128]
    pm[0:16, PM_WG:PM_WG + 128] = P['w_gla_gate_up'][l][:, 128 * r:128 * r + 128]
    for k2, nm in enumerate(('lambda_q1', 'lambda_k1', 'lambda_q2', 'lambda_k2')):
        pm[:, PM_LV + 64 * k2:PM_LV + 64 * k2 + 64] = P[nm][l][None, :]
    return pm


SEND_CH = [(0, 512), (512, 512), (1024, 512), (1536, 512), (2048, 512), (2560, 384)]


def build_program():
    from contextlib import ExitStack
    nc = bass.Bass("TRN2", target_bir_lowering=False)
    with ExitStack() as st:
        cx = make_ctx(nc, st)
        S_ = cx.S
        ext = lambda name, shape, dt: (nc.dram_tensor(name, shape, dt).ap() if globals().get("NOEXT") else nc.dram_tensor(name, shape, dt, kind="ExternalInput").ap())
        xT = ext("xT", [D, T], F32)
        def wl(name, shp):
            if globals().get("NOEXT"):
                return [nc.dram_tensor(f"{name}{l}", shp, F32).ap() for l in range(L)]
            t = ext(name, [L] + shp, F32)
            return [t[l] for l in range(L)]
        cx.d_w_in = wl("w_in", [2 * NHC, 1, 128, KC * 128])
        cx.d_w_out = wl("w_out", [KC, 1, 128, KC * 128])
        cx.d_w_gu = wl("w_gu", [2 * NFF, 1, 128, KC * 128])
        cx.d_w_down = wl("w_down", [KC, 2, 128, 22 * 128])
        gains = ext("gains", [L, 4, 128, KC], F32)
        pm = ext("pm", [L, 128, NPM], F32)
        cx.d_cbf = ext("cbf", [128, 640], BF16)
        cx.d_cos = ext("cos", [128, S], F32)
        cx.d_sin = ext("sin", [128, S], F32)
        out = nc.dram_tensor("out", [D, T], F32, kind="ExternalOutput").ap()
        cx.d_pre_mix = [gains[l, 0] for l in range(L)]
        cx.d_post_mix = [gains[l, 1] for l in range(L)]
        cx.d_pre_ffn = [gains[l, 2] for l in range(L)]
        cx.d_post_ffn = [gains[l, 3] for l in range(L)]
        cx.d_pm = [pm[l] for l in range(L)]
        cx.d_spill = nc.dram_tensor("spill", [D, 1024], F32).ap()
        x1d = nc.dram_tensor("x1d", [D, T], F32).ap()
        xa = nc.dram_tensor("xa", [D, T], F32).ap()
        xb = nc.dram_tensor("xb", [D, T], F32).ap()
        SEQR = 3072
        seq = nc.dram_tensor("seq", [SEQR, S], BF16).ap()
        mine1 = nc.dram_tensor("mine1", [HC, T], BF16).ap()
        send1 = nc.dram_tensor("send1", [SEQR, T], BF16).ap()
        recvall = nc.dram_tensor("recvall", [6, 1024, T], BF16).ap()
        yfull = nc.dram_tensor("yfull", [YH, S], BF16).ap()
        ymine = nc.dram_tensor("ymine", [YH, T], BF16).ap()
        ysend = nc.dram_tensor("ysend", [YH, T], BF16).ap()
        recv2all = nc.dram_tensor("recv2all", [2, 1024, T], BF16).ap()
        yoth = nc.dram_tensor("yoth", [YH, T], BF16).ap()
        seqh = seq.rearrange("r (h t) -> h r t", h=2)
        seqhj = seq.rearrange("(j r) (h t) -> h j r t", r=512, h=2)
        rva = recvall.rearrange("j (s r) t -> s j r t", s=2)
        yfh = yfull.rearrange("r (h t) -> h r t", h=2)
        rv2 = recv2all.rearrange("j (s r) t -> s j r t", s=2)
        pars = {}

        def par(e):
            k = id(e)
            if k not in pars:
                pars[k] = e.partition_id() % 2
            return pars[k]

        cx.seq_src = lambda row0, nrows, tok0, n: (seq[row0:row0 + nrows, tok0:tok0 + n], dbuf(cx, "seq", row0 // 128))
        cx.y_dst = lambda row0, tok0, n: (yfull[row0:row0 + 128, tok0:tok0 + n], dbuf(cx, "yfull", row0 // 128))
        mine_dst = lambda oc, tg: (mine1[oc * 128:(oc + 1) * 128, tg * 512:(tg + 1) * 512], dbuf(cx, "mine1", oc))
        send_dst = lambda oc, tg: (send1[oc * 128:(oc + 1) * 128, tg * 512:(tg + 1) * 512], dbuf(cx, "send1", (oc * 128) // 512))

        def y_load(kc, t0, yt):
            if kc < 8:
                S_.dma("sp", yt.ap[:, kc, :], ymine[kc * 128:(kc + 1) * 128, t0:t0 + 1024], reads=[dbuf(cx, "ymine")], writes=[yt.buf])
            else:
                S_.dma("sp", yt.ap[:, kc, :], yoth[(kc - 8) * 128:(kc - 7) * 128, t0:t0 + 1024], reads=[dbuf(cx, "yoth")], writes=[yt.buf])

        def copy_mine(r0, r1):
            S_.dma("pool", lambda e: seqh[bass.ds(par(e), 1), r0:r1, :].rearrange("h r t -> (h r) t"), mine1[r0:r1, :],
                   reads=[dbuf(cx, "mine1", oc) for oc in range(r0 // 128, r1 // 128)], writes=[dbuf(cx, "seq", oc) for oc in range(r0 // 128, r1 // 128)])

        def gather1(j):
            S_.op("pool", lambda e: e.collective_compute("AllGather", ALU.bypass, replica_groups=RG,
                                                          ins=[send1[512 * j:512 * (j + 1), :].opt()], outs=[recvall[j].opt()]),
                  reads=[dbuf(cx, "send1", j)], writes=[dbuf(cx, "recv1", j)], kind="x")

        def copy_recv(j0, j1):
            S_.dma("pool", lambda e: seqhj[bass.ds(1 - par(e), 1), j0:j1, :, :].rearrange("h j r t -> (h j) r t"),
                   lambda e: rva[bass.ds(1 - par(e), 1), j0:j1, :, :].rearrange("s j r t -> (s j) r t"),
                   reads=[dbuf(cx, "recv1", j) for j in range(j0, j1)], writes=[dbuf(cx, "seq", oc) for oc in range(4 * j0, 4 * j1)])

        def post_store_A(oc):
            if oc >= NHC:
                ocs = oc - NHC
                if ocs % 4 == 3 or ocs == NHC - 1:
                    j = ocs // 4
                    gather1(j)
                    if j == 2:
                        copy_recv(0, 3)
                    elif j == 5:
                        copy_recv(3, 6)
            elif oc == 11:
                copy_mine(0, 1536)
            elif oc == NHC - 1:
                copy_mine(1536, HC)

        def exchange_y(j):
            ybufs = [dbuf(cx, "yfull", k) for k in range(4 * j, 4 * j + 4)]
            rs_ = slice(512 * j, 512 * (j + 1))
            S_.dma("pool", ymine[rs_, :], lambda e: yfh[bass.ds(par(e), 1), rs_, :].rearrange("h r t -> (h r) t"), reads=ybufs, writes=[dbuf(cx, "ymine", j)])
            S_.dma("pool", ysend[rs_, :], lambda e: yfh[bass.ds(1 - par(e), 1), rs_, :].rearrange("h r t -> (h r) t"), reads=ybufs, writes=[dbuf(cx, "ysend", j)])
            S_.op("pool", lambda e: e.collective_compute("AllGather", ALU.bypass, replica_groups=RG,
                                                          ins=[ysend[rs_, :].opt()], outs=[recv2all[j].opt()]),
                  reads=[dbuf(cx, "ysend", j)], writes=[dbuf(cx, "recv2", j)], kind="x")
            S_.dma("pool", yoth[rs_, :], lambda e: rv2[bass.ds(1 - par(e), 1), j, :, :].rearrange("s r t -> (s r) t"),
                   reads=[dbuf(cx, "recv2", j)], writes=[dbuf(cx, "yoth", j)])

        def y_load(kc, t0, yt):
            j = (kc % 8) // 4
            if kc < 8:
                S_.dma("sp", yt.ap[:, kc, :], ymine[kc * 128:(kc + 1) * 128, t0:t0 + 1024], reads=[dbuf(cx, "ymine", j)], writes=[yt.buf])
            else:
                S_.dma("sp", yt.ap[:, kc, :], yoth[(kc - 8) * 128:(kc - 7) * 128, t0:t0 + 1024], reads=[dbuf(cx, "yoth", j)], writes=[yt.buf])

        order_A = list(range(NHC, 2 * NHC)) + list(range(NHC))
        x_cur = xT
        for l in range(L):
            phase_A(cx, l, x_cur, mine_dst, send_dst, oc_order=order_A, post_store=post_store_A)
            phase_B(cx, l, after_attn=lambda: exchange_y(0))
            exchange_y(1)
            x_next = out if l == L - 1 else (xa if l % 2 == 0 else xb)
            phase_C(cx, l, x_cur, x1d, x_next, y_load)
            x_cur = x_next
        cx.S.emit()
        n_ops = cx.S.n_inst
    return nc, n_ops


def make_in_maps(inp):
    P = {k: np.asarray(v, dtype=np.float32) for k, v in inp.items()}
    cos, sin, cbf = host_consts()
    w_in = [np.stack([w_in_layout(P['w_in'][l], r) for l in range(L)]) for r in range(2)]
    w_out = [np.stack([w_out_layout(P['w_out'][l], r) for l in range(L)]) for r in range(2)]
    w_gu = np.stack([w_gu_layout(P['w_ffn_gate'][l], P['w_ffn_up'][l]) for l in range(L)])
    w_dn = np.stack([tile_w(P['w_ffn_down'][l], 22) for l in range(L)])
    gains = np.stack([np.stack([gcol_layout(P[nm][l]) for nm in ('pre_mix_norm', 'post_mix_norm', 'pre_ffn_norm', 'post_ffn_norm')]) for l in range(L)])
    pm = [np.stack([pm_layout(P, l, r) for l in range(L)]) for r in range(2)]
    maps = []
    for c in range(NCORES):
        b, r = c // 2, c % 2
        maps.append({
            "xT": np.ascontiguousarray(P['x'][b, r * T:(r + 1) * T, :].T),
            "w_in": w_in[r], "w_out": w_out[r], "w_gu": w_gu, "w_down": w_dn,
            "gains": gains, "pm": pm[r], "cbf": cbf, "cos": cos, "sin": sin,
        })
    return maps


def kernel_fused(**inputs):
    nc, _ = build_program()
    maps = make_in_maps(inputs)
    res = run_bass_kernel_spmd(nc, maps, core_ids=list(range(NCORES)))
    outp = np.empty((B, S, D), np.float32)
    for c in range(NCORES):
        b, r = c // 2, c % 2
        outp[b, r * T:(r + 1) * T, :] = np.asarray(res.results[c]["out"], dtype=np.float32).T
    return outp


def _prog(builder):
    from contextlib import ExitStack
    nc = bass.Bass("TRN2", target_bir_lowering=False)
    with ExitStack() as st:
        cx = make_ctx(nc, st)
        builder(nc, cx)
        cx.S.emit()
    return nc


def build_A():
    def b(nc, cx):
        xT = nc.dram_tensor("xT", [D, T], F32, kind="ExternalInput").ap()
        w_in = nc.dram_tensor("w_in", [2 * NHC, 1, 128, KC * 128], F32, kind="ExternalInput").ap()
        gpre = nc.dram_tensor("gpre", [128, KC], F32, kind="ExternalInput").ap()
        mine = nc.dram_tensor("mine", [HC, T], BF16, kind="ExternalOutput").ap()
        send = nc.dram_tensor("send", [HC, T], BF16, kind="ExternalOutput").ap()
        cx.d_w_in = [w_in]
        cx.d_pre_mix = [gpre]
        md = lambda oc, tg: (mine[oc * 128:(oc + 1) * 128, tg * 512:(tg + 1) * 512], dbuf(cx, "mine", oc))
        sd = lambda oc, tg: (send[oc * 128:(oc + 1) * 128, tg * 512:(tg + 1) * 512], dbuf(cx, "send", oc))
        phase_A(cx, 0, xT, md, sd)
    return _prog(b)


def build_B(l):
    def b(nc, cx):
        seq = nc.dram_tensor("seq", [HC, S], BF16, kind="ExternalInput").ap()
        pm = nc.dram_tensor("pm", [128, NPM], F32, kind="ExternalInput").ap()
        cx.d_pm = {l: pm}
        cx.d_cbf = nc.dram_tensor("cbf", [128, 640], BF16, kind="ExternalInput").ap()
        cx.d_cos = nc.dram_tensor("cos", [128, S], F32, kind="ExternalInput").ap()
        cx.d_sin = nc.dram_tensor("sin", [128, S], F32, kind="ExternalInput").ap()
        yT = nc.dram_tensor("yT", [YH, S], BF16, kind="ExternalOutput").ap()
        cx.seq_src = lambda row0, nrows, tok0, n: (seq[row0:row0 + nrows, tok0:tok0 + n], dbuf(cx, "seq", 0))
        cx.y_dst = lambda row0, tok0, n: (yT[row0:row0 + 128, tok0:tok0 + n], dbuf(cx, "y", row0))
        phase_B(cx, l)
    return _prog(b)


def build_C():
    def b(nc, cx):
        xT = nc.dram_tensor("xT", [D, T], F32, kind="ExternalInput").ap()
        yT = nc.dram_tensor("yT", [D, T], BF16, kind="ExternalInput").ap()
        cx.d_w_out = [nc.dram_tensor("w_out", [KC, 1, 128, KC * 128], F32, kind="ExternalInput").ap()]
        cx.d_w_gu = [nc.dram_tensor("w_gu", [2 * NFF, 1, 128, KC * 128], F32, kind="ExternalInput").ap()]
        cx.d_w_down = [nc.dram_tensor("w_down", [KC, 2, 128, 22 * 128], F32, kind="ExternalInput").ap()]
        g = nc.dram_tensor("gains", [3, 128, KC], F32, kind="ExternalInput").ap()
        cx.d_post_mix, cx.d_pre_ffn, cx.d_post_ffn = [g[0]], [g[1]], [g[2]]
        cx.d_spill = nc.dram_tensor("spill", [D, 1024], F32).ap()
        x1 = nc.dram_tensor("x1", [D, T], F32).ap()
        x2 = nc.dram_tensor("x2", [D, T], F32, kind="ExternalOutput").ap()

        def y_load(kc, t0, yt):
            cx.S.dma("sp", yt.ap[:, kc, :], yT[kc * 128:(kc + 1) * 128, t0:t0 + 1024], writes=[yt.buf])
        phase_C(cx, 0, xT, x1, x2, y_load)
    return _prog(b)


def kernel_multi(**inputs):
    P = {k: np.asarray(v, dtype=np.float32) for k, v in inputs.items()}
    cos, sin, cbf = host_consts()
    cores = list(range(NCORES))
    xs = [np.ascontiguousarray(P['x'][c // 2, (c % 2) * T:(c % 2 + 1) * T, :].T) for c in cores]
    for l in range(L):
        wl = [w_in_layout(P['w_in'][l], r) for r in range(2)]
        gp = gcol_layout(P['pre_mix_norm'][l])
        res = run_bass_kernel_spmd(build_A(), [{"xT": xs[c], "w_in": wl[c % 2], "gpre": gp} for c in cores], core_ids=cores).results
        seqs = []
        for c in cores:
            r = c % 2
            halves = [None, None]
            halves[r] = res[c]["mine"]
            halves[1 - r] = res[c ^ 1]["send"]
            seqs.append(np.ascontiguousarray(np.concatenate(halves, axis=1)))
        del res
        pml = [pm_layout(P, l, r) for r in range(2)]
        res = run_bass_kernel_spmd(build_B(l), [{"seq": seqs[c], "pm": pml[c % 2], "cbf": cbf, "cos": cos, "sin": sin} for c in cores], core_ids=cores).results
        ys = []
        for c in cores:
            r = c % 2
            ys.append(np.ascontiguousarray(np.concatenate([res[c]["yT"][:, r * T:(r + 1) * T], res[c ^ 1]["yT"][:, r * T:(r + 1) * T]], axis=0)))
        del res, seqs
        wo = [w_out_layout(P['w_out'][l], r) for r in range(2)]
        wgu = w_gu_layout(P['w_ffn_gate'][l], P['w_ffn_up'][l])
        wd = tile_w(P['w_ffn_down'][l], 22)
        g3 = np.stack([gcol_layout(P[nm][l]) for nm in ('post_mix_norm', 'pre_ffn_norm', 'post_ffn_norm')])
        res = run_bass_kernel_spmd(build_C(), [{"xT": xs[c], "yT": ys[c], "w_out": wo[c % 2], "w_gu": wgu, "w_down": wd, "gains": g3} for c in cores], core_ids=cores).results
        xs = [np.asarray(res[c]["x2"]) for c in cores]
        del res, ys
    outp = np.empty((B, S, D), np.float32)
    for c in cores:
        outp[c // 2, (c % 2) * T:(c % 2 + 1) * T, :] = xs[c].T
    return outp


def kernel(**inputs):
    return kernel_fused(**inputs)
```

```python
import math
import numpy as np
import ml_dtypes
import concourse.bass as bass
import concourse.mybir as mybir
from concourse.bass_utils import run_bass_kernel_spmd

F32 = mybir.dt.float32
BF16 = mybir.dt.bfloat16
AF = mybir.ActivationFunctionType
ALU = mybir.AluOpType

D = 2048
B = 4
S = 4096
L = 4
T = 2048
NCORES = 8
HC = 2944
NHC = HC // 128
DFF = 5632
NFF = DFF // 128
KC = D // 128
EPS = 1e-6
RG = [[0, 1], [2, 3], [4, 5], [6, 7]]

R_Q, R_K, R_V, R_LG, R_LX, R_GQ, R_GK, R_GV, R_GO, R_GLR = 0, 512, 1024, 1536, 1792, 2048, 2176, 2304, 2560, 2816
YH = 1024
Y_ATT, Y_LRU, Y_GLA = 0, 512, 768


class Buf:
    __slots__ = ("name", "last_w", "readers")

    def __init__(self, name):
        self.name = name
        self.last_w = None
        self.readers = []


class _Op:
    __slots__ = ("q", "fn", "deps", "kind", "sig", "sem", "val", "slot_prev")

    def __init__(self, q, fn, deps, kind):
        self.q = q
        self.fn = fn
        self.deps = deps
        self.kind = kind
        self.sig = False
        self.sem = None
        self.val = 0
        self.slot_prev = None


QUEUES = ("pe", "act", "dve", "pool", "sp")
EPOCH = 30000


class Sched:
    def __init__(self, nc):
        self.nc = nc
        self.ops = []
        self.nslots = {"sp": 16, "act": 6, "pool": 8, "pe": 2, "dve": 2}

    def op(self, q, fn, reads=(), writes=(), kind="c"):
        idx = len(self.ops)
        deps = set()
        for b in reads:
            if b.last_w is not None:
                deps.add(b.last_w)
        for b in writes:
            if b.last_w is not None:
                deps.add(b.last_w)
            deps.update(b.readers)
        for b in reads:
            b.readers.append(idx)
        for b in writes:
            b.last_w = idx
            b.readers = []
        deps.discard(idx)
        self.ops.append(_Op(q, fn, deps, kind))
        return idx

    def dma(self, q, out, in_, reads=(), writes=(), **kw):
        def fn(e):
            o = out(e) if callable(out) else out
            i = in_(e) if callable(in_) else in_
            return e.dma_start(out=o, in_=i, **kw)
        return self.op(q, fn, reads, writes, kind="d")

    def emit(self):
        nc = self.nc
        ops = self.ops
        for o in ops:
            best = {}
            keep = set()
            for d in o.deps:
                p = ops[d]
                if p.kind == "c":
                    if p.q == "pe" and o.q == "pe":
                        continue
                    if p.q not in best or best[p.q] < d:
                        best[p.q] = d
                else:
                    keep.add(d)
            o.deps = keep | set(best.values())
            for d in o.deps:
                ops[d].sig = True
        sems = {}

        def get_sem(name):
            if name not in sems:
                sems[name] = nc.alloc_semaphore(name)
            return sems[name]

        ccount = {q: 0 for q in QUEUES}
        dcount = {q: 0 for q in QUEUES}
        xcount = 0
        slot_tot = {}
        slot_last = {}
        for i, o in enumerate(ops):
            if o.kind == "c":
                if not o.sig:
                    continue
                ccount[o.q] += 1
                ep = ccount[o.q] // EPOCH
                o.sem = get_sem(f"c_{o.q}_{ep}")
                o.val = ccount[o.q] - ep * EPOCH + (1 if ep > 0 else 0)
                if ep > 0:
                    o.val = ccount[o.q] - ep * EPOCH + 1
            elif o.kind == "d":
                k = dcount[o.q] % self.nslots[o.q]
                dcount[o.q] += 1
                name = f"d_{o.q}_{k}"
                o.sem = get_sem(name)
                slot_tot[name] = slot_tot.get(name, 0) + 16
                o.val = slot_tot[name]
                o.slot_prev = slot_last.get(name)
                slot_last[name] = i
                o.sig = True
            else:
                k = xcount % 4
                xcount += 1
                name = f"x_{k}"
                o.sem = get_sem(name)
                slot_tot[name] = slot_tot.get(name, 0) + 1
                o.val = slot_tot[name]
                o.slot_prev = slot_last.get(name)
                slot_last[name] = i
                o.sig = True
        per_q = {q: [] for q in QUEUES}
        for i, o in enumerate(ops):
            per_q[o.q].append(i)
        final_waits = [(o.sem, o.val) for o in (ops[i] for i in slot_last.values())]
        self.n_inst = len(ops)

        def run_queue(q, e):
            known = {}
            for i in per_q[q]:
                o = ops[i]
                deps = set(o.deps)
                if o.slot_prev is not None:
                    deps.add(o.slot_prev)
                need = {}
                for d in deps:
                    p = ops[d]
                    key = id(p.sem)
                    if key not in need or need[key][1] < p.val:
                        need[key] = (p.sem, p.val)
                for key, (sem, val) in need.items():
                    if known.get(key, 0) >= val:
                        continue
                    e.wait_ge(sem, val)
                    known[key] = val
                ins = o.fn(e)
                if o.sig:
                    if o.kind == "d":
                        ins.then_inc(o.sem, 16)
                    elif o.kind == "x":
                        ins.then_inc(o.sem)
                    else:
                        ins.then_inc(o.sem, 1)
            if q == "sp":
                for sem, val in final_waits:
                    e.wait_ge(sem, val)

        with nc.Block() as block:
            @block.tensor
            def _(e):
                run_queue("pe", e)

            @block.scalar
            def _(e):
                run_queue("act", e)

            @block.vector
            def _(e):
                run_queue("dve", e)

            @block.gpsimd
            def _(e):
                run_queue("pool", e)

            @block.sync
            def _(e):
                run_queue("sp", e)


class Tile:
    __slots__ = ("ap", "buf")

    def __init__(self, ap, buf):
        self.ap = ap
        self.buf = buf


class Arena:
    def __init__(self, base_ap, nwords):
        self.base = base_ap
        self.nwords = nwords
        self.top = 0
        self.live = []
        self.retired = []
        self.marks = []

    def alloc(self, name, free_shape, dtype):
        n = 1
        for s in free_shape:
            n *= s
        words = (n + 1) // 2 if dtype == BF16 else n
        words = (words + 7) // 8 * 8
        start = self.top
        end = start + words
        assert end <= self.nwords, f"SBUF arena overflow allocating {name}: {end*4} > {self.nwords*4}"
        self.top = end
        buf = Buf(name)
        for (s0, e0, b0) in self.retired:
            if s0 < end and start < e0:
                if b0.last_w is not None:
                    buf.readers.append(b0.last_w)
                buf.readers.extend(b0.readers)
        self.live.append((start, end, buf))
        ap = self.base[:, start:end]
        if dtype == BF16:
            ap = ap.bitcast(BF16)[:, 0:n]
        else:
            ap = ap[:, 0:n]
        if len(free_shape) == 2:
            ap = ap.rearrange("p (a b) -> p a b", a=free_shape[0])
        elif len(free_shape) == 3:
            ap = ap.rearrange("p (a b c) -> p a b c", a=free_shape[0], b=free_shape[1])
        return Tile(ap, buf)

    def mark(self):
        self.marks.append((self.top, len(self.live)))

    def release(self):
        top, nl = self.marks.pop()
        self.retired.extend(self.live[nl:])
        del self.live[nl:]
        self.top = top


class Ctx:
    pass


def dense(cx, name, w_ap, n_oc, n_kg, KG, A, Tn, evac, scale=None, banks=(2, 3, 4, 5), wq="sp", oc_order=None, nbuf=2):
    S, ar = cx.S, cx.arena
    ar.mark()
    wst = [ar.alloc(f"{name}_wst{j}", [KG, 128], F32) for j in range(nbuf)]
    wb = [ar.alloc(f"{name}_wb{j}", [KG, 128], BF16) for j in range(nbuf)]
    steps = [(oc, kg) for oc in (oc_order if oc_order is not None else range(n_oc)) for kg in range(n_kg)]
    ntg = Tn // 512

    def load(i):
        oc, kg = steps[i]
        t = wst[i % nbuf]
        S.dma(wq, t.ap, w_ap[oc, kg].rearrange("p (k n) -> p k n", k=KG), writes=[t.buf])

    def cast(i):
        oc, kg = steps[i]
        src, dst = wst[i % nbuf], wb[i % nbuf]
        if scale is None:
            S.op("act", lambda e, d=dst, s=src: e.copy(out=d.ap, in_=s.ap), reads=[src.buf], writes=[dst.buf])
        else:
            sc = scale.ap[:, kg * KG:(kg + 1) * KG].unsqueeze(2).broadcast_to([128, KG, 128])
            S.op("dve", lambda e, d=dst, s=src, sc=sc: e.tensor_tensor(out=d.ap, in0=s.ap, in1=sc, op=ALU.mult),
                 reads=[src.buf, scale.buf], writes=[dst.buf])

    for i0 in range(min(nbuf, len(steps))):
        load(i0)
    cast(0)
    bi = 0
    cur_bank = {}
    for i, (oc, kg) in enumerate(steps):
        if i + 1 < len(steps):
            cast(i + 1)
        if i + nbuf < len(steps):
            load(i + nbuf)
        w = wb[i % nbuf]
        for tg in range(ntg):
            if kg == 0:
                cur_bank[tg] = banks[bi % len(banks)]
                bi += 1
            bk = cur_bank[tg]
            ps = cx.psum[bk]
            for kc in range(KG):
                kk = kg * KG + kc
                S.op("pe", lambda e, ps=ps, w=w, kc=kc, kk=kk, tg=tg, st=(kk == 0), sp=(kk == n_kg * KG - 1):
                     e.matmul(ps.ap, lhsT=w.ap[:, kc, :], rhs=A.ap[:, kk, tg * 512:(tg + 1) * 512], start=st, stop=sp),
                     reads=[w.buf, A.buf], writes=[ps.buf])
            if kg == n_kg - 1:
                evac(oc, tg, bk)
    ar.release()


def dbuf(cx, name, key=0):
    k = (name, key)
    if k not in cx.dbufs:
        cx.dbufs[k] = Buf(f"{name}_{key}")
    return cx.dbufs[k]


def load_x_pass(cx, x_dram, t0, xg):
    xv = x_dram.rearrange("(k p) t -> p k t", p=128)
    for g in range(4):
        cx.S.dma("sp", xg[g].ap, xv[:, 4 * g:4 * g + 4, t0:t0 + 1024], reads=[dbuf(cx, "x", x_dram.tensor.name)], writes=[xg[g].buf])


def rms_stats(cx, src_fn, rstd, c0, bank):
    S = cx.S
    ps = cx.psum[bank]
    for kc in range(KC):
        sqt = cx.sq[kc % 2]
        ap, buf = src_fn(kc)
        S.op("act", lambda e, o=sqt, a=ap: e.activation(out=o.ap, in_=a, func=AF.Square), reads=[buf], writes=[sqt.buf])
        S.op("pe", lambda e, o=sqt, kc=kc: e.matmul(ps.ap, lhsT=cx.ones.ap, rhs=o.ap, start=(kc == 0), stop=(kc == KC - 1)),
             reads=[sqt.buf, cx.ones.buf], writes=[ps.buf])
    r = rstd.ap[:, c0:c0 + 512]
    S.op("act", lambda e: e.activation(out=r, in_=ps.ap, func=AF.Sqrt, scale=1.0 / D, bias=cx.epsc.ap), reads=[ps.buf, cx.epsc.buf], writes=[rstd.buf])
    S.op("dve", lambda e: e.reciprocal(out=r, in_=r), reads=[rstd.buf], writes=[rstd.buf])


def norm_to_bf16(cx, x_dram, hT, rstd, Tn, tbase):
    S, ar = cx.S, cx.arena
    for ps_ in range(Tn // 1024):
        ar.mark()
        xg = [ar.alloc(f"xg{g}", [4, 1024], F32) for g in range(4)]
        load_x_pass(cx, x_dram, tbase + ps_ * 1024, xg)
        for tg in range(2):
            c0 = ps_ * 1024 + tg * 512
            rms_stats(cx, lambda kc, tg=tg: (xg[kc // 4].ap[:, kc % 4, tg * 512:(tg + 1) * 512], xg[kc // 4].buf), rstd, c0, tg)
            rb = rstd.ap[:, c0:c0 + 512].unsqueeze(1).broadcast_to([128, 4, 512])
            for g in range(4):
                S.op("dve", lambda e, g=g, tg=tg, c0=c0, rb=rb: e.tensor_tensor(
                    out=hT.ap[:, 4 * g:4 * g + 4, c0:c0 + 512], in0=xg[g].ap[:, :, tg * 512:(tg + 1) * 512], in1=rb, op=ALU.mult),
                    reads=[xg[g].buf, rstd.buf], writes=[hT.buf])
        ar.release()


def phase_A(cx, l, x_dram, mine_dst, send_dst, oc_order=None, post_store=None):
    S, ar = cx.S, cx.arena
    ar.mark()
    hT = ar.alloc("hT", [KC, T], BF16)
    gcol = ar.alloc("gpre", [KC], F32)
    rstd = ar.alloc("rstdA", [T], F32)
    S.dma("sp", gcol.ap, cx.d_pre_mix[l], writes=[gcol.buf])
    norm_to_bf16(cx, x_dram, hT, rstd, T, 0)
    ob = [ar.alloc(f"obA{j}", [512], BF16) for j in range(4)]
    cnt = [0]

    def evac(oc, tg, bk):
        o = ob[cnt[0] % 4]
        ps = cx.psum[bk]
        if cnt[0] % 2 == 0:
            S.op("act", lambda e: e.copy(out=o.ap, in_=ps.ap), reads=[ps.buf], writes=[o.buf])
        else:
            S.op("dve", lambda e: e.tensor_copy(out=o.ap, in_=ps.ap), reads=[ps.buf], writes=[o.buf])
        cnt[0] += 1
        dst, db = mine_dst(oc, tg) if oc < NHC else send_dst(oc - NHC, tg)
        S.dma("sp", dst, o.ap, reads=[o.buf], writes=[db])
        if post_store is not None and tg == T // 512 - 1:
            post_store(oc)

    dense(cx, "inproj", cx.d_w_in[l], 2 * NHC, 1, KC, hT, T, evac, scale=gcol, banks=(2, 3, 4, 5, 6, 7), oc_order=oc_order, nbuf=3)
    ar.release()


def make_ctx(nc, stack):
    cx = Ctx()
    cx.nc = nc
    cx.S = Sched(nc)
    cx.dbufs = {}
    NW = 49152
    base = stack.enter_context(nc.sbuf_tensor("arena", [128, NW], F32))
    cx.arena = Arena(base, NW)
    cx.psum = []
    for i in range(8):
        t = stack.enter_context(nc.psum_tensor(f"psb{i}", [128, 512], F32))
        cx.psum.append(Tile(t[:, :], Buf(f"psum{i}")))
    cx.ones = cx.arena.alloc("ones", [128], F32)
    cx.epsc = cx.arena.alloc("epsc", [1], F32)
    cx.sq = [cx.arena.alloc(f"sq{j}", [512], F32) for j in range(2)]
    cx.S.op("pool", lambda e: e.memset(cx.ones.ap, 1.0), writes=[cx.ones.buf])
    cx.S.op("pool", lambda e: e.memset(cx.epsc.ap, EPS), writes=[cx.epsc.buf])
    return cx


def tile_w(w, KG):
    K, N = w.shape
    n_oc, n_kc = N // 128, K // 128
    n_kg = n_kc // KG
    return np.ascontiguousarray(w.reshape(n_kg, KG, 128, n_oc, 128).transpose(3, 0, 2, 1, 4)).reshape(n_oc, n_kg, 128, KG * 128)


def half_cols(h):
    idx = []
    idx += list(range(0 + 512 * h, 0 + 512 * h + 512))
    idx += list(range(1024 + 512 * h, 1024 + 512 * h + 512))
    idx += list(range(2048 + 512 * h, 2048 + 512 * h + 512))
    idx += list(range(3072 + 256 * h, 3072 + 256 * h + 256))
    idx += list(range(3584 + 256 * h, 3584 + 256 * h + 256))
    idx += list(range(4096 + 128 * h, 4096 + 128 * h + 128))
    idx += list(range(4352 + 128 * h, 4352 + 128 * h + 128))
    idx += list(range(4608 + 256 * h, 4608 + 256 * h + 256))
    idx += list(range(5120 + 256 * h, 5120 + 256 * h + 256))
    idx += list(range(5632, 5648))
    return idx


def w_in_layout(w, r):
    out = np.zeros((D, 2 * HC), np.float32)
    out[:, 0:2832] = w[:, half_cols(r)]
    out[:, HC:HC + 2832] = w[:, half_cols(1 - r)]
    return tile_w(out, KC)


def gcol_layout(g):
    return np.ascontiguousarray(g.reshape(-1, 128).T)


def proj_post_residual(cx, name, w_ap, n_kg, KG, A, g_dram_ap, x_in, x_out, t0, h_out=None):
    S, ar = cx.S, cx.arena
    Tn = 1024
    ar.mark()
    rstd = ar.alloc(f"{name}_rstd", [Tn], F32)
    gcol = ar.alloc(f"{name}_g", [KC], F32)
    S.dma("sp", gcol.ap, g_dram_ap, writes=[gcol.buf])
    mo = [ar.alloc(f"{name}_mo{j}", [512], F32) for j in range(3)]
    cnt = [0]
    spill = cx.d_spill

    def evac(oc, tg, bk):
        o = mo[cnt[0] % 3]
        cnt[0] += 1
        ps = cx.psum[bk]
        S.op("act", lambda e: e.copy(out=o.ap, in_=ps.ap), reads=[ps.buf], writes=[o.buf])
        sqt = cx.sq[cnt[0] % 2]
        S.op("dve", lambda e: e.tensor_tensor(out=sqt.ap, in0=ps.ap, in1=o.ap, op=ALU.mult), reads=[ps.buf, o.buf], writes=[sqt.buf])
        acc = cx.psum[tg]
        S.op("pe", lambda e: e.matmul(acc.ap, lhsT=cx.ones.ap, rhs=sqt.ap, start=(oc == 0), stop=(oc == KC - 1)),
             reads=[sqt.buf, cx.ones.buf], writes=[acc.buf])
        S.dma("sp", spill[oc * 128:(oc + 1) * 128, tg * 512:(tg + 1) * 512], o.ap, reads=[o.buf], writes=[dbuf(cx, "spill", oc)])

    dense(cx, name, w_ap, KC, n_kg, KG, A, Tn, evac, banks=(2, 3, 4, 5, 6, 7))
    for tg in range(2):
        r = rstd.ap[:, tg * 512:(tg + 1) * 512]
        acc = cx.psum[tg]
        S.op("act", lambda e, r=r, acc=acc: e.activation(out=r, in_=acc.ap, func=AF.Sqrt, scale=1.0 / D, bias=cx.epsc.ap),
             reads=[acc.buf, cx.epsc.buf], writes=[rstd.buf])
        S.op("dve", lambda e, r=r: e.reciprocal(out=r, in_=r), reads=[rstd.buf], writes=[rstd.buf])
    xp = None
    if h_out is not None:
        xp = ar.alloc(f"{name}_xp", [KC, Tn], F32)
    NB = 3
    mt = [ar.alloc(f"{name}_mt{j}", [Tn], F32) for j in range(NB)]
    xt = [ar.alloc(f"{name}_xt{j}", [Tn], F32) for j in range(NB)]
    xo = [ar.alloc(f"{name}_xo{j}", [Tn], F32) for j in range(NB)] if xp is None else None
    for kc in range(KC):
        m, x = mt[kc % NB], xt[kc % NB]
        S.dma("sp", m.ap, spill[kc * 128:(kc + 1) * 128, :], reads=[dbuf(cx, "spill", kc)], writes=[m.buf])
        S.dma("sp", x.ap, x_in[kc * 128:(kc + 1) * 128, t0:t0 + Tn], reads=[dbuf(cx, "x", x_in.tensor.name)], writes=[x.buf])
        S.op("dve", lambda e, m=m: e.tensor_tensor(out=m.ap, in0=m.ap, in1=rstd.ap, op=ALU.mult), reads=[m.buf, rstd.buf], writes=[m.buf])
        if xp is not None:
            dst_ap, dst_buf = xp.ap[:, kc, :], xp.buf
        else:
            dst_ap, dst_buf = xo[kc % NB].ap, xo[kc % NB].buf
        S.op("dve", lambda e, m=m, x=x, kc=kc, d=dst_ap: e.scalar_tensor_tensor(out=d, in0=m.ap, scalar=gcol.ap[:, kc:kc + 1], in1=x.ap, op0=ALU.mult, op1=ALU.add),
             reads=[m.buf, x.buf, gcol.buf], writes=[dst_buf])
        S.dma("act", x_out[kc * 128:(kc + 1) * 128, t0:t0 + Tn], dst_ap, reads=[dst_buf], writes=[dbuf(cx, "x", x_out.tensor.name)])
    if h_out is not None:
        rstd2 = ar.alloc(f"{name}_rstd2", [Tn], F32)
        for tg in range(2):
            rms_stats(cx, lambda kc, tg=tg: (xp.ap[:, kc, tg * 512:(tg + 1) * 512], xp.buf), rstd2, tg * 512, tg)
            rb = rstd2.ap[:, tg * 512:(tg + 1) * 512].unsqueeze(1).broadcast_to([128, 4, 512])
            for g in range(4):
                S.op("dve", lambda e, g=g, tg=tg, rb=rb: e.tensor_tensor(
                    out=h_out.ap[:, 4 * g:4 * g + 4, tg * 512:(tg + 1) * 512], in0=xp.ap[:, 4 * g:4 * g + 4, tg * 512:(tg + 1) * 512], in1=rb, op=ALU.mult),
                    reads=[xp.buf, rstd2.buf], writes=[h_out.buf])
    ar.release()


def ffn_hidden(cx, l, h2T, hidT):
    S, ar = cx.S, cx.arena
    ar.mark()
    gcol = ar.alloc("gffn", [KC], F32)
    S.dma("sp", gcol.ap, cx.d_pre_ffn[l], writes=[gcol.buf])
    sg = [ar.alloc(f"sg{j}", [512], F32) for j in range(4)]
    state = {}
    cnt = [0]

    def evac(oc2, tg, bk):
        ps = cx.psum[bk]
        if oc2 % 2 == 0:
            s = sg[cnt[0] % 4]
            cnt[0] += 1
            state[tg] = s
            S.op("act", lambda e: e.activation(out=s.ap, in_=ps.ap, func=AF.Silu), reads=[ps.buf], writes=[s.buf])
        else:
            s = state[tg]
            oc = oc2 // 2
            S.op("dve", lambda e: e.tensor_tensor(out=hidT.ap[:, oc, tg * 512:(tg + 1) * 512], in0=ps.ap, in1=s.ap, op=ALU.mult),
                 reads=[ps.buf, s.buf], writes=[hidT.buf])

    dense(cx, "gu", cx.d_w_gu[l], 2 * NFF, 1, KC, h2T, 1024, evac, scale=gcol, banks=(2, 3, 4, 5, 6, 7))
    ar.release()


def phase_C(cx, l, x_in, x1_dram, x_out, y_load):
    S, ar = cx.S, cx.arena
    for p in range(2):
        t0 = p * 1024
        ar.mark()
        h2T = ar.alloc("h2T", [KC, 1024], BF16)
        ar.mark()
        yT = ar.alloc("ycatT", [KC, 1024], BF16)
        for kc in range(KC):
            y_load(kc, t0, yT)
        proj_post_residual(cx, "op", cx.d_w_out[l], 1, KC, yT, cx.d_post_mix[l], x_in, x1_dram, t0, h_out=h2T)
        ar.release()
        hidT = ar.alloc("hidT", [NFF, 1024], BF16)
        ffn_hidden(cx, l, h2T, hidT)
        proj_post_residual(cx, "dn", cx.d_w_down[l], 2, 22, hidT, cx.d_post_ffn[l], x1_dram, x_out, t0)
        ar.release()


def ycat_half_cols(h):
    return list(range(512 * h, 512 * h + 512)) + list(range(1024 + 256 * h, 1024 + 256 * h + 256)) + list(range(1536 + 256 * h, 1536 + 256 * h + 256))


def w_out_layout(w, r):
    rows = ycat_half_cols(r) + ycat_half_cols(1 - r)
    return tile_w(np.ascontiguousarray(w[rows, :]), KC)


def w_gu_layout(wg, wu):
    tg, tu = tile_w(wg, KC), tile_w(wu, KC)
    out = np.empty((2 * NFF,) + tg.shape[1:], np.float32)
    out[0::2] = tg
    out[1::2] = tu
    return out


def ln_rstd(cx, dst, src_ps, n_feat, eps_t):
    S = cx.S
    S.op("act", lambda e: e.activation(out=dst.ap, in_=src_ps.ap, func=AF.Ln, scale=1.0 / n_feat, bias=eps_t.ap),
         reads=[src_ps.buf, eps_t.buf], writes=[dst.buf])
    S.op("act", lambda e: e.activation(out=dst.ap, in_=dst.ap, func=AF.Exp, scale=-0.5), reads=[dst.buf], writes=[dst.buf])


def load_bcast_vec(cx, name, dram_vec_ap, n):
    t = cx.arena.alloc(name, [n], F32)
    cx.S.dma("sp", t.ap, dram_vec_ap.partition_broadcast(128), writes=[t.buf])
    return t


def attn_alloc(cx):
    ar = cx.arena
    at = {}
    at["raw"] = {nm: ar.alloc(f"a_{nm}", [S_LEN], BF16) for nm in ("q", "k", "v")}
    at["KR"] = ar.alloc("KR", [S_LEN], BF16)
    at["sets"] = [{"QR": ar.alloc(f"QR{j}", [S_LEN], BF16), "KRm": [ar.alloc(f"KRm{j}_{m}", [S_LEN], BF16) for m in range(2)],
                   "Vt": ar.alloc(f"Vtok{j}", [32, 128], BF16)} for j in range(2)]
    at["t1"] = [ar.alloc(f"rt1_{j}", [512], F32) for j in range(2)]
    at["t2"] = [ar.alloc(f"rt2_{j}", [512], F32) for j in range(2)]
    at["pt"] = [ar.alloc(f"pt{j}", [512], BF16) for j in range(6)]
    at["sacc"] = [ar.alloc(f"sacc{j}", [512], F32) for j in range(2)]
    at["rs"] = [ar.alloc(f"rs{j}", [512], F32) for j in range(2)]
    at["tn"] = [ar.alloc(f"tn{j}", [512], F32) for j in range(2)]
    at["o_t"] = ar.alloc("o_t", [512], F32)
    at["sq_t"] = ar.alloc("sq_t", [512], F32)
    at["rstd_t"] = ar.alloc("rstd_t", [512], F32)
    at["yb"] = [ar.alloc(f"yb{j}", [512], BF16) for j in range(2)]
    return at


def attn_prep(cx, hd, cs, at, st):
    S = cx.S
    raw, KR, QR, KRm, Vt = at["raw"], at["KR"], st["QR"], st["KRm"], st["Vt"]
    for nm, r0 in (("q", R_Q), ("k", R_K), ("v", R_V)):
        t = raw[nm]
        for hf in range(2):
            src, sb = cx.seq_src(r0 + hd * 128, 128, hf * 2048, 2048)
            S.dma("sp", t.ap[:, hf * 2048:(hf + 1) * 2048], src, reads=[sb], writes=[t.buf])
    yield
    ps = cx.psum[3]
    i = 0
    for src, dst in ((raw["q"], QR), (raw["k"], KR)):
        for tg in range(8):
            sl = slice(tg * 512, (tg + 1) * 512)
            a, b2 = at["t1"][i % 2], at["t2"][i % 2]
            i += 1
            S.op("pe", lambda e, src=src, sl=sl: e.matmul(ps.ap, lhsT=cs["rm"].ap, rhs=src.ap[:, sl], start=True, stop=True),
                 reads=[cs["rm"].buf, src.buf], writes=[ps.buf])
            S.op("dve", lambda e, a=a, src=src, sl=sl: e.tensor_tensor(out=a.ap, in0=src.ap[:, sl], in1=cs["cos"].ap[:, sl], op=ALU.mult),
                 reads=[src.buf, cs["cos"].buf], writes=[a.buf])
            S.op("dve", lambda e, b2=b2, sl=sl: e.tensor_tensor(out=b2.ap, in0=ps.ap, in1=cs["sin"].ap[:, sl], op=ALU.mult),
                 reads=[ps.buf, cs["sin"].buf], writes=[b2.buf])
            S.op("pool", lambda e, a=a, b2=b2, dst=dst, sl=sl: e.tensor_tensor(out=dst.ap[:, sl], in0=a.ap, in1=b2.ap, op=ALU.add),
                 reads=[a.buf, b2.buf], writes=[dst.buf])
            if dst is KR:
                for m in range(2):
                    S.op("pool", lambda e, m=m, sl=sl: e.tensor_scalar(out=KRm[m].ap[:, sl], in0=KR.ap[:, sl], scalar1=cs["hm"][m].ap, scalar2=None, op0=ALU.mult),
                         reads=[KR.buf, cs["hm"][m].buf], writes=[KRm[m].buf])
            yield
    psb = ps.ap.bitcast(BF16)
    for g in range(4):
        for j in range(8):
            tt = g * 8 + j
            S.op("pe", lambda e, j=j, tt=tt: e.transpose(psb[:, j * 128:(j + 1) * 128], raw["v"].ap[:, tt * 128:(tt + 1) * 128], cs["ident"].ap),
                 reads=[raw["v"].buf, cs["ident"].buf], writes=[ps.buf])
        S.op("act", lambda e, g=g: e.copy(out=Vt.ap[:, g * 8:(g + 1) * 8, :], in_=psb.rearrange("p (a b) -> p a b", a=8)),
             reads=[ps.buf], writes=[Vt.buf])
        yield


def attn_main(cx, hd, cs, at, st, bg=None, bg_per_qg=3):
    S = cx.S
    QR, KRm, Vt = st["QR"], st["KRm"], st["Vt"]
    pt, rs, tn, o_t, sq_t, rstd_t, yb, sacc = at["pt"], at["rs"], at["tn"], at["o_t"], at["sq_t"], at["rstd_t"], at["yb"], at["sacc"]
    for qg in range(8):
        blocks = [(kt, m) for kt in range(4 * qg + 4) for m in range(2)]
        last_kt = 4 * qg + 3

        def geom(kt):
            j = kt - 4 * qg
            q0 = max(j, 0) * 128
            return j, q0, 512 - q0

        STB = (0, 1, 2, 6)
        LA = 3

        def ST(bi):
            kt, m = blocks[bi]
            j, q0, N = geom(kt)
            ps = cx.psum[STB[bi % 4]]
            S.op("pe", lambda e, qg=qg: e.matmul(ps.ap[:, 0:N], lhsT=KRm[m].ap[:, kt * 128:(kt + 1) * 128], rhs=QR.ap[:, qg * 512 + q0:(qg + 1) * 512], start=True, stop=True),
                 reads=[KRm[m].buf, QR.buf], writes=[ps.buf])

        for b0 in range(min(LA, len(blocks))):
            ST(b0)
        for bi, (kt, m) in enumerate(blocks):
            if bi + LA < len(blocks):
                ST(bi + LA)
            j, q0, N = geom(kt)
            ps = cx.psum[STB[bi % 4]]
            p = pt[bi % 6]
            S.op("act", lambda e, ps=ps, p=p, N=N: e.activation(out=p.ap[:, 0:N], in_=ps.ap[:, 0:N], func=AF.Exp, scale=0.125), reads=[ps.buf], writes=[p.buf])
            if j >= 0:
                S.op("pool", lambda e, p=p: e.tensor_tensor(out=p.ap[:, 0:128], in0=p.ap[:, 0:128], in1=cs["tri"].ap, op=ALU.mult),
                     reads=[p.buf, cs["tri"].buf], writes=[p.buf])
            po = cx.psum[4 + m]
            S.op("pe", lambda e, po=po, p=p, kt=kt, q0=q0, N=N, last_kt=last_kt: e.matmul(po.ap[:, q0:512], lhsT=Vt.ap[:, kt, :], rhs=p.ap[:, 0:N], start=(kt == 0), stop=(kt == last_kt)),
                 reads=[Vt.buf, p.buf], writes=[po.buf])
            if m == 0:
                sa = sacc[0]
                if kt == 0:
                    S.op("dve", lambda e, sa=sa, p=p: e.tensor_copy(out=sa.ap, in_=p.ap), reads=[p.buf], writes=[sa.buf])
                else:
                    S.op("dve", lambda e, sa=sa, p=p, q0=q0, N=N: e.tensor_tensor(out=sa.ap[:, q0:512], in0=sa.ap[:, q0:512], in1=p.ap[:, 0:N], op=ALU.add),
                         reads=[p.buf, sa.buf], writes=[sa.buf])
            else:
                psm = cx.psum[7]
                S.op("pe", lambda e, psm=psm, p=p, kt=kt, q0=q0, N=N, last_kt=last_kt: e.matmul(psm.ap[:, q0:512], lhsT=cs["ones_bf"].ap, rhs=p.ap[:, 0:N], start=(kt == 0), stop=(kt == last_kt)),
                     reads=[cs["ones_bf"].buf, p.buf], writes=[psm.buf])
        S.op("pe", lambda e: e.matmul(cx.psum[6].ap, lhsT=cx.ones.ap, rhs=sacc[0].ap, start=True, stop=True),
             reads=[cx.ones.buf, sacc[0].buf], writes=[cx.psum[6].buf])
        for m in range(2):
            S.op("dve", lambda e, m=m: e.reciprocal(out=rs[m].ap, in_=cx.psum[6 + m].ap), reads=[cx.psum[6 + m].buf], writes=[rs[m].buf])
            S.op("dve", lambda e, m=m: e.tensor_tensor(out=tn[m].ap, in0=cx.psum[4 + m].ap, in1=rs[m].ap, op=ALU.mult),
                 reads=[cx.psum[4 + m].buf, rs[m].buf], writes=[tn[m].buf])
        S.op("dve", lambda e: e.scalar_tensor_tensor(out=o_t.ap, in0=tn[1].ap, scalar=cs["neg_lam"].ap, in1=tn[0].ap, op0=ALU.mult, op1=ALU.add),
             reads=[tn[0].buf, tn[1].buf, cs["neg_lam"].buf], writes=[o_t.buf])
        S.op("pool", lambda e: e.tensor_tensor(out=sq_t.ap, in0=o_t.ap, in1=o_t.ap, op=ALU.mult), reads=[o_t.buf], writes=[sq_t.buf])
        pn = cx.psum[0]
        S.op("pe", lambda e: e.matmul(pn.ap, lhsT=cx.ones.ap, rhs=sq_t.ap, start=True, stop=True), reads=[cx.ones.buf, sq_t.buf], writes=[pn.buf])
        ln_rstd(cx, rstd_t, pn, 128, cx.epsc)
        y = yb[qg % 2]
        S.op("dve", lambda e, y=y: e.scalar_tensor_tensor(out=y.ap, in0=o_t.ap, scalar=cs["subw"].ap, in1=rstd_t.ap, op0=ALU.mult, op1=ALU.mult),
             reads=[o_t.buf, cs["subw"].buf, rstd_t.buf], writes=[y.buf])
        dst, db = cx.y_dst(Y_ATT + hd * 128, qg * 512, 512)
        S.dma("sp", dst, y.ap, reads=[y.buf], writes=[db])
        if bg is not None:
            for _ in range(bg_per_qg):
                next(bg, None)
    if bg is not None:
        for _ in bg:
            pass


def attention_all(cx, l, cs, after_attn=None):
    S, ar = cx.S, cx.arena
    ar.mark()
    cos = ar.alloc("cos", [S_LEN], F32)
    sin = ar.alloc("sin", [S_LEN], F32)
    for t, d in ((cos, cx.d_cos), (sin, cx.d_sin)):
        for hf in range(2):
            S.dma("sp", t.ap[:, hf * 2048:(hf + 1) * 2048], d[:, hf * 2048:(hf + 1) * 2048], writes=[t.buf])
    cs["cos"], cs["sin"] = cos, sin
    at = attn_alloc(cx)
    for _ in attn_prep(cx, 0, cs, at, at["sets"][0]):
        pass
    for hd in range(4):
        bg = attn_prep(cx, hd + 1, cs, at, at["sets"][(hd + 1) % 2]) if hd < 3 else None
        attn_main(cx, hd, cs, at, at["sets"][hd % 2], bg)
    ar.release()


S_LEN = S

PM_CW, PM_CB, PM_BA, PM_BX, PM_LAM, PM_SUBLN, PM_GNORM, PM_BG = 0, 8, 10, 12, 14, 16, 17, 18
PM_BD = 32
PM_WG = 544
PM_LV = 672
NPM = 928
CB_RM, CB_ID, CB_TRI, CB_ONES, CB_GM = 0, 128, 256, 384, 512


def mixer_consts(cx, l):
    S, ar = cx.S, cx.arena
    cs = {}
    pm = ar.alloc("pm", [NPM], F32)
    S.dma("sp", pm.ap, cx.d_pm[l], writes=[pm.buf])
    cbf = ar.alloc("cbf", [640], BF16)
    S.dma("sp", cbf.ap, cx.d_cbf, writes=[cbf.buf])
    cs["pm"] = pm
    for nm, off in (("rm", CB_RM), ("ident", CB_ID), ("tri", CB_TRI), ("ones_bf", CB_ONES), ("gmask", CB_GM)):
        cs[nm] = Tile(cbf.ap[:, off:off + 128], cbf.buf)
    sm = ar.alloc("smallc", [16], F32)
    cs["sm"] = sm
    onec = ar.alloc("onec", [1], F32)
    S.op("pool", lambda e: e.memset(onec.ap, 1.0), writes=[onec.buf])
    cs["one"] = onec
    lam_init = 0.8 - 0.6 * math.exp(-0.3 * l)
    tmp = ar.alloc("lamtmp", [128], F32)
    for k in range(2):
        a = pm.ap[:, PM_LV + 128 * k:PM_LV + 128 * k + 64]
        b2 = pm.ap[:, PM_LV + 128 * k + 64:PM_LV + 128 * k + 128]
        S.op("dve", lambda e, a=a, b2=b2, k=k: e.tensor_tensor(out=tmp.ap[:, 64 * k:64 * k + 64], in0=a, in1=b2, op=ALU.mult), reads=[pm.buf], writes=[tmp.buf])
        S.op("dve", lambda e, k=k: e.reduce_sum(out=sm.ap[:, k:k + 1], in_=tmp.ap[:, 64 * k:64 * k + 64], axis=mybir.AxisListType.X), reads=[tmp.buf], writes=[sm.buf])
    S.op("act", lambda e: e.activation(out=sm.ap[:, 0:2], in_=sm.ap[:, 0:2], func=AF.Exp), reads=[sm.buf], writes=[sm.buf])
    S.op("dve", lambda e: e.tensor_tensor(out=sm.ap[:, 2:3], in0=sm.ap[:, 1:2], in1=sm.ap[:, 0:1], op=ALU.subtract), reads=[sm.buf], writes=[sm.buf])
    S.op("dve", lambda e: e.tensor_scalar(out=sm.ap[:, 2:3], in0=sm.ap[:, 2:3], scalar1=-lam_init, scalar2=None, op0=ALU.add), reads=[sm.buf], writes=[sm.buf])
    cs["neg_lam"] = Tile(sm.ap[:, 2:3], sm.buf)
    S.op("dve", lambda e: e.tensor_scalar(out=sm.ap[:, 3:4], in0=pm.ap[:, PM_SUBLN:PM_SUBLN + 1], scalar1=1.0 - lam_init, scalar2=None, op0=ALU.mult), reads=[pm.buf], writes=[sm.buf])
    cs["subw"] = Tile(sm.ap[:, 3:4], sm.buf)
    S.op("act", lambda e: e.activation(out=sm.ap[:, 4:6], in_=pm.ap[:, PM_LAM:PM_LAM + 2], func=AF.Exp, scale=-1.0), reads=[pm.buf], writes=[sm.buf])
    S.op("act", lambda e: e.activation(out=sm.ap[:, 4:6], in_=sm.ap[:, 4:6], func=AF.Ln, bias=onec.ap), reads=[sm.buf, onec.buf], writes=[sm.buf])
    S.op("dve", lambda e: e.tensor_scalar(out=sm.ap[:, 6:8], in0=sm.ap[:, 4:6], scalar1=-16.0, scalar2=None, op0=ALU.mult), reads=[sm.buf], writes=[sm.buf])
    S.op("dve", lambda e: e.tensor_scalar(out=sm.ap[:, 4:6], in0=sm.ap[:, 4:6], scalar1=-8.0, scalar2=None, op0=ALU.mult), reads=[sm.buf], writes=[sm.buf])
    S.op("dve", lambda e: e.tensor_scalar(out=sm.ap[:, 8:9], in0=pm.ap[:, PM_BG:PM_BG + 1], scalar1=-1.0, scalar2=None, op0=ALU.mult), reads=[pm.buf], writes=[sm.buf])
    S.op("pool", lambda e: e.memset(sm.ap[0:64, 9:10], 1.0), writes=[sm.buf])
    S.op("pool", lambda e: e.memset(sm.ap[64:128, 9:10], 0.0), writes=[sm.buf])
    S.op("pool", lambda e: e.memset(sm.ap[0:64, 10:11], 0.0), writes=[sm.buf])
    S.op("pool", lambda e: e.memset(sm.ap[64:128, 10:11], 1.0), writes=[sm.buf])
    cs["hm"] = [Tile(sm.ap[:, 9:10], sm.buf), Tile(sm.ap[:, 10:11], sm.buf)]
    return cs


def lru_chunk(cx, l, c, cs):
    S, ar = cx.S, cx.arena
    pm, sm = cs["pm"], cs["sm"]
    BL = 1024
    xgb = [ar.alloc(f"l{c}_xg{j}", [BL], BF16) for j in range(2)]
    xrb = [ar.alloc(f"l{c}_xr{j}", [BL + 8], BF16) for j in range(2)]
    names = ("w1", "gate", "xc", "r", "i", "a", "m", "u", "hs")
    f = {nm: ar.alloc(f"l{c}_{nm}", [BL], F32) for nm in names}
    hprev = ar.alloc(f"l{c}_hprev", [1], F32)
    yb = [ar.alloc(f"l{c}_yb{j}", [BL], BF16) for j in range(2)]
    bda = pm.ap[:, PM_BD + c * 128:PM_BD + (c + 1) * 128]
    bdx = pm.ap[:, PM_BD + (2 + c) * 128:PM_BD + (3 + c) * 128]
    col = lambda off: pm.ap[:, off:off + 1]
    for blk in range(S_LEN // BL):
        t0 = blk * BL
        xg, xr = xgb[blk % 2], xrb[blk % 2]
        src, sb = cx.seq_src(R_LG + c * 128, 128, t0, BL)
        S.dma("sp", xg.ap, src, reads=[sb], writes=[xg.buf])
        if blk == 0:
            S.op("pool", lambda e, xr=xr: e.memset(xr.ap[:, 0:3], 0.0), writes=[xr.buf])
            src, sb = cx.seq_src(R_LX + c * 128, 128, 0, BL)
            S.dma("sp", xr.ap[:, 3:3 + BL], src, reads=[sb], writes=[xr.buf])
        else:
            src, sb = cx.seq_src(R_LX + c * 128, 128, t0 - 3, BL + 3)
            S.dma("sp", xr.ap[:, 0:3 + BL], src, reads=[sb], writes=[xr.buf])
        w1, gate, xc, r, i_, a, m, u, hs = (f[n] for n in names)
        yield
        S.op("act", lambda e, xg=xg: e.activation(out=w1.ap, in_=xg.ap, func=AF.Square), reads=[xg.buf], writes=[w1.buf])
        S.op("dve", lambda e: e.tensor_scalar(out=w1.ap, in0=w1.ap, scalar1=0.044715, scalar2=1.0, op0=ALU.mult, op1=ALU.add), reads=[w1.buf], writes=[w1.buf])
        S.op("dve", lambda e, xg=xg: e.tensor_tensor(out=w1.ap, in0=w1.ap, in1=xg.ap, op=ALU.mult), reads=[w1.buf, xg.buf], writes=[w1.buf])
        S.op("act", lambda e: e.activation(out=w1.ap, in_=w1.ap, func=AF.Sigmoid, scale=1.5957691216057308), reads=[w1.buf], writes=[w1.buf])
        S.op("pool", lambda e, xg=xg: e.tensor_tensor(out=gate.ap, in0=w1.ap, in1=xg.ap, op=ALU.mult), reads=[w1.buf, xg.buf], writes=[gate.buf])
        yield
        S.op("dve", lambda e, xr=xr: e.tensor_scalar(out=xc.ap, in0=xr.ap[:, 3:3 + BL], scalar1=col(PM_CW + 4 * c + 3), scalar2=col(PM_CB + c), op0=ALU.mult, op1=ALU.add),
             reads=[xr.buf, pm.buf], writes=[xc.buf])
        for j in (2, 1, 0):
            S.op("dve", lambda e, xr=xr, j=j: e.scalar_tensor_tensor(out=xc.ap, in0=xr.ap[:, j:j + BL], scalar=col(PM_CW + 4 * c + j), in1=xc.ap, op0=ALU.mult, op1=ALU.add),
                 reads=[xr.buf, pm.buf, xc.buf], writes=[xc.buf])
        yield
        for tg in range(BL // 512):
            sl = slice(tg * 512, (tg + 1) * 512)
            for (bd, bias_off, dst, bk) in ((bda, PM_BA + c, r, 4 * c + tg), (bdx, PM_BX + c, i_, 4 * c + 2 + tg)):
                ps = cx.psum[bk]
                S.op("pe", lambda e, ps=ps, bd=bd, sl=sl: e.matmul(ps.ap, lhsT=bd, rhs=xc.ap[:, sl], start=True, stop=True), reads=[pm.buf, xc.buf], writes=[ps.buf])
                S.op("act", lambda e, ps=ps, dst=dst, sl=sl, bo=bias_off: e.activation(out=dst.ap[:, sl], in_=ps.ap, func=AF.Sigmoid, bias=col(bo)),
                     reads=[ps.buf, pm.buf], writes=[dst.buf])
        yield
        S.op("act", lambda e: e.activation(out=a.ap, in_=r.ap, func=AF.Exp, scale=sm.ap[:, 4 + c:5 + c]), reads=[r.buf, sm.buf], writes=[a.buf])
        S.op("act", lambda e: e.activation(out=m.ap, in_=r.ap, func=AF.Exp, scale=sm.ap[:, 6 + c:7 + c]), reads=[r.buf, sm.buf], writes=[m.buf])
        S.op("act", lambda e: e.activation(out=m.ap, in_=m.ap, func=AF.Sqrt, scale=-1.0, bias=cs["one"].ap), reads=[m.buf, cs["one"].buf], writes=[m.buf])
        S.op("pool", lambda e: e.tensor_tensor(out=u.ap, in0=m.ap, in1=i_.ap, op=ALU.mult), reads=[m.buf, i_.buf], writes=[u.buf])
        S.op("pool", lambda e: e.tensor_tensor(out=u.ap, in0=u.ap, in1=xc.ap, op=ALU.mult), reads=[u.buf, xc.buf], writes=[u.buf])
        yield
        init = 0.0 if blk == 0 else hprev.ap
        S.op("dve", lambda e, init=init: e.tensor_tensor_scan(out=hs.ap, data0=a.ap, data1=u.ap, initial=init, op0=ALU.mult, op1=ALU.add),
             reads=[a.buf, u.buf, hprev.buf], writes=[hs.buf])
        S.op("pool", lambda e: e.tensor_copy(out=hprev.ap, in_=hs.ap[:, BL - 1:BL]), reads=[hs.buf], writes=[hprev.buf])
        y = yb[blk % 2]
        S.op("pool", lambda e, y=y: e.tensor_tensor(out=y.ap, in0=hs.ap, in1=gate.ap, op=ALU.mult), reads=[hs.buf, gate.buf], writes=[y.buf])
        dst, db = cx.y_dst(Y_LRU + c * 128, t0, BL)
        S.dma("sp", dst, y.ap, reads=[y.buf], writes=[db])
        yield


def gla_heads(cx, l, cs):
    S, ar = cx.S, cx.arena
    pm, sm, hm = cs["pm"], cs["sm"], cs["hm"]
    ar.mark()
    qe = ar.alloc("g_qe", [S_LEN], BF16)
    qeh = [ar.alloc(f"g_qe{h}", [S_LEN], BF16) for h in range(2)]
    keh = [ar.alloc(f"g_ke{h}", [S_LEN], BF16) for h in range(2)]
    dec = ar.alloc("g_dec", [64], F32)
    KDh = [ar.alloc(f"g_KDt{h}", [32, 128], BF16) for h in range(2)]
    Vt = [ar.alloc(f"g_Vt{h}", [32, 128], BF16) for h in range(2)]
    ar.mark()
    kd = ar.alloc("g_kd", [S_LEN], BF16)
    ar.mark()
    gq = ar.alloc("g_q", [S_LEN], BF16)
    gk = ar.alloc("g_k", [S_LEN], BF16)
    for t, r0 in ((gq, R_GQ), (gk, R_GK)):
        for hf in range(2):
            src, sb = cx.seq_src(r0, 128, hf * 2048, 2048)
            S.dma("sp", t.ap[:, hf * 2048:(hf + 1) * 2048], src, reads=[sb], writes=[t.buf])
    lrb = [ar.alloc(f"g_lr{j}", [512], BF16) for j in range(2)]
    lrf = [ar.alloc(f"g_lrf{j}", [512], F32) for j in range(2)]
    Bt = ar.alloc("g_Bt", [S_LEN], F32)
    Bc = ar.alloc("g_Bc", [S_LEN], F32)
    E = ar.alloc("g_E", [S_LEN], F32)
    cm = E
    S.op("pool", lambda e: e.memset(cm.ap, 1.0), writes=[cm.buf])
    S.op("pool", lambda e: e.memset(cm.ap.rearrange("p (n c) -> p n c", c=64)[:, :, 0:1], 0.0), writes=[cm.buf])
    wg = pm.ap[0:16, PM_WG:PM_WG + 128]
    for tg in range(8):
        sl = slice(tg * 512, (tg + 1) * 512)
        a, b2 = lrb[tg % 2], lrf[tg % 2]
        src, sb = cx.seq_src(R_GLR, 16, tg * 512, 512)
        S.dma("sp", a.ap[0:16, :], src, reads=[sb], writes=[a.buf])
        S.op("pool", lambda e, a=a, b2=b2: e.tensor_copy(out=b2.ap[0:16, :], in_=a.ap[0:16, :]), reads=[a.buf], writes=[b2.buf])
        ps = cx.psum[tg % 4]
        S.op("pe", lambda e, ps=ps, b2=b2: e.matmul(ps.ap, lhsT=wg, rhs=b2.ap[0:16, :], start=True, stop=True), reads=[pm.buf, b2.buf], writes=[ps.buf])
        S.op("act", lambda e, ps=ps, sl=sl: e.activation(out=Bt.ap[:, sl], in_=ps.ap, func=AF.Exp, scale=-1.0, bias=sm.ap[:, 8:9]), reads=[ps.buf, sm.buf], writes=[Bt.buf])
    S.op("act", lambda e: e.activation(out=Bt.ap, in_=Bt.ap, func=AF.Ln, bias=cs["one"].ap), reads=[Bt.buf, cs["one"].buf], writes=[Bt.buf])
    S.op("dve", lambda e: e.tensor_tensor_scan(out=Bc.ap, data0=cm.ap, data1=Bt.ap, initial=0.0, op0=ALU.mult, op1=ALU.add), reads=[cm.buf, Bt.buf], writes=[Bc.buf])
    S.op("act", lambda e: e.activation(out=E.ap, in_=Bc.ap, func=AF.Exp, scale=-1.0 / 16.0), reads=[Bc.buf], writes=[E.buf])
    S.op("dve", lambda e: e.scalar_tensor_tensor(out=qe.ap, in0=gq.ap, scalar=0.125, in1=E.ap, op0=ALU.mult, op1=ALU.mult), reads=[gq.buf, E.buf], writes=[qe.buf])
    for h in range(2):
        S.op("pool", lambda e, h=h: e.tensor_scalar(out=qeh[h].ap, in0=qe.ap, scalar1=hm[h].ap, scalar2=None, op0=ALU.mult), reads=[qe.buf, hm[h].buf], writes=[qeh[h].buf])
    S.op("act", lambda e: e.activation(out=E.ap, in_=Bc.ap, func=AF.Exp, scale=1.0 / 16.0), reads=[Bc.buf], writes=[E.buf])
    for h in range(2):
        S.op("dve", lambda e, h=h: e.scalar_tensor_tensor(out=keh[h].ap, in0=gk.ap, scalar=hm[h].ap, in1=E.ap, op0=ALU.mult, op1=ALU.mult),
             reads=[gk.buf, hm[h].buf, E.buf], writes=[keh[h].buf])
    Bc3 = Bc.ap.rearrange("p (n c) -> p n c", c=64)
    S.op("dve", lambda e: e.tensor_tensor(out=E.ap.rearrange("p (n c) -> p n c", c=64), in0=Bc3, in1=Bc3[:, :, 63:64].broadcast_to([128, 64, 64]), op=ALU.subtract),
         reads=[Bc.buf], writes=[E.buf])
    S.op("act", lambda e: e.activation(out=E.ap, in_=E.ap, func=AF.Exp, scale=1.0 / 16.0), reads=[E.buf], writes=[E.buf])
    S.op("dve", lambda e: e.tensor_tensor(out=kd.ap, in0=gk.ap, in1=E.ap, op=ALU.mult), reads=[gk.buf, E.buf], writes=[kd.buf])
    S.op("act", lambda e: e.activation(out=dec.ap, in_=Bc3[:, :, 63], func=AF.Exp, scale=-1.0 / 16.0), reads=[Bc.buf], writes=[dec.buf])
    ar.release()
    gv = ar.alloc("g_v", [S_LEN], BF16)
    k = 0
    for (srcT, dsts, row0) in ((kd, KDh, None), (gv, [Vt[0]], R_GV), (gv, [Vt[1]], R_GV + 128)):
        if row0 is not None:
            for hf in range(2):
                src, sb = cx.seq_src(row0, 128, hf * 2048, 2048)
                S.dma("sp", gv.ap[:, hf * 2048:(hf + 1) * 2048], src, reads=[sb], writes=[gv.buf])
        for g in range(4):
            ps = cx.psum[4 + k % 2]
            k += 1
            psb = ps.ap.bitcast(BF16)
            for j in range(8):
                tt = g * 8 + j
                S.op("pe", lambda e, psb=psb, j=j, tt=tt, srcT=srcT: e.transpose(psb[:, j * 128:(j + 1) * 128], srcT.ap[:, tt * 128:(tt + 1) * 128], cs["ident"].ap),
                     reads=[srcT.buf, cs["ident"].buf], writes=[ps.buf])
            pv = psb.rearrange("p (a b) -> p a b", a=8)
            if row0 is None:
                for h in range(2):
                    S.op("dve", lambda e, pv=pv, g=g, h=h: e.tensor_scalar(out=KDh[h].ap[:, g * 8:(g + 1) * 8, :], in0=pv, scalar1=hm[h].ap, scalar2=None, op0=ALU.mult),
                         reads=[ps.buf, hm[h].buf], writes=[KDh[h].buf])
            else:
                S.op("act", lambda e, pv=pv, g=g, d=dsts[0]: e.copy(out=d.ap[:, g * 8:(g + 1) * 8, :], in_=pv), reads=[ps.buf], writes=[dsts[0].buf])
    ar.release()
    if globals().get("GLA_STOP", 9) <= 1.5:
        ar.release()
        return
    KV = ar.alloc("g_KV", [64, 128], F32)
    Sst = ar.alloc("g_Sst", [65, 128], F32)
    Spb = ar.alloc("g_Spb", [64, 128], BF16)
    for g in range(16):
        for h2 in range(2):
            ps = cx.psum[(2 * g + h2) % 4]
            hs_ = slice(h2 * 64, (h2 + 1) * 64)
            for q in range(4):
                n = 4 * g + q
                tt, hf = n // 2, n % 2
                S.op("pe", lambda e, ps=ps, h2=h2, q=q, tt=tt, hf=hf: e.matmul(
                    ps.ap[:, q * 128:(q + 1) * 128], lhsT=KDh[hf].ap[:, tt, :], rhs=Vt[h2].ap[:, tt, :], start=True, stop=True),
                    reads=[KDh[hf].buf, Vt[h2].buf], writes=[ps.buf])
            if h2 == 0:
                S.op("act", lambda e, ps=ps, g=g, hs_=hs_: e.copy(out=KV.ap[hs_, 4 * g:4 * g + 4, :], in_=ps.ap[hs_, :].rearrange("p (a b) -> p a b", a=4)), reads=[ps.buf], writes=[KV.buf])
            else:
                S.op("dve", lambda e, ps=ps, g=g, hs_=hs_: e.tensor_copy(out=KV.ap[hs_, 4 * g:4 * g + 4, :], in_=ps.ap[hs_, :].rearrange("p (a b) -> p a b", a=4)), reads=[ps.buf], writes=[KV.buf])
    if globals().get("GLA_STOP", 9) <= 1.7:
        ar.release()
        return
    S.op("pool", lambda e: e.memset(Sst.ap[:, 0, :], 0.0), writes=[Sst.buf])
    for n in range(63):
        S.op("dve", lambda e, n=n: e.scalar_tensor_tensor(out=Sst.ap[:, n + 1, :], in0=Sst.ap[:, n, :], scalar=dec.ap[:, n:n + 1], in1=KV.ap[:, n, :], op0=ALU.mult, op1=ALU.add),
             reads=[Sst.buf, dec.buf, KV.buf], writes=[Sst.buf])
    S.op("pool", lambda e: e.tensor_copy(out=Spb.ap, in_=Sst.ap[:, 0:64, :]), reads=[Sst.buf], writes=[Spb.buf])
    if globals().get("GLA_STOP", 9) <= 2:
        ar.release()
        return
    at4 = [ar.alloc(f"g_at{j}", [512], BF16) for j in range(2)]
    gob = [ar.alloc(f"g_go{j}", [512], BF16) for j in range(2)]
    o_t = ar.alloc("g_o", [512], F32)
    sq_t = ar.alloc("g_sq", [512], F32)
    rstd_t = ar.alloc("g_rstd", [512], F32)
    sil = ar.alloc("g_sil", [512], F32)
    yb = [ar.alloc(f"g_yb{j}", [512], BF16) for j in range(2)]
    it = 0
    for h2 in range(2):
        for g4 in range(8):
            pa, po, pn = cx.psum[it % 2], cx.psum[2 + it % 2], cx.psum[4 + it % 2]
            at, go, y = at4[it % 2], gob[it % 2], yb[it % 2]
            it += 1
            src, sb = cx.seq_src(R_GO + h2 * 128, 128, g4 * 512, 512)
            S.dma("sp", go.ap, src, reads=[sb], writes=[go.buf])
            for j in range(4):
                tt = 4 * g4 + j
                ts = slice(tt * 128, (tt + 1) * 128)
                S.op("pe", lambda e, pa=pa, j=j, ts=ts, h2=h2: e.matmul(pa.ap[:, j * 128:(j + 1) * 128], lhsT=keh[h2].ap[:, ts], rhs=qe.ap[:, ts], start=True, stop=True),
                     reads=[keh[h2].buf, qe.buf], writes=[pa.buf])
            S.op("dve", lambda e, pa=pa, at=at: e.tensor_tensor(out=at.ap.rearrange("p (a b) -> p a b", a=4), in0=pa.ap.rearrange("p (a b) -> p a b", a=4),
                                                                 in1=cs["gmask"].ap.unsqueeze(1).broadcast_to([128, 4, 128]), op=ALU.mult),
                 reads=[pa.buf, cs["gmask"].buf], writes=[at.buf])
            for j in range(4):
                tt = 4 * g4 + j
                S.op("pe", lambda e, po=po, j=j, tt=tt, at=at, h2=h2: e.matmul(po.ap[:, j * 128:(j + 1) * 128], lhsT=Vt[h2].ap[:, tt, :], rhs=at.ap[:, j * 128:(j + 1) * 128], start=True, stop=False),
                     reads=[Vt[h2].buf, at.buf], writes=[po.buf])
                for hf in range(2):
                    n = 2 * tt + hf
                    S.op("pe", lambda e, po=po, j=j, hf=hf, n=n, h2=h2: e.matmul(po.ap[:, j * 128 + hf * 64:j * 128 + hf * 64 + 64], lhsT=Spb.ap[:, n, :], rhs=qeh[h2].ap[:, n * 64:(n + 1) * 64],
                                                                            start=False, stop=(hf == 1)),
                         reads=[Spb.buf, qeh[h2].buf], writes=[po.buf])
            S.op("act", lambda e, po=po: e.copy(out=o_t.ap, in_=po.ap), reads=[po.buf], writes=[o_t.buf])
            S.op("pool", lambda e: e.tensor_tensor(out=sq_t.ap, in0=o_t.ap, in1=o_t.ap, op=ALU.mult), reads=[o_t.buf], writes=[sq_t.buf])
            S.op("pe", lambda e, pn=pn: e.matmul(pn.ap, lhsT=cx.ones.ap, rhs=sq_t.ap, start=True, stop=True), reads=[cx.ones.buf, sq_t.buf], writes=[pn.buf])
            ln_rstd(cx, rstd_t, pn, 128, cx.epsc)
            S.op("act", lambda e, go=go: e.activation(out=sil.ap, in_=go.ap, func=AF.Silu), reads=[go.buf], writes=[sil.buf])
            S.op("dve", lambda e: e.scalar_tensor_tensor(out=o_t.ap, in0=o_t.ap, scalar=pm.ap[:, PM_GNORM:PM_GNORM + 1], in1=rstd_t.ap, op0=ALU.mult, op1=ALU.mult),
                 reads=[o_t.buf, pm.buf, rstd_t.buf], writes=[o_t.buf])
            S.op("dve", lambda e, y=y: e.tensor_tensor(out=y.ap, in0=o_t.ap, in1=sil.ap, op=ALU.mult), reads=[o_t.buf, sil.buf], writes=[y.buf])
            dst, db = cx.y_dst(Y_GLA + h2 * 128, g4 * 512, 512)
            S.dma("sp", dst, y.ap, reads=[y.buf], writes=[db])
    ar.release()


def phase_B(cx, l, after_attn=None):
    S, ar = cx.S, cx.arena
    ar.mark()
    cs = mixer_consts(cx, l)
    attention_all(cx, l, cs)
    if after_attn is not None:
        after_attn()
    ar.mark()
    alive = [lru_chunk(cx, l, c, cs) for c in range(2)]
    while alive:
        for g in list(alive):
            try:
                next(g)
            except StopIteration:
                alive.remove(g)
    ar.release()
    gla_heads(cx, l, cs)
    ar.release()


def host_consts():
    pos = np.arange(S, dtype=np.float32)
    j = np.arange(128) % 64
    inv = (10000.0 ** (-(2.0 * (j % 32)).astype(np.float32) / 64.0)).astype(np.float32)
    ang = inv[:, None] * pos[None, :]
    cos = np.cos(ang).astype(np.float32)
    sin = np.sin(ang).astype(np.float32)
    cb = np.zeros((128, 640), np.float32)
    for m in range(128):
        jj = m % 64
        base = m - jj
        if jj < 32:
            cb[base + jj + 32, CB_RM + m] = -1.0
        else:
            cb[base + jj - 32, CB_RM + m] = 1.0
    cb[:, CB_ID:CB_ID + 128] = np.eye(128)
    kk = np.arange(128)
    cb[:, CB_TRI:CB_TRI + 128] = (kk[:, None] <= kk[None, :])
    cb[:, CB_ONES:CB_ONES + 128] = 1.0
    cb[:, CB_GM:CB_GM + 128] = (kk[:, None] <= kk[None, :]) & ((kk[:, None] // 64) == (kk[None, :] // 64))
    return cos, sin, cb.astype(ml_dtypes.bfloat16)


def pm_layout(P, l, r):
    pm = np.zeros((128, NPM), np.float32)
    for c in range(2):
        ch = slice(256 * r + 128 * c, 256 * r + 128 * (c + 1))
        pm[:, PM_CW + 4 * c:PM_CW + 4 * c + 4] = P['conv_w'][l][:, ch].T
        pm[:, PM_CB + c] = P['conv_b'][l][ch]
        pm[:, PM_BA + c] = P['b_rgate'][l][ch]
        pm[:, PM_BX + c] = P['b_igate'][l][ch]
        pm[:, PM_LAM + c] = P['lru_lambda'][l][ch]
        for k2, wn in ((0, 'w_rgate'), (2, 'w_igate')):
            for bb in range(2):
                blk = 4 * r + 2 * c + bb
                pm[64 * bb:64 * bb + 64, PM_BD + (k2 + c) * 128 + 64 * bb:PM_BD + (k2 + c) * 128 + 64 * bb + 64] = P[wn][l][blk]
    pm[:, PM_SUBLN] = P['diff_subln'][l]
    pm[:, PM_GNORM] = P['gla_norm'][l]
    pm[:, PM_BG] = P['b_gla_gate'][l][128 * r:128 * r + 128]
    pm[0:16, PM_WG:PM_WG + 128] = P['w_gla_gate_up'][l][:, 128 * r:128 * r + 128]
    for k2, nm in enumerate(('lambda_q1', 'lambda_k1', 'lambda_q2', 'lambda_k2')):
        pm[:, PM_LV + 64 * k2:PM_LV + 64 * k2 + 64] = P[nm][l][None, :]
    return pm


SEND_CH = [(0, 512), (512, 512), (1024, 512), (1536, 512), (2048, 512), (2560, 384)]


def build_program():
    from contextlib import ExitStack
    nc = bass.Bass("TRN2", target_bir_lowering=False)
    with ExitStack() as st:
        cx = make_ctx(nc, st)
        S_ = cx.S
        ext = lambda name, shape, dt: (nc.dram_tensor(name, shape, dt).ap() if globals().get("NOEXT") else nc.dram_tensor(name, shape, dt, kind="ExternalInput").ap())
        xT = ext("xT", [D, T], F32)
        def wl(name, shp):
            if globals().get("NOEXT"):
                return [nc.dram_tensor(f"{name}{l}", shp, F32).ap() for l in range(L)]
            t = ext(name, [L] + shp, F32)
            return [t[l] for l in range(L)]
        cx.d_w_in = wl("w_in", [2 * NHC, 1, 128, KC * 128])
        cx.d_w_out = wl("w_out", [KC, 1, 128, KC * 128])
        cx.d_w_gu = wl("w_gu", [2 * NFF, 1, 128, KC * 128])
        cx.d_w_down = wl("w_down", [KC, 2, 128, 22 * 128])
        gains = ext("gains", [L, 4, 128, KC], F32)
        pm = ext("pm", [L, 128, NPM], F32)
        cx.d_cbf = ext("cbf", [128, 640], BF16)
        cx.d_cos = ext("cos", [128, S], F32)
        cx.d_sin = ext("sin", [128, S], F32)
        out = nc.dram_tensor("out", [D, T], F32, kind="ExternalOutput").ap()
        cx.d_pre_mix = [gains[l, 0] for l in range(L)]
        cx.d_post_mix = [gains[l, 1] for l in range(L)]
        cx.d_pre_ffn = [gains[l, 2] for l in range(L)]
        cx.d_post_ffn = [gains[l, 3] for l in range(L)]
        cx.d_pm = [pm[l] for l in range(L)]
        cx.d_spill = nc.dram_tensor("spill", [D, 1024], F32).ap()
        x1d = nc.dram_tensor("x1d", [D, T], F32).ap()
        xa = nc.dram_tensor("xa", [D, T], F32).ap()
        xb = nc.dram_tensor("xb", [D, T], F32).ap()
        SEQR = 3072
        seq = nc.dram_tensor("seq", [SEQR, S], BF16).ap()
        mine1 = nc.dram_tensor("mine1", [HC, T], BF16).ap()
        send1 = nc.dram_tensor("send1", [SEQR, T], BF16).ap()
        recvall = nc.dram_tensor("recvall", [6, 1024, T], BF16).ap()
        yfull = nc.dram_tensor("yfull", [YH, S], BF16).ap()
        ymine = nc.dram_tensor("ymine", [YH, T], BF16).ap()
        ysend = nc.dram_tensor("ysend", [YH, T], BF16).ap()
        recv2all = nc.dram_tensor("recv2all", [2, 1024, T], BF16).ap()
        yoth = nc.dram_tensor("yoth", [YH, T], BF16).ap()
        seqh = seq.rearrange("r (h t) -> h r t", h=2)
        seqhj = seq.rearrange("(j r) (h t) -> h j r t", r=512, h=2)
        rva = recvall.rearrange("j (s r) t -> s j r t", s=2)
        yfh = yfull.rearrange("r (h t) -> h r t", h=2)
        rv2 = recv2all.rearrange("j (s r) t -> s j r t", s=2)
        pars = {}

        def par(e):
            k = id(e)
            if k not in pars:
                pars[k] = e.partition_id() % 2
            return pars[k]

        cx.seq_src = lambda row0, nrows, tok0, n: (seq[row0:row0 + nrows, tok0:tok0 + n], dbuf(cx, "seq", row0 // 128))
        cx.y_dst = lambda row0, tok0, n: (yfull[row0:row0 + 128, tok0:tok0 + n], dbuf(cx, "yfull", row0 // 128))
        mine_dst = lambda oc, tg: (mine1[oc * 128:(oc + 1) * 128, tg * 512:(tg + 1) * 512], dbuf(cx, "mine1", oc))
        send_dst = lambda oc, tg: (send1[oc * 128:(oc + 1) * 128, tg * 512:(tg + 1) * 512], dbuf(cx, "send1", (oc * 128) // 512))

        def y_load(kc, t0, yt):
            if kc < 8:
                S_.dma("sp", yt.ap[:, kc, :], ymine[kc * 128:(kc + 1) * 128, t0:t0 + 1024], reads=[dbuf(cx, "ymine")], writes=[yt.buf])
            else:
                S_.dma("sp", yt.ap[:, kc, :], yoth[(kc - 8) * 128:(kc - 7) * 128, t0:t0 + 1024], reads=[dbuf(cx, "yoth")], writes=[yt.buf])

        def copy_mine(r0, r1):
            S_.dma("pool", lambda e: seqh[bass.ds(par(e), 1), r0:r1, :].rearrange("h r t -> (h r) t"), mine1[r0:r1, :],
                   reads=[dbuf(cx, "mine1", oc) for oc in range(r0 // 128, r1 // 128)], writes=[dbuf(cx, "seq", oc) for oc in range(r0 // 128, r1 // 128)])

        def gather1(j):
            S_.op("pool", lambda e: e.collective_compute("AllGather", ALU.bypass, replica_groups=RG,
                                                          ins=[send1[512 * j:512 * (j + 1), :].opt()], outs=[recvall[j].opt()]),
                  reads=[dbuf(cx, "send1", j)], writes=[dbuf(cx, "recv1", j)], kind="x")

        def copy_recv(j0, j1):
            S_.dma("pool", lambda e: seqhj[bass.ds(1 - par(e), 1), j0:j1, :, :].rearrange("h j r t -> (h j) r t"),
                   lambda e: rva[bass.ds(1 - par(e), 1), j0:j1, :, :].rearrange("s j r t -> (s j) r t"),
                   reads=[dbuf(cx, "recv1", j) for j in range(j0, j1)], writes=[dbuf(cx, "seq", oc) for oc in range(4 * j0, 4 * j1)])

        def post_store_A(oc):
            if oc >= NHC:
                ocs = oc - NHC
                if ocs % 4 == 3 or ocs == NHC - 1:
                    j = ocs // 4
                    gather1(j)
                    if j == 2:
                        copy_recv(0, 3)
                    elif j == 5:
                        copy_recv(3, 6)
            elif oc == 11:
                copy_mine(0, 1536)
            elif oc == NHC - 1:
                copy_mine(1536, HC)

        def exchange_y(j):
            ybufs = [dbuf(cx, "yfull", k) for k in range(4 * j, 4 * j + 4)]
            rs_ = slice(512 * j, 512 * (j + 1))
            S_.dma("pool", ymine[rs_, :], lambda e: yfh[bass.ds(par(e), 1), rs_, :].rearrange("h r t -> (h r) t"), reads=ybufs, writes=[dbuf(cx, "ymine", j)])
            S_.dma("pool", ysend[rs_, :], lambda e: yfh[bass.ds(1 - par(e), 1), rs_, :].rearrange("h r t -> (h r) t"), reads=ybufs, writes=[dbuf(cx, "ysend", j)])
            S_.op("pool", lambda e: e.collective_compute("AllGather", ALU.bypass, replica_groups=RG,
                                                          ins=[ysend[rs_, :].opt()], outs=[recv2all[j].opt()]),
                  reads=[dbuf(cx, "ysend", j)], writes=[dbuf(cx, "recv2", j)], kind="x")
            S_.dma("pool", yoth[rs_, :], lambda e: rv2[bass.ds(1 - par(e), 1), j, :, :].rearrange("s r t -> (s r) t"),
                   reads=[dbuf(cx, "recv2", j)], writes=[dbuf(cx, "yoth", j)])

        def y_load(kc, t0, yt):
            j = (kc % 8) // 4
            if kc < 8:
                S_.dma("sp", yt.ap[:, kc, :], ymine[kc * 128:(kc + 1) * 128, t0:t0 + 1024], reads=[dbuf(cx, "ymine", j)], writes=[yt.buf])
            else:
                S_.dma("sp", yt.ap[:, kc, :], yoth[(kc - 8) * 128:(kc - 7) * 128, t0:t0 + 1024], reads=[dbuf(cx, "yoth", j)], writes=[yt.buf])

        order_A = list(range(NHC, 2 * NHC)) + list(range(NHC))
        x_cur = xT
        for l in range(L):
            phase_A(cx, l, x_cur, mine_dst, send_dst, oc_order=order_A, post_store=post_store_A)
            phase_B(cx, l, after_attn=lambda: exchange_y(0))
            exchange_y(1)
            x_next = out if l == L - 1 else (xa if l % 2 == 0 else xb)
            phase_C(cx, l, x_cur, x1d, x_next, y_load)
            x_cur = x_next
        cx.S.emit()
        n_ops = cx.S.n_inst
    return nc, n_ops


def make_in_maps(inp):
    P = {k: np.asarray(v, dtype=np.float32) for k, v in inp.items()}
    cos, sin, cbf = host_consts()
    w_in = [np.stack([w_in_layout(P['w_in'][l], r) for l in range(L)]) for r in range(2)]
    w_out = [np.stack([w_out_layout(P['w_out'][l], r) for l in range(L)]) for r in range(2)]
    w_gu = np.stack([w_gu_layout(P['w_ffn_gate'][l], P['w_ffn_up'][l]) for l in range(L)])
    w_dn = np.stack([tile_w(P['w_ffn_down'][l], 22) for l in range(L)])
    gains = np.stack([np.stack([gcol_layout(P[nm][l]) for nm in ('pre_mix_norm', 'post_mix_norm', 'pre_ffn_norm', 'post_ffn_norm')]) for l in range(L)])
    pm = [np.stack([pm_layout(P, l, r) for l in range(L)]) for r in range(2)]
    maps = []
    for c in range(NCORES):
        b, r = c // 2, c % 2
        maps.append({
            "xT": np.ascontiguousarray(P['x'][b, r * T:(r + 1) * T, :].T),
            "w_in": w_in[r], "w_out": w_out[r], "w_gu": w_gu, "w_down": w_dn,
            "gains": gains, "pm": pm[r], "cbf": cbf, "cos": cos, "sin": sin,
        })
    return maps


def kernel_fused(**inputs):
    nc, _ = build_program()
    maps = make_in_maps(inputs)
    res = run_bass_kernel_spmd(nc, maps, core_ids=list(range(NCORES)))
    outp = np.empty((B, S, D), np.float32)
    for c in range(NCORES):
        b, r = c // 2, c % 2
        outp[b, r * T:(r + 1) * T, :] = np.asarray(res.results[c]["out"], dtype=np.float32).T
    return outp


def _prog(builder):
    from contextlib import ExitStack
    nc = bass.Bass("TRN2", target_bir_lowering=False)
    with ExitStack() as st:
        cx = make_ctx(nc, st)
        builder(nc, cx)
        cx.S.emit()
    return nc


def build_A():
    def b(nc, cx):
        xT = nc.dram_tensor("xT", [D, T], F32, kind="ExternalInput").ap()
        w_in = nc.dram_tensor("w_in", [2 * NHC, 1, 128, KC * 128], F32, kind="ExternalInput").ap()
        gpre = nc.dram_tensor("gpre", [128, KC], F32, kind="ExternalInput").ap()
        mine = nc.dram_tensor("mine", [HC, T], BF16, kind="ExternalOutput").ap()
        send = nc.dram_tensor("send", [HC, T], BF16, kind="ExternalOutput").ap()
        cx.d_w_in = [w_in]
        cx.d_pre_mix = [gpre]
        md = lambda oc, tg: (mine[oc * 128:(oc + 1) * 128, tg * 512:(tg + 1) * 512], dbuf(cx, "mine", oc))
        sd = lambda oc, tg: (send[oc * 128:(oc + 1) * 128, tg * 512:(tg + 1) * 512], dbuf(cx, "send", oc))
        phase_A(cx, 0, xT, md, sd)
    return _prog(b)


def build_B(l):
    def b(nc, cx):
        seq = nc.dram_tensor("seq", [HC, S], BF16, kind="ExternalInput").ap()
        pm = nc.dram_tensor("pm", [128, NPM], F32, kind="ExternalInput").ap()
        cx.d_pm = {l: pm}
        cx.d_cbf = nc.dram_tensor("cbf", [128, 640], BF16, kind="ExternalInput").ap()
        cx.d_cos = nc.dram_tensor("cos", [128, S], F32, kind="ExternalInput").ap()
        cx.d_sin = nc.dram_tensor("sin", [128, S], F32, kind="ExternalInput").ap()
        yT = nc.dram_tensor("yT", [YH, S], BF16, kind="ExternalOutput").ap()
        cx.seq_src = lambda row0, nrows, tok0, n: (seq[row0:row0 + nrows, tok0:tok0 + n], dbuf(cx, "seq", 0))
        cx.y_dst = lambda row0, tok0, n: (yT[row0:row0 + 128, tok0:tok0 + n], dbuf(cx, "y", row0))
        phase_B(cx, l)
    return _prog(b)


def build_C():
    def b(nc, cx):
        xT = nc.dram_tensor("xT", [D, T], F32, kind="ExternalInput").ap()
        yT = nc.dram_tensor("yT", [D, T], BF16, kind="ExternalInput").ap()
        cx.d_w_out = [nc.dram_tensor("w_out", [KC, 1, 128, KC * 128], F32, kind="ExternalInput").ap()]
        cx.d_w_gu = [nc.dram_tensor("w_gu", [2 * NFF, 1, 128, KC * 128], F32, kind="ExternalInput").ap()]
        cx.d_w_down = [nc.dram_tensor("w_down", [KC, 2, 128, 22 * 128], F32, kind="ExternalInput").ap()]
        g = nc.dram_tensor("gains", [3, 128, KC], F32, kind="ExternalInput").ap()
        cx.d_post_mix, cx.d_pre_ffn, cx.d_post_ffn = [g[0]], [g[1]], [g[2]]
        cx.d_spill = nc.dram_tensor("spill", [D, 1024], F32).ap()
        x1 = nc.dram_tensor("x1", [D, T], F32).ap()
        x2 = nc.dram_tensor("x2", [D, T], F32, kind="ExternalOutput").ap()

        def y_load(kc, t0, yt):
            cx.S.dma("sp", yt.ap[:, kc, :], yT[kc * 128:(kc + 1) * 128, t0:t0 + 1024], writes=[yt.buf])
        phase_C(cx, 0, xT, x1, x2, y_load)
    return _prog(b)


def kernel_multi(**inputs):
    P = {k: np.asarray(v, dtype=np.float32) for k, v in inputs.items()}
    cos, sin, cbf = host_consts()
    cores = list(range(NCORES))
    xs = [np.ascontiguousarray(P['x'][c // 2, (c % 2) * T:(c % 2 + 1) * T, :].T) for c in cores]
    for l in range(L):
        wl = [w_in_layout(P['w_in'][l], r) for r in range(2)]
        gp = gcol_layout(P['pre_mix_norm'][l])
        res = run_bass_kernel_spmd(build_A(), [{"xT": xs[c], "w_in": wl[c % 2], "gpre": gp} for c in cores], core_ids=cores).results
        seqs = []
        for c in cores:
            r = c % 2
            halves = [None, None]
            halves[r] = res[c]["mine"]
            halves[1 - r] = res[c ^ 1]["send"]
            seqs.append(np.ascontiguousarray(np.concatenate(halves, axis=1)))
        del res
        pml = [pm_layout(P, l, r) for r in range(2)]
        res = run_bass_kernel_spmd(build_B(l), [{"seq": seqs[c], "pm": pml[c % 2], "cbf": cbf, "cos": cos, "sin": sin} for c in cores], core_ids=cores).results
        ys = []
        for c in cores:
            r = c % 2
            ys.append(np.ascontiguousarray(np.concatenate([res[c]["yT"][:, r * T:(r + 1) * T], res[c ^ 1]["yT"][:, r * T:(r + 1) * T]], axis=0)))
        del res, seqs
        wo = [w_out_layout(P['w_out'][l], r) for r in range(2)]
        wgu = w_gu_layout(P['w_ffn_gate'][l], P['w_ffn_up'][l])
        wd = tile_w(P['w_ffn_down'][l], 22)
        g3 = np.stack([gcol_layout(P[nm][l]) for nm in ('post_mix_norm', 'pre_ffn_norm', 'post_ffn_norm')])
        res = run_bass_kernel_spmd(build_C(), [{"xT": xs[c], "yT": ys[c], "w_out": wo[c % 2], "w_gu": wgu, "w_down": wd, "gains": g3} for c in cores], core_ids=cores).results
        xs = [np.asarray(res[c]["x2"]) for c in cores]
        del res, ys
    outp = np.empty((B, S, D), np.float32)
    for c in cores:
        outp[c // 2, (c % 2) * T:(c % 2 + 1) * T, :] = xs[c].T
    return outp


def kernel(**inputs):
    return kernel_fused(**inputs)
```

```python
import math
import numpy as np
import ml_dtypes
import concourse.bass as bass
import concourse.mybir as mybir
from concourse.bass_utils import run_bass_kernel_spmd

F32 = mybir.dt.float32
BF16 = mybir.dt.bfloat16
AF = mybir.ActivationFunctionType
ALU = mybir.AluOpType

D = 2048
B = 4
S = 4096
L = 4
T = 2048
NCORES = 8
HC = 2944
NHC = HC // 128
DFF = 5632
NFF = DFF // 128
KC = D // 128
EPS = 1e-6
RG = [[0, 1], [2, 3], [4, 5], [6, 7]]

R_Q, R_K, R_V, R_LG, R_LX, R_GQ, R_GK, R_GV, R_GO, R_GLR = 0, 512, 1024, 1536, 1792, 2048, 2176, 2304, 2560, 2816
YH = 1024
Y_ATT, Y_LRU, Y_GLA = 0, 512, 768


class Buf:
    __slots__ = ("name", "last_w", "readers")

    def __init__(self, name):
        self.name = name
        self.last_w = None
        self.readers = []


class _Op:
    __slots__ = ("q", "fn", "deps", "kind", "sig", "sem", "val", "slot_prev")

    def __init__(self, q, fn, deps, kind):
        self.q = q
        self.fn = fn
        self.deps = deps
        self.kind = kind
        self.sig = False
        self.sem = None
        self.val = 0
        self.slot_prev = None


QUEUES = ("pe", "act", "dve", "pool", "sp")
EPOCH = 30000


class Sched:
    def __init__(self, nc):
        self.nc = nc
        self.ops = []
        self.nslots = {"sp": 16, "act": 6, "pool": 8, "pe": 2, "dve": 2}

    def op(self, q, fn, reads=(), writes=(), kind="c"):
        idx = len(self.ops)
        deps = set()
        for b in reads:
            if b.last_w is not None:
                deps.add(b.last_w)
        for b in writes:
            if b.last_w is not None:
                deps.add(b.last_w)
            deps.update(b.readers)
        for b in reads:
            b.readers.append(idx)
        for b in writes:
            b.last_w = idx
            b.readers = []
        deps.discard(idx)
        self.ops.append(_Op(q, fn, deps, kind))
        return idx

    def dma(self, q, out, in_, reads=(), writes=(), **kw):
        def fn(e):
            o = out(e) if callable(out) else out
            i = in_(e) if callable(in_) else in_
            return e.dma_start(out=o, in_=i, **kw)
        return self.op(q, fn, reads, writes, kind="d")

    def emit(self):
        nc = self.nc
        ops = self.ops
        for o in ops:
            best = {}
            keep = set()
            for d in o.deps:
                p = ops[d]
                if p.kind == "c":
                    if p.q == "pe" and o.q == "pe":
                        continue
                    if p.q not in best or best[p.q] < d:
                        best[p.q] = d
                else:
                    keep.add(d)
            o.deps = keep | set(best.values())
            for d in o.deps:
                ops[d].sig = True
        sems = {}

        def get_sem(name):
            if name not in sems:
                sems[name] = nc.alloc_semaphore(name)
            return sems[name]

        ccount = {q: 0 for q in QUEUES}
        dcount = {q: 0 for q in QUEUES}
        xcount = 0
        slot_tot = {}
        slot_last = {}
        for i, o in enumerate(ops):
            if o.kind == "c":
                if not o.sig:
                    continue
                ccount[o.q] += 1
                ep = ccount[o.q] // EPOCH
                o.sem = get_sem(f"c_{o.q}_{ep}")
                o.val = ccount[o.q] - ep * EPOCH + (1 if ep > 0 else 0)
                if ep > 0:
                    o.val = ccount[o.q] - ep * EPOCH + 1
            elif o.kind == "d":
                k = dcount[o.q] % self.nslots[o.q]
                dcount[o.q] += 1
                name = f"d_{o.q}_{k}"
                o.sem = get_sem(name)
                slot_tot[name] = slot_tot.get(name, 0) + 16
                o.val = slot_tot[name]
                o.slot_prev = slot_last.get(name)
                slot_last[name] = i
                o.sig = True
            else:
                k = xcount % 4
                xcount += 1
                name = f"x_{k}"
                o.sem = get_sem(name)
                slot_tot[name] = slot_tot.get(name, 0) + 1
                o.val = slot_tot[name]
                o.slot_prev = slot_last.get(name)
                slot_last[name] = i
                o.sig = True
        per_q = {q: [] for q in QUEUES}
        for i, o in enumerate(ops):
            per_q[o.q].append(i)
        final_waits = [(o.sem, o.val) for o in (ops[i] for i in slot_last.values())]
        self.n_inst = len(ops)

        def run_queue(q, e):
            known = {}
            for i in per_q[q]:
                o = ops[i]
                deps = set(o.deps)
                if o.slot_prev is not None:
                    deps.add(o.slot_prev)
                need = {}
                for d in deps:
                    p = ops[d]
                    key = id(p.sem)
                    if key not in need or need[key][1] < p.val:
                        need[key] = (p.sem, p.val)
                for key, (sem, val) in need.items():
                    if known.get(key, 0) >= val:
                        continue
                    e.wait_ge(sem, val)
                    known[key] = val
                ins = o.fn(e)
                if o.sig:
                    if o.kind == "d":
                        ins.then_inc(o.sem, 16)
                    elif o.kind == "x":
                        ins.then_inc(o.sem)
                    else:
                        ins.then_inc(o.sem, 1)
            if q == "sp":
                for sem, val in final_waits:
                    e.wait_ge(sem, val)

        with nc.Block() as block:
            @block.tensor
            def _(e):
                run_queue("pe", e)

            @block.scalar
            def _(e):
                run_queue("act", e)

            @block.vector
            def _(e):
                run_queue("dve", e)

            @block.gpsimd
            def _(e):
                run_queue("pool", e)

            @block.sync
            def _(e):
                run_queue("sp", e)


class Tile:
    __slots__ = ("ap", "buf")

    def __init__(self, ap, buf):
        self.ap = ap
        self.buf = buf


class Arena:
    def __init__(self, base_ap, nwords):
        self.base = base_ap
        self.nwords = nwords
        self.top = 0
        self.live = []
        self.retired = []
        self.marks = []

    def alloc(self, name, free_shape, dtype):
        n = 1
        for s in free_shape:
            n *= s
        words = (n + 1) // 2 if dtype == BF16 else n
        words = (words + 7) // 8 * 8
        start = self.top
        end = start + words
        assert end <= self.nwords, f"SBUF arena overflow allocating {name}: {end*4} > {self.nwords*4}"
        self.top = end
        buf = Buf(name)
        for (s0, e0, b0) in self.retired:
            if s0 < end and start < e0:
                if b0.last_w is not None:
                    buf.readers.append(b0.last_w)
                buf.readers.extend(b0.readers)
        self.live.append((start, end, buf))
        ap = self.base[:, start:end]
        if dtype == BF16:
            ap = ap.bitcast(BF16)[:, 0:n]
        else:
            ap = ap[:, 0:n]
        if len(free_shape) == 2:
            ap = ap.rearrange("p (a b) -> p a b", a=free_shape[0])
        elif len(free_shape) == 3:
            ap = ap.rearrange("p (a b c) -> p a b c", a=free_shape[0], b=free_shape[1])
        return Tile(ap, buf)

    def mark(self):
        self.marks.append((self.top, len(self.live)))

    def release(self):
        top, nl = self.marks.pop()
        self.retired.extend(self.live[nl:])
        del self.live[nl:]
        self.top = top


class Ctx:
    pass


def dense(cx, name, w_ap, n_oc, n_kg, KG, A, Tn, evac, scale=None, banks=(2, 3, 4, 5), wq="sp", oc_order=None, nbuf=2):
    S, ar = cx.S, cx.arena
    ar.mark()
    wst = [ar.alloc(f"{name}_wst{j}", [KG, 128], F32) for j in range(nbuf)]
    wb = [ar.alloc(f"{name}_wb{j}", [KG, 128], BF16) for j in range(nbuf)]
    steps = [(oc, kg) for oc in (oc_order if oc_order is not None else range(n_oc)) for kg in range(n_kg)]
    ntg = Tn // 512

    def load(i):
        oc, kg = steps[i]
        t = wst[i % nbuf]
        S.dma(wq, t.ap, w_ap[oc, kg].rearrange("p (k n) -> p k n", k=KG), writes=[t.buf])

    def cast(i):
        oc, kg = steps[i]
        src, dst = wst[i % nbuf], wb[i % nbuf]
        if scale is None:
            S.op("act", lambda e, d=dst, s=src: e.copy(out=d.ap, in_=s.ap), reads=[src.buf], writes=[dst.buf])
        else:
            sc = scale.ap[:, kg * KG:(kg + 1) * KG].unsqueeze(2).broadcast_to([128, KG, 128])
            S.op("dve", lambda e, d=dst, s=src, sc=sc: e.tensor_tensor(out=d.ap, in0=s.ap, in1=sc, op=ALU.mult),
                 reads=[src.buf, scale.buf], writes=[dst.buf])

    for i0 in range(min(nbuf, len(steps))):
        load(i0)
    cast(0)
    bi = 0
    cur_bank = {}
    for i, (oc, kg) in enumerate(steps):
        if i + 1 < len(steps):
            cast(i + 1)
        if i + nbuf < len(steps):
            load(i + nbuf)
        w = wb[i % nbuf]
        for tg in range(ntg):
            if kg == 0:
                cur_bank[tg] = banks[bi % len(banks)]
                bi += 1
            bk = cur_bank[tg]
            ps = cx.psum[bk]
            for kc in range(KG):
                kk = kg * KG + kc
                S.op("pe", lambda e, ps=ps, w=w, kc=kc, kk=kk, tg=tg, st=(kk == 0), sp=(kk == n_kg * KG - 1):
                     e.matmul(ps.ap, lhsT=w.ap[:, kc, :], rhs=A.ap[:, kk, tg * 512:(tg + 1) * 512], start=st, stop=sp),
                     reads=[w.buf, A.buf], writes=[ps.buf])
            if kg == n_kg - 1:
                evac(oc, tg, bk)
    ar.release()


def dbuf(cx, name, key=0):
    k = (name, key)
    if k not in cx.dbufs:
        cx.dbufs[k] = Buf(f"{name}_{key}")
    return cx.dbufs[k]


def load_x_pass(cx, x_dram, t0, xg):
    xv = x_dram.rearrange("(k p) t -> p k t", p=128)
    for g in range(4):
        cx.S.dma("sp", xg[g].ap, xv[:, 4 * g:4 * g + 4, t0:t0 + 1024], reads=[dbuf(cx, "x", x_dram.tensor.name)], writes=[xg[g].buf])


def rms_stats(cx, src_fn, rstd, c0, bank):
    S = cx.S
    ps = cx.psum[bank]
    for kc in range(KC):
        sqt = cx.sq[kc % 2]
        ap, buf = src_fn(kc)
        S.op("act", lambda e, o=sqt, a=ap: e.activation(out=o.ap, in_=a, func=AF.Square), reads=[buf], writes=[sqt.buf])
        S.op("pe", lambda e, o=sqt, kc=kc: e.matmul(ps.ap, lhsT=cx.ones.ap, rhs=o.ap, start=(kc == 0), stop=(kc == KC - 1)),
             reads=[sqt.buf, cx.ones.buf], writes=[ps.buf])
    r = rstd.ap[:, c0:c0 + 512]
    S.op("act", lambda e: e.activation(out=r, in_=ps.ap, func=AF.Sqrt, scale=1.0 / D, bias=cx.epsc.ap), reads=[ps.buf, cx.epsc.buf], writes=[rstd.buf])
    S.op("dve", lambda e: e.reciprocal(out=r, in_=r), reads=[rstd.buf], writes=[rstd.buf])


def norm_to_bf16(cx, x_dram, hT, rstd, Tn, tbase):
    S, ar = cx.S, cx.arena
    for ps_ in range(Tn // 1024):
        ar.mark()
        xg = [ar.alloc(f"xg{g}", [4, 1024], F32) for g in range(4)]
        load_x_pass(cx, x_dram, tbase + ps_ * 1024, xg)
        for tg in range(2):
            c0 = ps_ * 1024 + tg * 512
            rms_stats(cx, lambda kc, tg=tg: (xg[kc // 4].ap[:, kc % 4, tg * 512:(tg + 1) * 512], xg[kc // 4].buf), rstd, c0, tg)
            rb = rstd.ap[:, c0:c0 + 512].unsqueeze(1).broadcast_to([128, 4, 512])
            for g in range(4):
                S.op("dve", lambda e, g=g, tg=tg, c0=c0, rb=rb: e.tensor_tensor(
                    out=hT.ap[:, 4 * g:4 * g + 4, c0:c0 + 512], in0=xg[g].ap[:, :, tg * 512:(tg + 1) * 512], in1=rb, op=ALU.mult),
                    reads=[xg[g].buf, rstd.buf], writes=[hT.buf])
        ar.release()


def phase_A(cx, l, x_dram, mine_dst, send_dst, oc_order=None, post_store=None):
    S, ar = cx.S, cx.arena
    ar.mark()
    hT = ar.alloc("hT", [KC, T], BF16)
    gcol = ar.alloc("gpre", [KC], F32)
    rstd = ar.alloc("rstdA", [T], F32)
    S.dma("sp", gcol.ap, cx.d_pre_mix[l], writes=[gcol.buf])
    norm_to_bf16(cx, x_dram, hT, rstd, T, 0)
    ob = [ar.alloc(f"obA{j}", [512], BF16) for j in range(4)]
    cnt = [0]

    def evac(oc, tg, bk):
        o = ob[cnt[0] % 4]
        ps = cx.psum[bk]
        if cnt[0] % 2 == 0:
            S.op("act", lambda e: e.copy(out=o.ap, in_=ps.ap), reads=[ps.buf], writes=[o.buf])
        else:
            S.op("dve", lambda e: e.tensor_copy(out=o.ap, in_=ps.ap), reads=[ps.buf], writes=[o.buf])
        cnt[0] += 1
        dst, db = mine_dst(oc, tg) if oc < NHC else send_dst(oc - NHC, tg)
        S.dma("sp", dst, o.ap, reads=[o.buf], writes=[db])
        if post_store is not None and tg == T // 512 - 1:
            post_store(oc)

    dense(cx, "inproj", cx.d_w_in[l], 2 * NHC, 1, KC, hT, T, evac, scale=gcol, banks=(2, 3, 4, 5, 6, 7), oc_order=oc_order, nbuf=3)
    ar.release()


def make_ctx(nc, stack):
    cx = Ctx()
    cx.nc = nc
    cx.S = Sched(nc)
    cx.dbufs = {}
    NW = 49152
    base = stack.enter_context(nc.sbuf_tensor("arena", [128, NW], F32))
    cx.arena = Arena(base, NW)
    cx.psum = []
    for i in range(8):
        t = stack.enter_context(nc.psum_tensor(f"psb{i}", [128, 512], F32))
        cx.psum.append(Tile(t[:, :], Buf(f"psum{i}")))
    cx.ones = cx.arena.alloc("ones", [128], F32)
    cx.epsc = cx.arena.alloc("epsc", [1], F32)
    cx.sq = [cx.arena.alloc(f"sq{j}", [512], F32) for j in range(2)]
    cx.S.op("pool", lambda e: e.memset(cx.ones.ap, 1.0), writes=[cx.ones.buf])
    cx.S.op("pool", lambda e: e.memset(cx.epsc.ap, EPS), writes=[cx.epsc.buf])
    return cx


def tile_w(w, KG):
    K, N = w.shape
    n_oc, n_kc = N // 128, K // 128
    n_kg = n_kc // KG
    return np.ascontiguousarray(w.reshape(n_kg, KG, 128, n_oc, 128).transpose(3, 0, 2, 1, 4)).reshape(n_oc, n_kg, 128, KG * 128)


def half_cols(h):
    idx = []
    idx += list(range(0 + 512 * h, 0 + 512 * h + 512))
    idx += list(range(1024 + 512 * h, 1024 + 512 * h + 512))
    idx += list(range(2048 + 512 * h, 2048 + 512 * h + 512))
    idx += list(range(3072 + 256 * h, 3072 + 256 * h + 256))
    idx += list(range(3584 + 256 * h, 3584 + 256 * h + 256))
    idx += list(range(4096 + 128 * h, 4096 + 128 * h + 128))
    idx += list(range(4352 + 128 * h, 4352 + 128 * h + 128))
    idx += list(range(4608 + 256 * h, 4608 + 256 * h + 256))
    idx += list(range(5120 + 256 * h, 5120 + 256 * h + 256))
    idx += list(range(5632, 5648))
    return idx


def w_in_layout(w, r):
    out = np.zeros((D, 2 * HC), np.float32)
    out[:, 0:2832] = w[:, half_cols(r)]
    out[:, HC:HC + 2832] = w[:, half_cols(1 - r)]
    return tile_w(out, KC)


def gcol_layout(g):
    return np.ascontiguousarray(g.reshape(-1, 128).T)


def proj_post_residual(cx, name, w_ap, n_kg, KG, A, g_dram_ap, x_in, x_out, t0, h_out=None):
    S, ar = cx.S, cx.arena
    Tn = 1024
    ar.mark()
    rstd = ar.alloc(f"{name}_rstd", [Tn], F32)
    gcol = ar.alloc(f"{name}_g", [KC], F32)
    S.dma("sp", gcol.ap, g_dram_ap, writes=[gcol.buf])
    mo = [ar.alloc(f"{name}_mo{j}", [512], F32) for j in range(3)]
    cnt = [0]
    spill = cx.d_spill

    def evac(oc, tg, bk):
        o = mo[cnt[0] % 3]
        cnt[0] += 1
        ps = cx.psum[bk]
        S.op("act", lambda e: e.copy(out=o.ap, in_=ps.ap), reads=[ps.buf], writes=[o.buf])
        sqt = cx.sq[cnt[0] % 2]
        S.op("dve", lambda e: e.tensor_tensor(out=sqt.ap, in0=ps.ap, in1=o.ap, op=ALU.mult), reads=[ps.buf, o.buf], writes=[sqt.buf])
        acc = cx.psum[tg]
        S.op("pe", lambda e: e.matmul(acc.ap, lhsT=cx.ones.ap, rhs=sqt.ap, start=(oc == 0), stop=(oc == KC - 1)),
             reads=[sqt.buf, cx.ones.buf], writes=[acc.buf])
        S.dma("sp", spill[oc * 128:(oc + 1) * 128, tg * 512:(tg + 1) * 512], o.ap, reads=[o.buf], writes=[dbuf(cx, "spill", oc)])

    dense(cx, name, w_ap, KC, n_kg, KG, A, Tn, evac, banks=(2, 3, 4, 5, 6, 7))
    for tg in range(2):
        r = rstd.ap[:, tg * 512:(tg + 1) * 512]
        acc = cx.psum[tg]
        S.op("act", lambda e, r=r, acc=acc: e.activation(out=r, in_=acc.ap, func=AF.Sqrt, scale=1.0 / D, bias=cx.epsc.ap),
             reads=[acc.buf, cx.epsc.buf], writes=[rstd.buf])
        S.op("dve", lambda e, r=r: e.reciprocal(out=r, in_=r), reads=[rstd.buf], writes=[rstd.buf])
    xp = None
    if h_out is not None:
        xp = ar.alloc(f"{name}_xp", [KC, Tn], F32)
    NB = 3
    mt = [ar.alloc(f"{name}_mt{j}", [Tn], F32) for j in range(NB)]
    xt = [ar.alloc(f"{name}_xt{j}", [Tn], F32) for j in range(NB)]
    xo = [ar.alloc(f"{name}_xo{j}", [Tn], F32) for j in range(NB)] if xp is None else None
    for kc in range(KC):
        m, x = mt[kc % NB], xt[kc % NB]
        S.dma("sp", m.ap, spill[kc * 128:(kc + 1) * 128, :], reads=[dbuf(cx, "spill", kc)], writes=[m.buf])
        S.dma("sp", x.ap, x_in[kc * 128:(kc + 1) * 128, t0:t0 + Tn], reads=[dbuf(cx, "x", x_in.tensor.name)], writes=[x.buf])
        S.op("dve", lambda e, m=m: e.tensor_tensor(out=m.ap, in0=m.ap, in1=rstd.ap, op=ALU.mult), reads=[m.buf, rstd.buf], writes=[m.buf])
        if xp is not None:
            dst_ap, dst_buf = xp.ap[:, kc, :], xp.buf
        else:
            dst_ap, dst_buf = xo[kc % NB].ap, xo[kc % NB].buf
        S.op("dve", lambda e, m=m, x=x, kc=kc, d=dst_ap: e.scalar_tensor_tensor(out=d, in0=m.ap, scalar=gcol.ap[:, kc:kc + 1], in1=x.ap, op0=ALU.mult, op1=ALU.add),
             reads=[m.buf, x.buf, gcol.buf], writes=[dst_buf])
        S.dma("act", x_out[kc * 128:(kc + 1) * 128, t0:t0 + Tn], dst_ap, reads=[dst_buf], writes=[dbuf(cx, "x", x_out.tensor.name)])
    if h_out is not None:
        rstd2 = ar.alloc(f"{name}_rstd2", [Tn], F32)
        for tg in range(2):
            rms_stats(cx, lambda kc, tg=tg: (xp.ap[:, kc, tg * 512:(tg + 1) * 512], xp.buf), rstd2, tg * 512, tg)
            rb = rstd2.ap[:, tg * 512:(tg + 1) * 512].unsqueeze(1).broadcast_to([128, 4, 512])
            for g in range(4):
                S.op("dve", lambda e, g=g, tg=tg, rb=rb: e.tensor_tensor(
                    out=h_out.ap[:, 4 * g:4 * g + 4, tg * 512:(tg + 1) * 512], in0=xp.ap[:, 4 * g:4 * g + 4, tg * 512:(tg + 1) * 512], in1=rb, op=ALU.mult),
                    reads=[xp.buf, rstd2.buf], writes=[h_out.buf])
    ar.release()


def ffn_hidden(cx, l, h2T, hidT):
    S, ar = cx.S, cx.arena
    ar.mark()
    gcol = ar.alloc("gffn", [KC], F32)
    S.dma("sp", gcol.ap, cx.d_pre_ffn[l], writes=[gcol.buf])
    sg = [ar.alloc(f"sg{j}", [512], F32) for j in range(4)]
    state = {}
    cnt = [0]

    def evac(oc2, tg, bk):
        ps = cx.psum[bk]
        if oc2 % 2 == 0:
            s = sg[cnt[0] % 4]
            cnt[0] += 1
            state[tg] = s
            S.op("act", lambda e: e.activation(out=s.ap, in_=ps.ap, func=AF.Silu), reads=[ps.buf], writes=[s.buf])
        else:
            s = state[tg]
            oc = oc2 // 2
            S.op("dve", lambda e: e.tensor_tensor(out=hidT.ap[:, oc, tg * 512:(tg + 1) * 512], in0=ps.ap, in1=s.ap, op=ALU.mult),
                 reads=[ps.buf, s.buf], writes=[hidT.buf])

    dense(cx, "gu", cx.d_w_gu[l], 2 * NFF, 1, KC, h2T, 1024, evac, scale=gcol, banks=(2, 3, 4, 5, 6, 7))
    ar.release()


def phase_C(cx, l, x_in, x1_dram, x_out, y_load):
    S, ar = cx.S, cx.arena
    for p in range(2):
        t0 = p * 1024
        ar.mark()
        h2T = ar.alloc("h2T", [KC, 1024], BF16)
        ar.mark()
        yT = ar.alloc("ycatT", [KC, 1024], BF16)
        for kc in range(KC):
            y_load(kc, t0, yT)
        proj_post_residual(cx, "op", cx.d_w_out[l], 1, KC, yT, cx.d_post_mix[l], x_in, x1_dram, t0, h_out=h2T)
        ar.release()
        hidT = ar.alloc("hidT", [NFF, 1024], BF16)
        ffn_hidden(cx, l, h2T, hidT)
        proj_post_residual(cx, "dn", cx.d_w_down[l], 2, 22, hidT, cx.d_post_ffn[l], x1_dram, x_out, t0)
        ar.release()


def ycat_half_cols(h):
    return list(range(512 * h, 512 * h + 512)) + list(range(1024 + 256 * h, 1024 + 256 * h + 256)) + list(range(1536 + 256 * h, 1536 + 256 * h + 256))


def w_out_layout(w, r):
    rows = ycat_half_cols(r) + ycat_half_cols(1 - r)
    return tile_w(np.ascontiguousarray(w[rows, :]), KC)


def w_gu_layout(wg, wu):
    tg, tu = tile_w(wg, KC), tile_w(wu, KC)
    out = np.empty((2 * NFF,) + tg.shape[1:], np.float32)
    out[0::2] = tg
    out[1::2] = tu
    return out


def ln_rstd(cx, dst, src_ps, n_feat, eps_t):
    S = cx.S
    S.op("act", lambda e: e.activation(out=dst.ap, in_=src_ps.ap, func=AF.Ln, scale=1.0 / n_feat, bias=eps_t.ap),
         reads=[src_ps.buf, eps_t.buf], writes=[dst.buf])
    S.op("act", lambda e: e.activation(out=dst.ap, in_=dst.ap, func=AF.Exp, scale=-0.5), reads=[dst.buf], writes=[dst.buf])


def load_bcast_vec(cx, name, dram_vec_ap, n):
    t = cx.arena.alloc(name, [n], F32)
    cx.S.dma("sp", t.ap, dram_vec_ap.partition_broadcast(128), writes=[t.buf])
    return t


def attn_alloc(cx):
    ar = cx.arena
    at = {}
    at["raw"] = {nm: ar.alloc(f"a_{nm}", [S_LEN], BF16) for nm in ("q", "k", "v")}
    at["KR"] = ar.alloc("KR", [S_LEN], BF16)
    at["sets"] = [{"QR": ar.alloc(f"QR{j}", [S_LEN], BF16), "KRm": [ar.alloc(f"KRm{j}_{m}", [S_LEN], BF16) for m in range(2)],
                   "Vt": ar.alloc(f"Vtok{j}", [32, 128], BF16)} for j in range(2)]
    at["t1"] = [ar.alloc(f"rt1_{j}", [512], F32) for j in range(2)]
    at["t2"] = [ar.alloc(f"rt2_{j}", [512], F32) for j in range(2)]
    at["pt"] = [ar.alloc(f"pt{j}", [512], BF16) for j in range(6)]
    at["sacc"] = [ar.alloc(f"sacc{j}", [512], F32) for j in range(2)]
    at["rs"] = [ar.alloc(f"rs{j}", [512], F32) for j in range(2)]
    at["tn"] = [ar.alloc(f"tn{j}", [512], F32) for j in range(2)]
    at["o_t"] = ar.alloc("o_t", [512], F32)
    at["sq_t"] = ar.alloc("sq_t", [512], F32)
    at["rstd_t"] = ar.alloc("rstd_t", [512], F32)
    at["yb"] = [ar.alloc(f"yb{j}", [512], BF16) for j in range(2)]
    return at


def attn_prep(cx, hd, cs, at, st):
    S = cx.S
    raw, KR, QR, KRm, Vt = at["raw"], at["KR"], st["QR"], st["KRm"], st["Vt"]
    for nm, r0 in (("q", R_Q), ("k", R_K), ("v", R_V)):
        t = raw[nm]
        for hf in range(2):
            src, sb = cx.seq_src(r0 + hd * 128, 128, hf * 2048, 2048)
            S.dma("sp", t.ap[:, hf * 2048:(hf + 1) * 2048], src, reads=[sb], writes=[t.buf])
    yield
    ps = cx.psum[3]
    i = 0
    for src, dst in ((raw["q"], QR), (raw["k"], KR)):
        for tg in range(8):
            sl = slice(tg * 512, (tg + 1) * 512)
            a, b2 = at["t1"][i % 2], at["t2"][i % 2]
            i += 1
            S.op("pe", lambda e, src=src, sl=sl: e.matmul(ps.ap, lhsT=cs["rm"].ap, rhs=src.ap[:, sl], start=True, stop=True),
                 reads=[cs["rm"].buf, src.buf], writes=[ps.buf])
            S.op("dve", lambda e, a=a, src=src, sl=sl: e.tensor_tensor(out=a.ap, in0=src.ap[:, sl], in1=cs["cos"].ap[:, sl], op=ALU.mult),
                 reads=[src.buf, cs["cos"].buf], writes=[a.buf])
            S.op("dve", lambda e, b2=b2, sl=sl: e.tensor_tensor(out=b2.ap, in0=ps.ap, in1=cs["sin"].ap[:, sl], op=ALU.mult),
                 reads=[ps.buf, cs["sin"].buf], writes=[b2.buf])
            S.op("pool", lambda e, a=a, b2=b2, dst=dst, sl=sl: e.tensor_tensor(out=dst.ap[:, sl], in0=a.ap, in1=b2.ap, op=ALU.add),
                 reads=[a.buf, b2.buf], writes=[dst.buf])
            if dst is KR:
                for m in range(2):
                    S.op("act", lambda e, m=m, sl=sl: e.activation(out=KRm[m].ap[:, sl], in_=KR.ap[:, sl], func=AF.Copy, scale=cs["hm"][m].ap),
                         reads=[KR.buf, cs["hm"][m].buf], writes=[KRm[m].buf])
            yield
    psb = ps.ap.bitcast(BF16)
    for g in range(4):
        for j in range(8):
            tt = g * 8 + j
            S.op("pe", lambda e, j=j, tt=tt: e.transpose(psb[:, j * 128:(j + 1) * 128], raw["v"].ap[:, tt * 128:(tt + 1) * 128], cs["ident"].ap),
                 reads=[raw["v"].buf, cs["ident"].buf], writes=[ps.buf])
        S.op("act", lambda e, g=g: e.copy(out=Vt.ap[:, g * 8:(g + 1) * 8, :], in_=psb.rearrange("p (a b) -> p a b", a=8)),
             reads=[ps.buf], writes=[Vt.buf])
        yield


def attn_main(cx, hd, cs, at, st, bg=None, bg_per_qg=3):
    S = cx.S
    QR, KRm, Vt = st["QR"], st["KRm"], st["Vt"]
    pt, rs, tn, o_t, sq_t, rstd_t, yb, sacc = at["pt"], at["rs"], at["tn"], at["o_t"], at["sq_t"], at["rstd_t"], at["yb"], at["sacc"]
    STB = (0, 1, 2, 6)
    LA = 3

    def blocks_of(qg):
        return [(kt, m) for kt in range(4 * qg + 4) for m in range(2)]

    def geom(qg, kt):
        j = kt - 4 * qg
        q0 = max(j, 0) * 128
        return j, q0, 512 - q0

    def ST(qg, bi):
        kt, m = blocks_of(qg)[bi]
        j, q0, N = geom(qg, kt)
        ps = cx.psum[STB[bi % 4]]
        S.op("pe", lambda e: e.matmul(ps.ap[:, 0:N], lhsT=KRm[m].ap[:, kt * 128:(kt + 1) * 128], rhs=QR.ap[:, qg * 512 + q0:(qg + 1) * 512], start=True, stop=True),
             reads=[KRm[m].buf, QR.buf], writes=[ps.buf])

    for b0 in range(LA):
        ST(0, b0)
    for qg in range(8):
        blocks = blocks_of(qg)
        last_kt = 4 * qg + 3
        for bi, (kt, m) in enumerate(blocks):
            if bi + LA < len(blocks):
                ST(qg, bi + LA)
            j, q0, N = geom(qg, kt)
            ps = cx.psum[STB[bi % 4]]
            p = pt[bi % 6]
            S.op("act", lambda e, ps=ps, p=p, N=N: e.activation(out=p.ap[:, 0:N], in_=ps.ap[:, 0:N], func=AF.Exp, scale=0.125), reads=[ps.buf], writes=[p.buf])
            if j >= 0:
                S.op("pool", lambda e, p=p: e.tensor_tensor(out=p.ap[:, 0:128], in0=p.ap[:, 0:128], in1=cs["tri"].ap, op=ALU.mult),
                     reads=[p.buf, cs["tri"].buf], writes=[p.buf])
            po = cx.psum[4 + m]
            S.op("pe", lambda e, po=po, p=p, kt=kt, q0=q0, N=N, last_kt=last_kt: e.matmul(po.ap[:, q0:512], lhsT=Vt.ap[:, kt, :], rhs=p.ap[:, 0:N], start=(kt == 0), stop=(kt == last_kt)),
                 reads=[Vt.buf, p.buf], writes=[po.buf])
            if m == 0:
                sa = sacc[0]
                if kt == 0:
                    S.op("dve", lambda e, sa=sa, p=p: e.tensor_copy(out=sa.ap, in_=p.ap), reads=[p.buf], writes=[sa.buf])
                else:
                    S.op("dve", lambda e, sa=sa, p=p, q0=q0, N=N: e.tensor_tensor(out=sa.ap[:, q0:512], in0=sa.ap[:, q0:512], in1=p.ap[:, 0:N], op=ALU.add),
                         reads=[p.buf, sa.buf], writes=[sa.buf])
            else:
                psm = cx.psum[7]
                S.op("pe", lambda e, psm=psm, p=p, kt=kt, q0=q0, N=N, last_kt=last_kt: e.matmul(psm.ap[:, q0:512], lhsT=cs["ones_bf"].ap, rhs=p.ap[:, 0:N], start=(kt == 0), stop=(kt == last_kt)),
                     reads=[cs["ones_bf"].buf, p.buf], writes=[psm.buf])
        S.op("pe", lambda e: e.matmul(cx.psum[6].ap, lhsT=cx.ones.ap, rhs=sacc[0].ap, start=True, stop=True),
             reads=[cx.ones.buf, sacc[0].buf], writes=[cx.psum[6].buf])
        if qg + 1 < 8:
            for b0 in range(LA):
                ST(qg + 1, b0)
        for m in range(2):
            S.op("act", lambda e, m=m: e.activation(out=rs[m].ap, in_=cx.psum[6 + m].ap, func=AF.Ln), reads=[cx.psum[6 + m].buf], writes=[rs[m].buf])
            S.op("act", lambda e, m=m: e.activation(out=rs[m].ap, in_=rs[m].ap, func=AF.Exp, scale=-1.0), reads=[rs[m].buf], writes=[rs[m].buf])
            S.op("dve", lambda e, m=m: e.tensor_tensor(out=tn[m].ap, in0=cx.psum[4 + m].ap, in1=rs[m].ap, op=ALU.mult),
                 reads=[cx.psum[4 + m].buf, rs[m].buf], writes=[tn[m].buf])
        S.op("dve", lambda e: e.scalar_tensor_tensor(out=o_t.ap, in0=tn[1].ap, scalar=cs["neg_lam"].ap, in1=tn[0].ap, op0=ALU.mult, op1=ALU.add),
             reads=[tn[0].buf, tn[1].buf, cs["neg_lam"].buf], writes=[o_t.buf])
        S.op("pool", lambda e: e.tensor_tensor(out=sq_t.ap, in0=o_t.ap, in1=o_t.ap, op=ALU.mult), reads=[o_t.buf], writes=[sq_t.buf])
        pn = cx.psum[3]
        S.op("pe", lambda e: e.matmul(pn.ap, lhsT=cx.ones.ap, rhs=sq_t.ap, start=True, stop=True), reads=[cx.ones.buf, sq_t.buf], writes=[pn.buf])
        ln_rstd(cx, rstd_t, pn, 128, cx.epsc)
        y = yb[qg % 2]
        S.op("dve", lambda e, y=y: e.scalar_tensor_tensor(out=y.ap, in0=o_t.ap, scalar=cs["subw"].ap, in1=rstd_t.ap, op0=ALU.mult, op1=ALU.mult),
             reads=[o_t.buf, cs["subw"].buf, rstd_t.buf], writes=[y.buf])
        dst, db = cx.y_dst(Y_ATT + hd * 128, qg * 512, 512)
        S.dma("sp", dst, y.ap, reads=[y.buf], writes=[db])
        if bg is not None:
            for _ in range(bg_per_qg):
                next(bg, None)
    if bg is not None:
        for _ in bg:
            pass


def attention_all(cx, l, cs, after_attn=None):
    S, ar = cx.S, cx.arena
    ar.mark()
    cos = ar.alloc("cos", [S_LEN], F32)
    sin = ar.alloc("sin", [S_LEN], F32)
    for t, d in ((cos, cx.d_cos), (sin, cx.d_sin)):
        for hf in range(2):
            S.dma("sp", t.ap[:, hf * 2048:(hf + 1) * 2048], d[:, hf * 2048:(hf + 1) * 2048], writes=[t.buf])
    cs["cos"], cs["sin"] = cos, sin
    at = attn_alloc(cx)
    for _ in attn_prep(cx, 0, cs, at, at["sets"][0]):
        pass
    for hd in range(4):
        bg = attn_prep(cx, hd + 1, cs, at, at["sets"][(hd + 1) % 2]) if hd < 3 else None
        attn_main(cx, hd, cs, at, at["sets"][hd % 2], bg)
    ar.release()


S_LEN = S

PM_CW, PM_CB, PM_BA, PM_BX, PM_LAM, PM_SUBLN, PM_GNORM, PM_BG = 0, 8, 10, 12, 14, 16, 17, 18
PM_BD = 32
PM_WG = 544
PM_LV = 672
NPM = 928
CB_RM, CB_ID, CB_TRI, CB_ONES, CB_GM = 0, 128, 256, 384, 512


def mixer_consts(cx, l):
    S, ar = cx.S, cx.arena
    cs = {}
    pm = ar.alloc("pm", [NPM], F32)
    S.dma("sp", pm.ap, cx.d_pm[l], writes=[pm.buf])
    cbf = ar.alloc("cbf", [640], BF16)
    S.dma("sp", cbf.ap, cx.d_cbf, writes=[cbf.buf])
    cs["pm"] = pm
    for nm, off in (("rm", CB_RM), ("ident", CB_ID), ("tri", CB_TRI), ("ones_bf", CB_ONES), ("gmask", CB_GM)):
        cs[nm] = Tile(cbf.ap[:, off:off + 128], cbf.buf)
    sm = ar.alloc("smallc", [16], F32)
    cs["sm"] = sm
    onec = ar.alloc("onec", [1], F32)
    S.op("pool", lambda e: e.memset(onec.ap, 1.0), writes=[onec.buf])
    cs["one"] = onec
    lam_init = 0.8 - 0.6 * math.exp(-0.3 * l)
    tmp = ar.alloc("lamtmp", [128], F32)
    for k in range(2):
        a = pm.ap[:, PM_LV + 128 * k:PM_LV + 128 * k + 64]
        b2 = pm.ap[:, PM_LV + 128 * k + 64:PM_LV + 128 * k + 128]
        S.op("dve", lambda e, a=a, b2=b2, k=k: e.tensor_tensor(out=tmp.ap[:, 64 * k:64 * k + 64], in0=a, in1=b2, op=ALU.mult), reads=[pm.buf], writes=[tmp.buf])
        S.op("dve", lambda e, k=k: e.reduce_sum(out=sm.ap[:, k:k + 1], in_=tmp.ap[:, 64 * k:64 * k + 64], axis=mybir.AxisListType.X), reads=[tmp.buf], writes=[sm.buf])
    S.op("act", lambda e: e.activation(out=sm.ap[:, 0:2], in_=sm.ap[:, 0:2], func=AF.Exp), reads=[sm.buf], writes=[sm.buf])
    S.op("dve", lambda e: e.tensor_tensor(out=sm.ap[:, 2:3], in0=sm.ap[:, 1:2], in1=sm.ap[:, 0:1], op=ALU.subtract), reads=[sm.buf], writes=[sm.buf])
    S.op("dve", lambda e: e.tensor_scalar(out=sm.ap[:, 2:3], in0=sm.ap[:, 2:3], scalar1=-lam_init, scalar2=None, op0=ALU.add), reads=[sm.buf], writes=[sm.buf])
    cs["neg_lam"] = Tile(sm.ap[:, 2:3], sm.buf)
    S.op("dve", lambda e: e.tensor_scalar(out=sm.ap[:, 3:4], in0=pm.ap[:, PM_SUBLN:PM_SUBLN + 1], scalar1=1.0 - lam_init, scalar2=None, op0=ALU.mult), reads=[pm.buf], writes=[sm.buf])
    cs["subw"] = Tile(sm.ap[:, 3:4], sm.buf)
    S.op("act", lambda e: e.activation(out=sm.ap[:, 4:6], in_=pm.ap[:, PM_LAM:PM_LAM + 2], func=AF.Exp, scale=-1.0), reads=[pm.buf], writes=[sm.buf])
    S.op("act", lambda e: e.activation(out=sm.ap[:, 4:6], in_=sm.ap[:, 4:6], func=AF.Ln, bias=onec.ap), reads=[sm.buf, onec.buf], writes=[sm.buf])
    S.op("dve", lambda e: e.tensor_scalar(out=sm.ap[:, 6:8], in0=sm.ap[:, 4:6], scalar1=-16.0, scalar2=None, op0=ALU.mult), reads=[sm.buf], writes=[sm.buf])
    S.op("dve", lambda e: e.tensor_scalar(out=sm.ap[:, 4:6], in0=sm.ap[:, 4:6], scalar1=-8.0, scalar2=None, op0=ALU.mult), reads=[sm.buf], writes=[sm.buf])
    S.op("dve", lambda e: e.tensor_scalar(out=sm.ap[:, 8:9], in0=pm.ap[:, PM_BG:PM_BG + 1], scalar1=-1.0, scalar2=None, op0=ALU.mult), reads=[pm.buf], writes=[sm.buf])
    S.op("pool", lambda e: e.memset(sm.ap[0:64, 9:10], 1.0), writes=[sm.buf])
    S.op("pool", lambda e: e.memset(sm.ap[64:128, 9:10], 0.0), writes=[sm.buf])
    S.op("pool", lambda e: e.memset(sm.ap[0:64, 10:11], 0.0), writes=[sm.buf])
    S.op("pool", lambda e: e.memset(sm.ap[64:128, 10:11], 1.0), writes=[sm.buf])
    cs["hm"] = [Tile(sm.ap[:, 9:10], sm.buf), Tile(sm.ap[:, 10:11], sm.buf)]
    return cs


def lru_chunk(cx, l, c, cs):
    S, ar = cx.S, cx.arena
    pm, sm = cs["pm"], cs["sm"]
    BL = 1024
    xgb = [ar.alloc(f"l{c}_xg{j}", [BL], BF16) for j in range(2)]
    xrb = [ar.alloc(f"l{c}_xr{j}", [BL + 8], BF16) for j in range(2)]
    names = ("w1", "gate", "xc", "r", "i", "a", "m", "u", "hs")
    f = {nm: ar.alloc(f"l{c}_{nm}", [BL], F32) for nm in names}
    hprev = ar.alloc(f"l{c}_hprev", [1], F32)
    yb = [ar.alloc(f"l{c}_yb{j}", [BL], BF16) for j in range(2)]
    bda = pm.ap[:, PM_BD + c * 128:PM_BD + (c + 1) * 128]
    bdx = pm.ap[:, PM_BD + (2 + c) * 128:PM_BD + (3 + c) * 128]
    col = lambda off: pm.ap[:, off:off + 1]
    for blk in range(S_LEN // BL):
        t0 = blk * BL
        xg, xr = xgb[blk % 2], xrb[blk % 2]
        src, sb = cx.seq_src(R_LG + c * 128, 128, t0, BL)
        S.dma("sp", xg.ap, src, reads=[sb], writes=[xg.buf])
        if blk == 0:
            S.op("pool", lambda e, xr=xr: e.memset(xr.ap[:, 0:3], 0.0), writes=[xr.buf])
            src, sb = cx.seq_src(R_LX + c * 128, 128, 0, BL)
            S.dma("sp", xr.ap[:, 3:3 + BL], src, reads=[sb], writes=[xr.buf])
        else:
            src, sb = cx.seq_src(R_LX + c * 128, 128, t0 - 3, BL + 3)
            S.dma("sp", xr.ap[:, 0:3 + BL], src, reads=[sb], writes=[xr.buf])
        w1, gate, xc, r, i_, a, m, u, hs = (f[n] for n in names)
        yield
        S.op("act", lambda e, xg=xg: e.activation(out=w1.ap, in_=xg.ap, func=AF.Square), reads=[xg.buf], writes=[w1.buf])
        S.op("dve", lambda e: e.tensor_scalar(out=w1.ap, in0=w1.ap, scalar1=0.044715, scalar2=1.0, op0=ALU.mult, op1=ALU.add), reads=[w1.buf], writes=[w1.buf])
        S.op("dve", lambda e, xg=xg: e.tensor_tensor(out=w1.ap, in0=w1.ap, in1=xg.ap, op=ALU.mult), reads=[w1.buf, xg.buf], writes=[w1.buf])
        S.op("act", lambda e: e.activation(out=w1.ap, in_=w1.ap, func=AF.Sigmoid, scale=1.5957691216057308), reads=[w1.buf], writes=[w1.buf])
        S.op("pool", lambda e, xg=xg: e.tensor_tensor(out=gate.ap, in0=w1.ap, in1=xg.ap, op=ALU.mult), reads=[w1.buf, xg.buf], writes=[gate.buf])
        yield
        S.op("dve", lambda e, xr=xr: e.tensor_scalar(out=xc.ap, in0=xr.ap[:, 3:3 + BL], scalar1=col(PM_CW + 4 * c + 3), scalar2=col(PM_CB + c), op0=ALU.mult, op1=ALU.add),
             reads=[xr.buf, pm.buf], writes=[xc.buf])
        for j in (2, 1, 0):
            S.op("dve", lambda e, xr=xr, j=j: e.scalar_tensor_tensor(out=xc.ap, in0=xr.ap[:, j:j + BL], scalar=col(PM_CW + 4 * c + j), in1=xc.ap, op0=ALU.mult, op1=ALU.add),
                 reads=[xr.buf, pm.buf, xc.buf], writes=[xc.buf])
        yield
        for tg in range(BL // 512):
            sl = slice(tg * 512, (tg + 1) * 512)
            for (bd, bias_off, dst, bk) in ((bda, PM_BA + c, r, 4 * c + tg), (bdx, PM_BX + c, i_, 4 * c + 2 + tg)):
                ps = cx.psum[bk]
                S.op("pe", lambda e, ps=ps, bd=bd, sl=sl: e.matmul(ps.ap, lhsT=bd, rhs=xc.ap[:, sl], start=True, stop=True), reads=[pm.buf, xc.buf], writes=[ps.buf])
                S.op("act", lambda e, ps=ps, dst=dst, sl=sl, bo=bias_off: e.activation(out=dst.ap[:, sl], in_=ps.ap, func=AF.Sigmoid, bias=col(bo)),
                     reads=[ps.buf, pm.buf], writes=[dst.buf])
        yield
        S.op("act", lambda e: e.activation(out=a.ap, in_=r.ap, func=AF.Exp, scale=sm.ap[:, 4 + c:5 + c]), reads=[r.buf, sm.buf], writes=[a.buf])
        S.op("act", lambda e: e.activation(out=m.ap, in_=r.ap, func=AF.Exp, scale=sm.ap[:, 6 + c:7 + c]), reads=[r.buf, sm.buf], writes=[m.buf])
        S.op("act", lambda e: e.activation(out=m.ap, in_=m.ap, func=AF.Sqrt, scale=-1.0, bias=cs["one"].ap), reads=[m.buf, cs["one"].buf], writes=[m.buf])
        S.op("pool", lambda e: e.tensor_tensor(out=u.ap, in0=m.ap, in1=i_.ap, op=ALU.mult), reads=[m.buf, i_.buf], writes=[u.buf])
        S.op("pool", lambda e: e.tensor_tensor(out=u.ap, in0=u.ap, in1=xc.ap, op=ALU.mult), reads=[u.buf, xc.buf], writes=[u.buf])
        yield
        init = 0.0 if blk == 0 else hprev.ap
        S.op("dve", lambda e, init=init: e.tensor_tensor_scan(out=hs.ap, data0=a.ap, data1=u.ap, initial=init, op0=ALU.mult, op1=ALU.add),
             reads=[a.buf, u.buf, hprev.buf], writes=[hs.buf])
        S.op("pool", lambda e: e.tensor_copy(out=hprev.ap, in_=hs.ap[:, BL - 1:BL]), reads=[hs.buf], writes=[hprev.buf])
        y = yb[blk % 2]
        S.op("pool", lambda e, y=y: e.tensor_tensor(out=y.ap, in0=hs.ap, in1=gate.ap, op=ALU.mult), reads=[hs.buf, gate.buf], writes=[y.buf])
        dst, db = cx.y_dst(Y_LRU + c * 128, t0, BL)
        S.dma("sp", dst, y.ap, reads=[y.buf], writes=[db])
        yield


def gla_heads(cx, l, cs):
    S, ar = cx.S, cx.arena
    pm, sm, hm = cs["pm"], cs["sm"], cs["hm"]
    ar.mark()
    qe = ar.alloc("g_qe", [S_LEN], BF16)
    qeh = [ar.alloc(f"g_qe{h}", [S_LEN], BF16) for h in range(2)]
    keh = [ar.alloc(f"g_ke{h}", [S_LEN], BF16) for h in range(2)]
    dec = ar.alloc("g_dec", [64], F32)
    KDh = [ar.alloc(f"g_KDt{h}", [32, 128], BF16) for h in range(2)]
    Vt = [ar.alloc(f"g_Vt{h}", [32, 128], BF16) for h in range(2)]
    ar.mark()
    kd = ar.alloc("g_kd", [S_LEN], BF16)
    ar.mark()
    gq = ar.alloc("g_q", [S_LEN], BF16)
    gk = ar.alloc("g_k", [S_LEN], BF16)
    for t, r0 in ((gq, R_GQ), (gk, R_GK)):
        for hf in range(2):
            src, sb = cx.seq_src(r0, 128, hf * 2048, 2048)
            S.dma("sp", t.ap[:, hf * 2048:(hf + 1) * 2048], src, reads=[sb], writes=[t.buf])
    lrb = [ar.alloc(f"g_lr{j}", [512], BF16) for j in range(2)]
    lrf = [ar.alloc(f"g_lrf{j}", [512], F32) for j in range(2)]
    Bt = ar.alloc("g_Bt", [S_LEN], F32)
    Bc = ar.alloc("g_Bc", [S_LEN], F32)
    E = ar.alloc("g_E", [S_LEN], F32)
    cm = E
    S.op("pool", lambda e: e.memset(cm.ap, 1.0), writes=[cm.buf])
    S.op("pool", lambda e: e.memset(cm.ap.rearrange("p (n c) -> p n c", c=64)[:, :, 0:1], 0.0), writes=[cm.buf])
    wg = pm.ap[0:16, PM_WG:PM_WG + 128]
    for tg in range(8):
        sl = slice(tg * 512, (tg + 1) * 512)
        a, b2 = lrb[tg % 2], lrf[tg % 2]
        src, sb = cx.seq_src(R_GLR, 16, tg * 512, 512)
        S.dma("sp", a.ap[0:16, :], src, reads=[sb], writes=[a.buf])
        S.op("pool", lambda e, a=a, b2=b2: e.tensor_copy(out=b2.ap[0:16, :], in_=a.ap[0:16, :]), reads=[a.buf], writes=[b2.buf])
        ps = cx.psum[tg % 4]
        S.op("pe", lambda e, ps=ps, b2=b2: e.matmul(ps.ap, lhsT=wg, rhs=b2.ap[0:16, :], start=True, stop=True), reads=[pm.buf, b2.buf], writes=[ps.buf])
        S.op("act", lambda e, ps=ps, sl=sl: e.activation(out=Bt.ap[:, sl], in_=ps.ap, func=AF.Exp, scale=-1.0, bias=sm.ap[:, 8:9]), reads=[ps.buf, sm.buf], writes=[Bt.buf])
    S.op("act", lambda e: e.activation(out=Bt.ap, in_=Bt.ap, func=AF.Ln, bias=cs["one"].ap), reads=[Bt.buf, cs["one"].buf], writes=[Bt.buf])
    S.op("dve", lambda e: e.tensor_tensor_scan(out=Bc.ap, data0=cm.ap, data1=Bt.ap, initial=0.0, op0=ALU.mult, op1=ALU.add), reads=[cm.buf, Bt.buf], writes=[Bc.buf])
    S.op("act", lambda e: e.activation(out=E.ap, in_=Bc.ap, func=AF.Exp, scale=-1.0 / 16.0), reads=[Bc.buf], writes=[E.buf])
    S.op("dve", lambda e: e.scalar_tensor_tensor(out=qe.ap, in0=gq.ap, scalar=0.125, in1=E.ap, op0=ALU.mult, op1=ALU.mult), reads=[gq.buf, E.buf], writes=[qe.buf])
    for h in range(2):
        S.op("pool", lambda e, h=h: e.tensor_scalar(out=qeh[h].ap, in0=qe.ap, scalar1=hm[h].ap, scalar2=None, op0=ALU.mult), reads=[qe.buf, hm[h].buf], writes=[qeh[h].buf])
    S.op("act", lambda e: e.activation(out=E.ap, in_=Bc.ap, func=AF.Exp, scale=1.0 / 16.0), reads=[Bc.buf], writes=[E.buf])
    for h in range(2):
        S.op("dve", lambda e, h=h: e.scalar_tensor_tensor(out=keh[h].ap, in0=gk.ap, scalar=hm[h].ap, in1=E.ap, op0=ALU.mult, op1=ALU.mult),
             reads=[gk.buf, hm[h].buf, E.buf], writes=[keh[h].buf])
    Bc3 = Bc.ap.rearrange("p (n c) -> p n c", c=64)
    S.op("dve", lambda e: e.tensor_tensor(out=E.ap.rearrange("p (n c) -> p n c", c=64), in0=Bc3, in1=Bc3[:, :, 63:64].broadcast_to([128, 64, 64]), op=ALU.subtract),
         reads=[Bc.buf], writes=[E.buf])
    S.op("act", lambda e: e.activation(out=E.ap, in_=E.ap, func=AF.Exp, scale=1.0 / 16.0), reads=[E.buf], writes=[E.buf])
    S.op("dve", lambda e: e.tensor_tensor(out=kd.ap, in0=gk.ap, in1=E.ap, op=ALU.mult), reads=[gk.buf, E.buf], writes=[kd.buf])
    S.op("act", lambda e: e.activation(out=dec.ap, in_=Bc3[:, :, 63], func=AF.Exp, scale=-1.0 / 16.0), reads=[Bc.buf], writes=[dec.buf])
    ar.release()
    gv = ar.alloc("g_v", [S_LEN], BF16)
    k = 0
    for (srcT, dsts, row0) in ((kd, KDh, None), (gv, [Vt[0]], R_GV), (gv, [Vt[1]], R_GV + 128)):
        if row0 is not None:
            for hf in range(2):
                src, sb = cx.seq_src(row0, 128, hf * 2048, 2048)
                S.dma("sp", gv.ap[:, hf * 2048:(hf + 1) * 2048], src, reads=[sb], writes=[gv.buf])
        for g in range(4):
            ps = cx.psum[4 + k % 2]
            k += 1
            psb = ps.ap.bitcast(BF16)
            for j in range(8):
                tt = g * 8 + j
                S.op("pe", lambda e, psb=psb, j=j, tt=tt, srcT=srcT: e.transpose(psb[:, j * 128:(j + 1) * 128], srcT.ap[:, tt * 128:(tt + 1) * 128], cs["ident"].ap),
                     reads=[srcT.buf, cs["ident"].buf], writes=[ps.buf])
            pv = psb.rearrange("p (a b) -> p a b", a=8)
            if row0 is None:
                for h in range(2):
                    S.op("dve", lambda e, pv=pv, g=g, h=h: e.tensor_scalar(out=KDh[h].ap[:, g * 8:(g + 1) * 8, :], in0=pv, scalar1=hm[h].ap, scalar2=None, op0=ALU.mult),
                         reads=[ps.buf, hm[h].buf], writes=[KDh[h].buf])
            else:
                S.op("act", lambda e, pv=pv, g=g, d=dsts[0]: e.copy(out=d.ap[:, g * 8:(g + 1) * 8, :], in_=pv), reads=[ps.buf], writes=[dsts[0].buf])
    ar.release()
    if globals().get("GLA_STOP", 9) <= 1.5:
        ar.release()
        return
    KV = ar.alloc("g_KV", [64, 128], F32)
    Sst = ar.alloc("g_Sst", [65, 128], F32)
    Spb = ar.alloc("g_Spb", [64, 128], BF16)
    for g in range(16):
        for h2 in range(2):
            ps = cx.psum[(2 * g + h2) % 4]
            hs_ = slice(h2 * 64, (h2 + 1) * 64)
            for q in range(4):
                n = 4 * g + q
                tt, hf = n // 2, n % 2
                S.op("pe", lambda e, ps=ps, h2=h2, q=q, tt=tt, hf=hf: e.matmul(
                    ps.ap[:, q * 128:(q + 1) * 128], lhsT=KDh[hf].ap[:, tt, :], rhs=Vt[h2].ap[:, tt, :], start=True, stop=True),
                    reads=[KDh[hf].buf, Vt[h2].buf], writes=[ps.buf])
            if h2 == 0:
                S.op("act", lambda e, ps=ps, g=g, hs_=hs_: e.copy(out=KV.ap[hs_, 4 * g:4 * g + 4, :], in_=ps.ap[hs_, :].rearrange("p (a b) -> p a b", a=4)), reads=[ps.buf], writes=[KV.buf])
            else:
                S.op("dve", lambda e, ps=ps, g=g, hs_=hs_: e.tensor_copy(out=KV.ap[hs_, 4 * g:4 * g + 4, :], in_=ps.ap[hs_, :].rearrange("p (a b) -> p a b", a=4)), reads=[ps.buf], writes=[KV.buf])
    if globals().get("GLA_STOP", 9) <= 1.7:
        ar.release()
        return
    S.op("pool", lambda e: e.memset(Sst.ap[:, 0, :], 0.0), writes=[Sst.buf])
    for n in range(63):
        S.op("dve", lambda e, n=n: e.scalar_tensor_tensor(out=Sst.ap[:, n + 1, :], in0=Sst.ap[:, n, :], scalar=dec.ap[:, n:n + 1], in1=KV.ap[:, n, :], op0=ALU.mult, op1=ALU.add),
             reads=[Sst.buf, dec.buf, KV.buf], writes=[Sst.buf])
    S.op("pool", lambda e: e.tensor_copy(out=Spb.ap, in_=Sst.ap[:, 0:64, :]), reads=[Sst.buf], writes=[Spb.buf])
    if globals().get("GLA_STOP", 9) <= 2:
        ar.release()
        return
    at4 = [ar.alloc(f"g_at{j}", [512], BF16) for j in range(2)]
    gob = [ar.alloc(f"g_go{j}", [512], BF16) for j in range(2)]
    o_t = ar.alloc("g_o", [512], F32)
    sq_t = ar.alloc("g_sq", [512], F32)
    rstd_t = ar.alloc("g_rstd", [512], F32)
    sil = ar.alloc("g_sil", [512], F32)
    yb = [ar.alloc(f"g_yb{j}", [512], BF16) for j in range(2)]
    it = 0
    for h2 in range(2):
        for g4 in range(8):
            pa, po, pn = cx.psum[it % 2], cx.psum[2 + it % 2], cx.psum[4 + it % 2]
            at, go, y = at4[it % 2], gob[it % 2], yb[it % 2]
            it += 1
            src, sb = cx.seq_src(R_GO + h2 * 128, 128, g4 * 512, 512)
            S.dma("sp", go.ap, src, reads=[sb], writes=[go.buf])
            for j in range(4):
                tt = 4 * g4 + j
                ts = slice(tt * 128, (tt + 1) * 128)
                S.op("pe", lambda e, pa=pa, j=j, ts=ts, h2=h2: e.matmul(pa.ap[:, j * 128:(j + 1) * 128], lhsT=keh[h2].ap[:, ts], rhs=qe.ap[:, ts], start=True, stop=True),
                     reads=[keh[h2].buf, qe.buf], writes=[pa.buf])
            S.op("dve", lambda e, pa=pa, at=at: e.tensor_tensor(out=at.ap.rearrange("p (a b) -> p a b", a=4), in0=pa.ap.rearrange("p (a b) -> p a b", a=4),
                                                                 in1=cs["gmask"].ap.unsqueeze(1).broadcast_to([128, 4, 128]), op=ALU.mult),
                 reads=[pa.buf, cs["gmask"].buf], writes=[at.buf])
            for j in range(4):
                tt = 4 * g4 + j
                S.op("pe", lambda e, po=po, j=j, tt=tt, at=at, h2=h2: e.matmul(po.ap[:, j * 128:(j + 1) * 128], lhsT=Vt[h2].ap[:, tt, :], rhs=at.ap[:, j * 128:(j + 1) * 128], start=True, stop=False),
                     reads=[Vt[h2].buf, at.buf], writes=[po.buf])
                for hf in range(2):
                    n = 2 * tt + hf
                    S.op("pe", lambda e, po=po, j=j, hf=hf, n=n, h2=h2: e.matmul(po.ap[:, j * 128 + hf * 64:j * 128 + hf * 64 + 64], lhsT=Spb.ap[:, n, :], rhs=qeh[h2].ap[:, n * 64:(n + 1) * 64],
                                                                            start=False, stop=(hf == 1)),
                         reads=[Spb.buf, qeh[h2].buf], writes=[po.buf])
            S.op("act", lambda e, po=po: e.copy(out=o_t.ap, in_=po.ap), reads=[po.buf], writes=[o_t.buf])
            S.op("pool", lambda e: e.tensor_tensor(out=sq_t.ap, in0=o_t.ap, in1=o_t.ap, op=ALU.mult), reads=[o_t.buf], writes=[sq_t.buf])
            S.op("pe", lambda e, pn=pn: e.matmul(pn.ap, lhsT=cx.ones.ap, rhs=sq_t.ap, start=True, stop=True), reads=[cx.ones.buf, sq_t.buf], writes=[pn.buf])
            ln_rstd(cx, rstd_t, pn, 128, cx.epsc)
            S.op("act", lambda e, go=go: e.activation(out=sil.ap, in_=go.ap, func=AF.Silu), reads=[go.buf], writes=[sil.buf])
            S.op("dve", lambda e: e.scalar_tensor_tensor(out=o_t.ap, in0=o_t.ap, scalar=pm.ap[:, PM_GNORM:PM_GNORM + 1], in1=rstd_t.ap, op0=ALU.mult, op1=ALU.mult),
                 reads=[o_t.buf, pm.buf, rstd_t.buf], writes=[o_t.buf])
            S.op("dve", lambda e, y=y: e.tensor_tensor(out=y.ap, in0=o_t.ap, in1=sil.ap, op=ALU.mult), reads=[o_t.buf, sil.buf], writes=[y.buf])
            dst, db = cx.y_dst(Y_GLA + h2 * 128, g4 * 512, 512)
            S.dma("sp", dst, y.ap, reads=[y.buf], writes=[db])
    ar.release()


def phase_B(cx, l, after_attn=None):
    S, ar = cx.S, cx.arena
    ar.mark()
    cs = mixer_consts(cx, l)
    attention_all(cx, l, cs)
    if after_attn is not None:
        after_attn()
    ar.mark()
    alive = [lru_chunk(cx, l, c, cs) for c in range(2)]
    while alive:
        for g in list(alive):
            try:
                next(g)
            except StopIteration:
                alive.remove(g)
    ar.release()
    gla_heads(cx, l, cs)
    ar.release()


def host_consts():
    pos = np.arange(S, dtype=np.float32)
    j = np.arange(128) % 64
    inv = (10000.0 ** (-(2.0 * (j % 32)).astype(np.float32) / 64.0)).astype(np.float32)
    ang = inv[:, None] * pos[None, :]
    cos = np.cos(ang).astype(np.float32)
    sin = np.sin(ang).astype(np.float32)
    cb = np.zeros((128, 640), np.float32)
    for m in range(128):
        jj = m % 64
        base = m - jj
        if jj < 32:
            cb[base + jj + 32, CB_RM + m] = -1.0
        else:
            cb[base + jj - 32, CB_RM + m] = 1.0
    cb[:, CB_ID:CB_ID + 128] = np.eye(128)
    kk = np.arange(128)
    cb[:, CB_TRI:CB_TRI + 128] = (kk[:, None] <= kk[None, :])
    cb[:, CB_ONES:CB_ONES + 128] = 1.0
    cb[:, CB_GM:CB_GM + 128] = (kk[:, None] <= kk[None, :]) & ((kk[:, None] // 64) == (kk[None, :] // 64))
    return cos, sin, cb.astype(ml_dtypes.bfloat16)


def pm_layout(P, l, r):
    pm = np.zeros((128, NPM), np.float32)
    for c in range(2):
        ch = slice(256 * r + 128 * c, 256 * r + 128 * (c + 1))
        pm[:, PM_CW + 4 * c:PM_CW + 4 * c + 4] = P['conv_w'][l][:, ch].T
        pm[:, PM_CB + c] = P['conv_b'][l][ch]
        pm[:, PM_BA + c] = P['b_rgate'][l][ch]
        pm[:, PM_BX + c] = P['b_igate'][l][ch]
        pm[:, PM_LAM + c] = P['lru_lambda'][l][ch]
        for k2, wn in ((0, 'w_rgate'), (2, 'w_igate')):
            for bb in range(2):
                blk = 4 * r + 2 * c + bb
                pm[64 * bb:64 * bb + 64, PM_BD + (k2 + c) * 128 + 64 * bb:PM_BD + (k2 + c) * 128 + 64 * bb + 64] = P[wn][l][blk]
    pm[:, PM_SUBLN] = P['diff_subln'][l]
    pm[:, PM_GNORM] = P['gla_norm'][l]
    pm[:, PM_BG] = P['b_gla_gate'][l][128 * r:128 * r + 128]
    pm[0:16, PM_WG:PM_WG + 128] = P['w_gla_gate_up'][l][:, 128 * r:128 * r + 128]
    for k2, nm in enumerate(('lambda_q1', 'lambda_k1', 'lambda_q2', 'lambda_k2')):
        pm[:, PM_LV + 64 * k2:PM_LV + 64 * k2 + 64] = P[nm][l][None, :]
    return pm


SEND_CH = [(0, 512), (512, 512), (1024, 512), (1536, 512), (2048, 512), (2560, 384)]


def build_program():
    from contextlib import ExitStack
    nc = bass.Bass("TRN2", target_bir_lowering=False)
    with ExitStack() as st:
        cx = make_ctx(nc, st)
        S_ = cx.S
        ext = lambda name, shape, dt: (nc.dram_tensor(name, shape, dt).ap() if globals().get("NOEXT") else nc.dram_tensor(name, shape, dt, kind="ExternalInput").ap())
        xT = ext("xT", [D, T], F32)
        def wl(name, shp):
            if globals().get("NOEXT"):
                return [nc.dram_tensor(f"{name}{l}", shp, F32).ap() for l in range(L)]
            t = ext(name, [L] + shp, F32)
            return [t[l] for l in range(L)]
        cx.d_w_in = wl("w_in", [2 * NHC, 1, 128, KC * 128])
        cx.d_w_out = wl("w_out", [KC, 1, 128, KC * 128])
        cx.d_w_gu = wl("w_gu", [2 * NFF, 1, 128, KC * 128])
        cx.d_w_down = wl("w_down", [KC, 2, 128, 22 * 128])
        gains = ext("gains", [L, 4, 128, KC], F32)
        pm = ext("pm", [L, 128, NPM], F32)
        cx.d_cbf = ext("cbf", [128, 640], BF16)
        cx.d_cos = ext("cos", [128, S], F32)
        cx.d_sin = ext("sin", [128, S], F32)
        out = nc.dram_tensor("out", [D, T], F32, kind="ExternalOutput").ap()
        cx.d_pre_mix = [gains[l, 0] for l in range(L)]
        cx.d_post_mix = [gains[l, 1] for l in range(L)]
        cx.d_pre_ffn = [gains[l, 2] for l in range(L)]
        cx.d_post_ffn = [gains[l, 3] for l in range(L)]
        cx.d_pm = [pm[l] for l in range(L)]
        cx.d_spill = nc.dram_tensor("spill", [D, 1024], F32).ap()
        x1d = nc.dram_tensor("x1d", [D, T], F32).ap()
        xa = nc.dram_tensor("xa", [D, T], F32).ap()
        xb = nc.dram_tensor("xb", [D, T], F32).ap()
        SEQR = 3072
        seq = nc.dram_tensor("seq", [SEQR, S], BF16).ap()
        mine1 = nc.dram_tensor("mine1", [HC, T], BF16).ap()
        send1 = nc.dram_tensor("send1", [SEQR, T], BF16).ap()
        recvall = nc.dram_tensor("recvall", [6, 1024, T], BF16).ap()
        yfull = nc.dram_tensor("yfull", [YH, S], BF16).ap()
        ymine = nc.dram_tensor("ymine", [YH, T], BF16).ap()
        ysend = nc.dram_tensor("ysend", [YH, T], BF16).ap()
        recv2all = nc.dram_tensor("recv2all", [2, 1024, T], BF16).ap()
        yoth = nc.dram_tensor("yoth", [YH, T], BF16).ap()
        seqh = seq.rearrange("r (h t) -> h r t", h=2)
        seqhj = seq.rearrange("(j r) (h t) -> h j r t", r=512, h=2)
        rva = recvall.rearrange("j (s r) t -> s j r t", s=2)
        yfh = yfull.rearrange("r (h t) -> h r t", h=2)
        rv2 = recv2all.rearrange("j (s r) t -> s j r t", s=2)
        pars = {}

        def par(e):
            k = id(e)
            if k not in pars:
                pars[k] = e.partition_id() % 2
            return pars[k]

        cx.seq_src = lambda row0, nrows, tok0, n: (seq[row0:row0 + nrows, tok0:tok0 + n], dbuf(cx, "seq", row0 // 128))
        cx.y_dst = lambda row0, tok0, n: (yfull[row0:row0 + 128, tok0:tok0 + n], dbuf(cx, "yfull", row0 // 128))
        mine_dst = lambda oc, tg: (mine1[oc * 128:(oc + 1) * 128, tg * 512:(tg + 1) * 512], dbuf(cx, "mine1", oc))
        send_dst = lambda oc, tg: (send1[oc * 128:(oc + 1) * 128, tg * 512:(tg + 1) * 512], dbuf(cx, "send1", (oc * 128) // 512))

        def y_load(kc, t0, yt):
            if kc < 8:
                S_.dma("sp", yt.ap[:, kc, :], ymine[kc * 128:(kc + 1) * 128, t0:t0 + 1024], reads=[dbuf(cx, "ymine")], writes=[yt.buf])
            else:
                S_.dma("sp", yt.ap[:, kc, :], yoth[(kc - 8) * 128:(kc - 7) * 128, t0:t0 + 1024], reads=[dbuf(cx, "yoth")], writes=[yt.buf])

        def copy_mine(r0, r1):
            S_.dma("pool", lambda e: seqh[bass.ds(par(e), 1), r0:r1, :].rearrange("h r t -> (h r) t"), mine1[r0:r1, :],
                   reads=[dbuf(cx, "mine1", oc) for oc in range(r0 // 128, r1 // 128)], writes=[dbuf(cx, "seq", oc) for oc in range(r0 // 128, r1 // 128)])

        def gather1(j):
            S_.op("pool", lambda e: e.collective_compute("AllGather", ALU.bypass, replica_groups=RG,
                                                          ins=[send1[512 * j:512 * (j + 1), :].opt()], outs=[recvall[j].opt()]),
                  reads=[dbuf(cx, "send1", j)], writes=[dbuf(cx, "recv1", j)], kind="x")

        def copy_recv(j0, j1):
            S_.dma("pool", lambda e: seqhj[bass.ds(1 - par(e), 1), j0:j1, :, :].rearrange("h j r t -> (h j) r t"),
                   lambda e: rva[bass.ds(1 - par(e), 1), j0:j1, :, :].rearrange("s j r t -> (s j) r t"),
                   reads=[dbuf(cx, "recv1", j) for j in range(j0, j1)], writes=[dbuf(cx, "seq", oc) for oc in range(4 * j0, 4 * j1)])

        def post_store_A(oc):
            if oc >= NHC:
                ocs = oc - NHC
                if ocs % 4 == 3 or ocs == NHC - 1:
                    j = ocs // 4
                    gather1(j)
                    if j == 2:
                        copy_recv(0, 3)
                    elif j == 5:
                        copy_recv(3, 6)
            elif oc == 11:
                copy_mine(0, 1536)
            elif oc == NHC - 1:
                copy_mine(1536, HC)

        def exchange_y(j):
            ybufs = [dbuf(cx, "yfull", k) for k in range(4 * j, 4 * j + 4)]
            rs_ = slice(512 * j, 512 * (j + 1))
            S_.dma("pool", ymine[rs_, :], lambda e: yfh[bass.ds(par(e), 1), rs_, :].rearrange("h r t -> (h r) t"), reads=ybufs, writes=[dbuf(cx, "ymine", j)])
            S_.dma("pool", ysend[rs_, :], lambda e: yfh[bass.ds(1 - par(e), 1), rs_, :].rearrange("h r t -> (h r) t"), reads=ybufs, writes=[dbuf(cx, "ysend", j)])
            S_.op("pool", lambda e: e.collective_compute("AllGather", ALU.bypass, replica_groups=RG,
                                                          ins=[ysend[rs_, :].opt()], outs=[recv2all[j].opt()]),
                  reads=[dbuf(cx, "ysend", j)], writes=[dbuf(cx, "recv2", j)], kind="x")
            S_.dma("pool", yoth[rs_, :], lambda e: rv2[bass.ds(1 - par(e), 1), j, :, :].rearrange("s r t -> (s r) t"),
                   reads=[dbuf(cx, "recv2", j)], writes=[dbuf(cx, "yoth", j)])

        def y_load(kc, t0, yt):
            j = (kc % 8) // 4
            if kc < 8:
                S_.dma("sp", yt.ap[:, kc, :], ymine[kc * 128:(kc + 1) * 128, t0:t0 + 1024], reads=[dbuf(cx, "ymine", j)], writes=[yt.buf])
            else:
                S_.dma("sp", yt.ap[:, kc, :], yoth[(kc - 8) * 128:(kc - 7) * 128, t0:t0 + 1024], reads=[dbuf(cx, "yoth", j)], writes=[yt.buf])

        order_A = list(range(NHC, 2 * NHC)) + list(range(NHC))
        x_cur = xT
        for l in range(L):
            phase_A(cx, l, x_cur, mine_dst, send_dst, oc_order=order_A, post_store=post_store_A)
            phase_B(cx, l, after_attn=lambda: exchange_y(0))
            exchange_y(1)
            x_next = out if l == L - 1 else (xa if l % 2 == 0 else xb)
            phase_C(cx, l, x_cur, x1d, x_next, y_load)
            x_cur = x_next
        cx.S.emit()
        n_ops = cx.S.n_inst
    return nc, n_ops


def make_in_maps(inp):
    P = {k: np.asarray(v, dtype=np.float32) for k, v in inp.items()}
    cos, sin, cbf = host_consts()
    w_in = [np.stack([w_in_layout(P['w_in'][l], r) for l in range(L)]) for r in range(2)]
    w_out = [np.stack([w_out_layout(P['w_out'][l], r) for l in range(L)]) for r in range(2)]
    w_gu = np.stack([w_gu_layout(P['w_ffn_gate'][l], P['w_ffn_up'][l]) for l in range(L)])
    w_dn = np.stack([tile_w(P['w_ffn_down'][l], 22) for l in range(L)])
    gains = np.stack([np.stack([gcol_layout(P[nm][l]) for nm in ('pre_mix_norm', 'post_mix_norm', 'pre_ffn_norm', 'post_ffn_norm')]) for l in range(L)])
    pm = [np.stack([pm_layout(P, l, r) for l in range(L)]) for r in range(2)]
    maps = []
    for c in range(NCORES):
        b, r = c // 2, c % 2
        maps.append({
            "xT": np.ascontiguousarray(P['x'][b, r * T:(r + 1) * T, :].T),
            "w_in": w_in[r], "w_out": w_out[r], "w_gu": w_gu, "w_down": w_dn,
            "gains": gains, "pm": pm[r], "cbf": cbf, "cos": cos, "sin": sin,
        })
    return maps


def kernel_fused(**inputs):
    nc, _ = build_program()
    maps = make_in_maps(inputs)
    res = run_bass_kernel_spmd(nc, maps, core_ids=list(range(NCORES)))
    outp = np.empty((B, S, D), np.float32)
    for c in range(NCORES):
        b, r = c // 2, c % 2
        outp[b, r * T:(r + 1) * T, :] = np.asarray(res.results[c]["out"], dtype=np.float32).T
    return outp


def _prog(builder):
    from contextlib import ExitStack
    nc = bass.Bass("TRN2", target_bir_lowering=False)
    with ExitStack() as st:
        cx = make_ctx(nc, st)
        builder(nc, cx)
        cx.S.emit()
    return nc


def build_A():
    def b(nc, cx):
        xT = nc.dram_tensor("xT", [D, T], F32, kind="ExternalInput").ap()
        w_in = nc.dram_tensor("w_in", [2 * NHC, 1, 128, KC * 128], F32, kind="ExternalInput").ap()
        gpre = nc.dram_tensor("gpre", [128, KC], F32, kind="ExternalInput").ap()
        mine = nc.dram_tensor("mine", [HC, T], BF16, kind="ExternalOutput").ap()
        send = nc.dram_tensor("send", [HC, T], BF16, kind="ExternalOutput").ap()
        cx.d_w_in = [w_in]
        cx.d_pre_mix = [gpre]
        md = lambda oc, tg: (mine[oc * 128:(oc + 1) * 128, tg * 512:(tg + 1) * 512], dbuf(cx, "mine", oc))
        sd = lambda oc, tg: (send[oc * 128:(oc + 1) * 128, tg * 512:(tg + 1) * 512], dbuf(cx, "send", oc))
        phase_A(cx, 0, xT, md, sd)
    return _prog(b)


def build_B(l):
    def b(nc, cx):
        seq = nc.dram_tensor("seq", [HC, S], BF16, kind="ExternalInput").ap()
        pm = nc.dram_tensor("pm", [128, NPM], F32, kind="ExternalInput").ap()
        cx.d_pm = {l: pm}
        cx.d_cbf = nc.dram_tensor("cbf", [128, 640], BF16, kind="ExternalInput").ap()
        cx.d_cos = nc.dram_tensor("cos", [128, S], F32, kind="ExternalInput").ap()
        cx.d_sin = nc.dram_tensor("sin", [128, S], F32, kind="ExternalInput").ap()
        yT = nc.dram_tensor("yT", [YH, S], BF16, kind="ExternalOutput").ap()
        cx.seq_src = lambda row0, nrows, tok0, n: (seq[row0:row0 + nrows, tok0:tok0 + n], dbuf(cx, "seq", 0))
        cx.y_dst = lambda row0, tok0, n: (yT[row0:row0 + 128, tok0:tok0 + n], dbuf(cx, "y", row0))
        phase_B(cx, l)
    return _prog(b)


def build_C():
    def b(nc, cx):
        xT = nc.dram_tensor("xT", [D, T], F32, kind="ExternalInput").ap()
        yT = nc.dram_tensor("yT", [D, T], BF16, kind="ExternalInput").ap()
        cx.d_w_out = [nc.dram_tensor("w_out", [KC, 1, 128, KC * 128], F32, kind="ExternalInput").ap()]
        cx.d_w_gu = [nc.dram_tensor("w_gu", [2 * NFF, 1, 128, KC * 128], F32, kind="ExternalInput").ap()]
        cx.d_w_down = [nc.dram_tensor("w_down", [KC, 2, 128, 22 * 128], F32, kind="ExternalInput").ap()]
        g = nc.dram_tensor("gains", [3, 128, KC], F32, kind="ExternalInput").ap()
        cx.d_post_mix, cx.d_pre_ffn, cx.d_post_ffn = [g[0]], [g[1]], [g[2]]
        cx.d_spill = nc.dram_tensor("spill", [D, 1024], F32).ap()
        x1 = nc.dram_tensor("x1", [D, T], F32).ap()
        x2 = nc.dram_tensor("x2", [D, T], F32, kind="ExternalOutput").ap()

        def y_load(kc, t0, yt):
            cx.S.dma("sp", yt.ap[:, kc, :], yT[kc * 128:(kc + 1) * 128, t0:t0 + 1024], writes=[yt.buf])
        phase_C(cx, 0, xT, x1, x2, y_load)
    return _prog(b)


def kernel_multi(**inputs):
    P = {k: np.asarray(v, dtype=np.float32) for k, v in inputs.items()}
    cos, sin, cbf = host_consts()
    cores = list(range(NCORES))
    xs = [np.ascontiguousarray(P['x'][c // 2, (c % 2) * T:(c % 2 + 1) * T, :].T) for c in cores]
    for l in range(L):
        wl = [w_in_layout(P['w_in'][l], r) for r in range(2)]
        gp = gcol_layout(P['pre_mix_norm'][l])
        res = run_bass_kernel_spmd(build_A(), [{"xT": xs[c], "w_in": wl[c % 2], "gpre": gp} for c in cores], core_ids=cores).results
        seqs = []
        for c in cores:
            r = c % 2
            halves = [None, None]
            halves[r] = res[c]["mine"]
            halves[1 - r] = res[c ^ 1]["send"]
            seqs.append(np.ascontiguousarray(np.concatenate(halves, axis=1)))
        del res
        pml = [pm_layout(P, l, r) for r in range(2)]
        res = run_bass_kernel_spmd(build_B(l), [{"seq": seqs[c], "pm": pml[c % 2], "cbf": cbf, "cos": cos, "sin": sin} for c in cores], core_ids=cores).results
        ys = []
        for c in cores:
            r = c % 2
            ys.append(np.ascontiguousarray(np.concatenate([res[c]["yT"][:, r * T:(r + 1) * T], res[c ^ 1]["yT"][:, r * T:(r + 1) * T]], axis=0)))
        del res, seqs
        wo = [w_out_layout(P['w_out'][l], r) for r in range(2)]
        wgu = w_gu_layout(P['w_ffn_gate'][l], P['w_ffn_up'][l])
        wd = tile_w(P['w_ffn_down'][l], 22)
        g3 = np.stack([gcol_layout(P[nm][l]) for nm in ('post_mix_norm', 'pre_ffn_norm', 'post_ffn_norm')])
        res = run_bass_kernel_spmd(build_C(), [{"xT": xs[c], "yT": ys[c], "w_out": wo[c % 2], "w_gu": wgu, "w_down": wd, "gains": g3} for c in cores], core_ids=cores).results
        xs = [np.asarray(res[c]["x2"]) for c in cores]
        del res, ys
    outp = np.empty((B, S, D), np.float32)
    for c in cores:
        outp[c // 2, (c % 2) * T:(c % 2 + 1) * T, :] = xs[c].T
    return outp


def kernel(**inputs):
    return kernel_fused(**inputs)
```

```python
import math
import numpy as np
import ml_dtypes
import concourse.bass as bass
import concourse.mybir as mybir
from concourse.bass_utils import run_bass_kernel_spmd

F32 = mybir.dt.float32
BF16 = mybir.dt.bfloat16
AF = mybir.ActivationFunctionType
ALU = mybir.AluOpType

D = 2048
B = 4
S = 4096
L = 4
T = 2048
NCORES = 8
HC = 2944
NHC = HC // 128
DFF = 5632
NFF = DFF // 128
KC = D // 128
EPS = 1e-6
RG = [[0, 1], [2, 3], [4, 5], [6, 7]]

R_Q, R_K, R_V, R_LG, R_LX, R_GQ, R_GK, R_GV, R_GO, R_GLR = 0, 512, 1024, 1536, 1792, 2048, 2176, 2304, 2560, 2816
YH = 1024
Y_ATT, Y_LRU, Y_GLA = 0, 512, 768


class Buf:
    __slots__ = ("name", "last_w", "readers")

    def __init__(self, name):
        self.name = name
        self.last_w = None
        self.readers = []


class _Op:
    __slots__ = ("q", "fn", "deps", "kind", "sig", "sem", "val", "slot_prev")

    def __init__(self, q, fn, deps, kind):
        self.q = q
        self.fn = fn
        self.deps = deps
        self.kind = kind
        self.sig = False
        self.sem = None
        self.val = 0
        self.slot_prev = None


QUEUES = ("pe", "act", "dve", "pool", "sp")
EPOCH = 30000


class Sched:
    def __init__(self, nc):
        self.nc = nc
        self.ops = []
        self.nslots = {"sp": 16, "act": 6, "pool": 8, "pe": 2, "dve": 2}

    def op(self, q, fn, reads=(), writes=(), kind="c"):
        idx = len(self.ops)
        deps = set()
        for b in reads:
            if b.last_w is not None:
                deps.add(b.last_w)
        for b in writes:
            if b.last_w is not None:
                deps.add(b.last_w)
            deps.update(b.readers)
        for b in reads:
            b.readers.append(idx)
        for b in writes:
            b.last_w = idx
            b.readers = []
        deps.discard(idx)
        self.ops.append(_Op(q, fn, deps, kind))
        return idx

    def dma(self, q, out, in_, reads=(), writes=(), **kw):
        def fn(e):
            o = out(e) if callable(out) else out
            i = in_(e) if callable(in_) else in_
            return e.dma_start(out=o, in_=i, **kw)
        return self.op(q, fn, reads, writes, kind="d")

    def emit(self):
        nc = self.nc
        ops = self.ops
        for o in ops:
            best = {}
            keep = set()
            for d in o.deps:
                p = ops[d]
                if p.kind == "c":
                    if p.q == "pe" and o.q == "pe":
                        continue
                    if p.q not in best or best[p.q] < d:
                        best[p.q] = d
                else:
                    keep.add(d)
            o.deps = keep | set(best.values())
            for d in o.deps:
                ops[d].sig = True
        sems = {}

        def get_sem(name):
            if name not in sems:
                sems[name] = nc.alloc_semaphore(name)
            return sems[name]

        ccount = {q: 0 for q in QUEUES}
        dcount = {q: 0 for q in QUEUES}
        xcount = 0
        slot_tot = {}
        slot_last = {}
        for i, o in enumerate(ops):
            if o.kind == "c":
                if not o.sig:
                    continue
                ccount[o.q] += 1
                ep = ccount[o.q] // EPOCH
                o.sem = get_sem(f"c_{o.q}_{ep}")
                o.val = ccount[o.q] - ep * EPOCH + (1 if ep > 0 else 0)
                if ep > 0:
                    o.val = ccount[o.q] - ep * EPOCH + 1
            elif o.kind == "d":
                k = dcount[o.q] % self.nslots[o.q]
                dcount[o.q] += 1
                name = f"d_{o.q}_{k}"
                o.sem = get_sem(name)
                slot_tot[name] = slot_tot.get(name, 0) + 16
                o.val = slot_tot[name]
                o.slot_prev = slot_last.get(name)
                slot_last[name] = i
                o.sig = True
            else:
                k = xcount % 4
                xcount += 1
                name = f"x_{k}"
                o.sem = get_sem(name)
                slot_tot[name] = slot_tot.get(name, 0) + 1
                o.val = slot_tot[name]
                o.slot_prev = slot_last.get(name)
                slot_last[name] = i
                o.sig = True
        per_q = {q: [] for q in QUEUES}
        for i, o in enumerate(ops):
            per_q[o.q].append(i)
        final_waits = [(o.sem, o.val) for o in (ops[i] for i in slot_last.values())]
        self.n_inst = len(ops)

        def run_queue(q, e):
            known = {}
            for i in per_q[q]:
                o = ops[i]
                deps = set(o.deps)
                if o.slot_prev is not None:
                    deps.add(o.slot_prev)
                need = {}
                for d in deps:
                    p = ops[d]
                    key = id(p.sem)
                    if key not in need or need[key][1] < p.val:
                        need[key] = (p.sem, p.val)
                for key, (sem, val) in need.items():
                    if known.get(key, 0) >= val:
                        continue
                    e.wait_ge(sem, val)
                    known[key] = val
                ins = o.fn(e)
                if o.sig:
                    if o.kind == "d":
                        ins.then_inc(o.sem, 16)
                    elif o.kind == "x":
                        ins.then_inc(o.sem)
                    else:
                        ins.then_inc(o.sem, 1)
            if q == "sp":
                for sem, val in final_waits:
                    e.wait_ge(sem, val)

        with nc.Block() as block:
            @block.tensor
            def _(e):
                run_queue("pe", e)

            @block.scalar
            def _(e):
                run_queue("act", e)

            @block.vector
            def _(e):
                run_queue("dve", e)

            @block.gpsimd
            def _(e):
                run_queue("pool", e)

            @block.sync
            def _(e):
                run_queue("sp", e)


class Tile:
    __slots__ = ("ap", "buf")

    def __init__(self, ap, buf):
        self.ap = ap
        self.buf = buf


class Arena:
    def __init__(self, base_ap, nwords):
        self.base = base_ap
        self.nwords = nwords
        self.top = 0
        self.live = []
        self.retired = []
        self.marks = []

    def alloc(self, name, free_shape, dtype):
        n = 1
        for s in free_shape:
            n *= s
        words = (n + 1) // 2 if dtype == BF16 else n
        words = (words + 7) // 8 * 8
        start = self.top
        end = start + words
        assert end <= self.nwords, f"SBUF arena overflow allocating {name}: {end*4} > {self.nwords*4}"
        self.top = end
        buf = Buf(name)
        for (s0, e0, b0) in self.retired:
            if s0 < end and start < e0:
                if b0.last_w is not None:
                    buf.readers.append(b0.last_w)
                buf.readers.extend(b0.readers)
        self.live.append((start, end, buf))
        ap = self.base[:, start:end]
        if dtype == BF16:
            ap = ap.bitcast(BF16)[:, 0:n]
        else:
            ap = ap[:, 0:n]
        if len(free_shape) == 2:
            ap = ap.rearrange("p (a b) -> p a b", a=free_shape[0])
        elif len(free_shape) == 3:
            ap = ap.rearrange("p (a b c) -> p a b c", a=free_shape[0], b=free_shape[1])
        return Tile(ap, buf)

    def mark(self):
        self.marks.append((self.top, len(self.live)))

    def release(self):
        top, nl = self.marks.pop()
        self.retired.extend(self.live[nl:])
        del self.live[nl:]
        self.top = top


class Ctx:
    pass


def dense(cx, name, w_ap, n_oc, n_kg, KG, A, Tn, evac, scale=None, banks=(2, 3, 4, 5), wq="sp", oc_order=None, nbuf=2):
    S, ar = cx.S, cx.arena
    ar.mark()
    wst = [ar.alloc(f"{name}_wst{j}", [KG, 128], F32) for j in range(nbuf)]
    wb = [ar.alloc(f"{name}_wb{j}", [KG, 128], BF16) for j in range(nbuf)]
    steps = [(oc, kg) for oc in (oc_order if oc_order is not None else range(n_oc)) for kg in range(n_kg)]
    ntg = Tn // 512

    def load(i):
        oc, kg = steps[i]
        t = wst[i % nbuf]
        S.dma(wq, t.ap, w_ap[oc, kg].rearrange("p (k n) -> p k n", k=KG), writes=[t.buf])

    def cast(i):
        oc, kg = steps[i]
        src, dst = wst[i % nbuf], wb[i % nbuf]
        if scale is None:
            S.op("act", lambda e, d=dst, s=src: e.copy(out=d.ap, in_=s.ap), reads=[src.buf], writes=[dst.buf])
        else:
            sc = scale.ap[:, kg * KG:(kg + 1) * KG].unsqueeze(2).broadcast_to([128, KG, 128])
            S.op("dve", lambda e, d=dst, s=src, sc=sc: e.tensor_tensor(out=d.ap, in0=s.ap, in1=sc, op=ALU.mult),
                 reads=[src.buf, scale.buf], writes=[dst.buf])

    for i0 in range(min(nbuf, len(steps))):
        load(i0)
    cast(0)
    bi = 0
    cur_bank = {}
    for i, (oc, kg) in enumerate(steps):
        if i + 1 < len(steps):
            cast(i + 1)
        if i + nbuf < len(steps):
            load(i + nbuf)
        w = wb[i % nbuf]
        for tg in range(ntg):
            if kg == 0:
                cur_bank[tg] = banks[bi % len(banks)]
                bi += 1
            bk = cur_bank[tg]
            ps = cx.psum[bk]
            for kc in range(KG):
                kk = kg * KG + kc
                S.op("pe", lambda e, ps=ps, w=w, kc=kc, kk=kk, tg=tg, st=(kk == 0), sp=(kk == n_kg * KG - 1):
                     e.matmul(ps.ap, lhsT=w.ap[:, kc, :], rhs=A.ap[:, kk, tg * 512:(tg + 1) * 512], start=st, stop=sp),
                     reads=[w.buf, A.buf], writes=[ps.buf])
            if kg == n_kg - 1:
                evac(oc, tg, bk)
    ar.release()


def dbuf(cx, name, key=0):
    k = (name, key)
    if k not in cx.dbufs:
        cx.dbufs[k] = Buf(f"{name}_{key}")
    return cx.dbufs[k]


def load_x_pass(cx, x_dram, t0, xg):
    xv = x_dram.rearrange("(k p) t -> p k t", p=128)
    for g in range(4):
        cx.S.dma("sp", xg[g].ap, xv[:, 4 * g:4 * g + 4, t0:t0 + 1024], reads=[dbuf(cx, "x", x_dram.tensor.name)], writes=[xg[g].buf])


def rms_stats(cx, src_fn, rstd, c0, bank):
    S = cx.S
    ps = cx.psum[bank]
    for kc in range(KC):
        sqt = cx.sq[kc % 2]
        ap, buf = src_fn(kc)
        S.op("act", lambda e, o=sqt, a=ap: e.activation(out=o.ap, in_=a, func=AF.Square), reads=[buf], writes=[sqt.buf])
        S.op("pe", lambda e, o=sqt, kc=kc: e.matmul(ps.ap, lhsT=cx.ones.ap, rhs=o.ap, start=(kc == 0), stop=(kc == KC - 1)),
             reads=[sqt.buf, cx.ones.buf], writes=[ps.buf])
    r = rstd.ap[:, c0:c0 + 512]
    S.op("act", lambda e: e.activation(out=r, in_=ps.ap, func=AF.Sqrt, scale=1.0 / D, bias=cx.epsc.ap), reads=[ps.buf, cx.epsc.buf], writes=[rstd.buf])
    S.op("dve", lambda e: e.reciprocal(out=r, in_=r), reads=[rstd.buf], writes=[rstd.buf])


def norm_to_bf16(cx, x_dram, hT, rstd, Tn, tbase):
    S, ar = cx.S, cx.arena
    for ps_ in range(Tn // 1024):
        ar.mark()
        xg = [ar.alloc(f"xg{g}", [4, 1024], F32) for g in range(4)]
        load_x_pass(cx, x_dram, tbase + ps_ * 1024, xg)
        for tg in range(2):
            c0 = ps_ * 1024 + tg * 512
            rms_stats(cx, lambda kc, tg=tg: (xg[kc // 4].ap[:, kc % 4, tg * 512:(tg + 1) * 512], xg[kc // 4].buf), rstd, c0, tg)
            rb = rstd.ap[:, c0:c0 + 512].unsqueeze(1).broadcast_to([128, 4, 512])
            for g in range(4):
                S.op("dve", lambda e, g=g, tg=tg, c0=c0, rb=rb: e.tensor_tensor(
                    out=hT.ap[:, 4 * g:4 * g + 4, c0:c0 + 512], in0=xg[g].ap[:, :, tg * 512:(tg + 1) * 512], in1=rb, op=ALU.mult),
                    reads=[xg[g].buf, rstd.buf], writes=[hT.buf])
        ar.release()


def phase_A(cx, l, x_dram, mine_dst, send_dst, oc_order=None, post_store=None):
    S, ar = cx.S, cx.arena
    ar.mark()
    hT = ar.alloc("hT", [KC, T], BF16)
    gcol = ar.alloc("gpre", [KC], F32)
    rstd = ar.alloc("rstdA", [T], F32)
    S.dma("sp", gcol.ap, cx.d_pre_mix[l], writes=[gcol.buf])
    norm_to_bf16(cx, x_dram, hT, rstd, T, 0)
    ob = [ar.alloc(f"obA{j}", [512], BF16) for j in range(4)]
    cnt = [0]

    def evac(oc, tg, bk):
        o = ob[cnt[0] % 4]
        ps = cx.psum[bk]
        if cnt[0] % 2 == 0:
            S.op("act", lambda e: e.copy(out=o.ap, in_=ps.ap), reads=[ps.buf], writes=[o.buf])
        else:
            S.op("dve", lambda e: e.tensor_copy(out=o.ap, in_=ps.ap), reads=[ps.buf], writes=[o.buf])
        cnt[0] += 1
        dst, db = mine_dst(oc, tg) if oc < NHC else send_dst(oc - NHC, tg)
        S.dma("sp", dst, o.ap, reads=[o.buf], writes=[db])
        if post_store is not None and tg == T // 512 - 1:
            post_store(oc)

    dense(cx, "inproj", cx.d_w_in[l], 2 * NHC, 1, KC, hT, T, evac, scale=gcol, banks=(2, 3, 4, 5, 6, 7), oc_order=oc_order, nbuf=3)
    ar.release()


def make_ctx(nc, stack):
    cx = Ctx()
    cx.nc = nc
    cx.S = Sched(nc)
    cx.dbufs = {}
    NW = 49152
    base = stack.enter_context(nc.sbuf_tensor("arena", [128, NW], F32))
    cx.arena = Arena(base, NW)
    cx.psum = []
    for i in range(8):
        t = stack.enter_context(nc.psum_tensor(f"psb{i}", [128, 512], F32))
        cx.psum.append(Tile(t[:, :], Buf(f"psum{i}")))
    cx.ones = cx.arena.alloc("ones", [128], F32)
    cx.epsc = cx.arena.alloc("epsc", [1], F32)
    cx.sq = [cx.arena.alloc(f"sq{j}", [512], F32) for j in range(2)]
    cx.S.op("pool", lambda e: e.memset(cx.ones.ap, 1.0), writes=[cx.ones.buf])
    cx.S.op("pool", lambda e: e.memset(cx.epsc.ap, EPS), writes=[cx.epsc.buf])
    return cx


def tile_w(w, KG):
    K, N = w.shape
    n_oc, n_kc = N // 128, K // 128
    n_kg = n_kc // KG
    return np.ascontiguousarray(w.reshape(n_kg, KG, 128, n_oc, 128).transpose(3, 0, 2, 1, 4)).reshape(n_oc, n_kg, 128, KG * 128)


def half_cols(h):
    idx = []
    idx += list(range(0 + 512 * h, 0 + 512 * h + 512))
    idx += list(range(1024 + 512 * h, 1024 + 512 * h + 512))
    idx += list(range(2048 + 512 * h, 2048 + 512 * h + 512))
    idx += list(range(3072 + 256 * h, 3072 + 256 * h + 256))
    idx += list(range(3584 + 256 * h, 3584 + 256 * h + 256))
    idx += list(range(4096 + 128 * h, 4096 + 128 * h + 128))
    idx += list(range(4352 + 128 * h, 4352 + 128 * h + 128))
    idx += list(range(4608 + 256 * h, 4608 + 256 * h + 256))
    idx += list(range(5120 + 256 * h, 5120 + 256 * h + 256))
    idx += list(range(5632, 5648))
    return idx


def w_in_layout(w, r):
    out = np.zeros((D, 2 * HC), np.float32)
    out[:, 0:2832] = w[:, half_cols(r)]
    out[:, HC:HC + 2832] = w[:, half_cols(1 - r)]
    return tile_w(out, KC)


def gcol_layout(g):
    return np.ascontiguousarray(g.reshape(-1, 128).T)


def proj_post_residual(cx, name, w_ap, n_kg, KG, A, g_dram_ap, x_in, x_out, t0, h_out=None):
    S, ar = cx.S, cx.arena
    Tn = 1024
    ar.mark()
    rstd = ar.alloc(f"{name}_rstd", [Tn], F32)
    gcol = ar.alloc(f"{name}_g", [KC], F32)
    S.dma("sp", gcol.ap, g_dram_ap, writes=[gcol.buf])
    mo = [ar.alloc(f"{name}_mo{j}", [512], F32) for j in range(3)]
    cnt = [0]
    spill = cx.d_spill

    def evac(oc, tg, bk):
        o = mo[cnt[0] % 3]
        cnt[0] += 1
        ps = cx.psum[bk]
        S.op("act", lambda e: e.copy(out=o.ap, in_=ps.ap), reads=[ps.buf], writes=[o.buf])
        sqt = cx.sq[cnt[0] % 2]
        S.op("dve", lambda e: e.tensor_tensor(out=sqt.ap, in0=ps.ap, in1=o.ap, op=ALU.mult), reads=[ps.buf, o.buf], writes=[sqt.buf])
        acc = cx.psum[tg]
        S.op("pe", lambda e: e.matmul(acc.ap, lhsT=cx.ones.ap, rhs=sqt.ap, start=(oc == 0), stop=(oc == KC - 1)),
             reads=[sqt.buf, cx.ones.buf], writes=[acc.buf])
        S.dma("sp", spill[oc * 128:(oc + 1) * 128, tg * 512:(tg + 1) * 512], o.ap, reads=[o.buf], writes=[dbuf(cx, "spill", oc)])

    dense(cx, name, w_ap, KC, n_kg, KG, A, Tn, evac, banks=(2, 3, 4, 5, 6, 7), nbuf=(3 if n_kg == 1 else 2))
    for tg in range(2):
        r = rstd.ap[:, tg * 512:(tg + 1) * 512]
        acc = cx.psum[tg]
        S.op("act", lambda e, r=r, acc=acc: e.activation(out=r, in_=acc.ap, func=AF.Sqrt, scale=1.0 / D, bias=cx.epsc.ap),
             reads=[acc.buf, cx.epsc.buf], writes=[rstd.buf])
        S.op("dve", lambda e, r=r: e.reciprocal(out=r, in_=r), reads=[rstd.buf], writes=[rstd.buf])
    xp = None
    if h_out is not None:
        xp = ar.alloc(f"{name}_xp", [KC, Tn], F32)
    NB = 3
    mt = [ar.alloc(f"{name}_mt{j}", [Tn], F32) for j in range(NB)]
    xt = [ar.alloc(f"{name}_xt{j}", [Tn], F32) for j in range(NB)]
    xo = [ar.alloc(f"{name}_xo{j}", [Tn], F32) for j in range(NB)] if xp is None else None
    for kc in range(KC):
        m, x = mt[kc % NB], xt[kc % NB]
        S.dma("sp", m.ap, spill[kc * 128:(kc + 1) * 128, :], reads=[dbuf(cx, "spill", kc)], writes=[m.buf])
        S.dma("sp", x.ap, x_in[kc * 128:(kc + 1) * 128, t0:t0 + Tn], reads=[dbuf(cx, "x", x_in.tensor.name)], writes=[x.buf])
        S.op("dve", lambda e, m=m: e.tensor_tensor(out=m.ap, in0=m.ap, in1=rstd.ap, op=ALU.mult), reads=[m.buf, rstd.buf], writes=[m.buf])
        if xp is not None:
            dst_ap, dst_buf = xp.ap[:, kc, :], xp.buf
        else:
            dst_ap, dst_buf = xo[kc % NB].ap, xo[kc % NB].buf
        S.op("dve", lambda e, m=m, x=x, kc=kc, d=dst_ap: e.scalar_tensor_tensor(out=d, in0=m.ap, scalar=gcol.ap[:, kc:kc + 1], in1=x.ap, op0=ALU.mult, op1=ALU.add),
             reads=[m.buf, x.buf, gcol.buf], writes=[dst_buf])
        S.dma("act", x_out[kc * 128:(kc + 1) * 128, t0:t0 + Tn], dst_ap, reads=[dst_buf], writes=[dbuf(cx, "x", x_out.tensor.name)])
    if h_out is not None:
        rstd2 = ar.alloc(f"{name}_rstd2", [Tn], F32)
        for tg in range(2):
            rms_stats(cx, lambda kc, tg=tg: (xp.ap[:, kc, tg * 512:(tg + 1) * 512], xp.buf), rstd2, tg * 512, tg)
            rb = rstd2.ap[:, tg * 512:(tg + 1) * 512].unsqueeze(1).broadcast_to([128, 4, 512])
            for g in range(4):
                S.op("dve", lambda e, g=g, tg=tg, rb=rb: e.tensor_tensor(
                    out=h_out.ap[:, 4 * g:4 * g + 4, tg * 512:(tg + 1) * 512], in0=xp.ap[:, 4 * g:4 * g + 4, tg * 512:(tg + 1) * 512], in1=rb, op=ALU.mult),
                    reads=[xp.buf, rstd2.buf], writes=[h_out.buf])
    ar.release()


def ffn_hidden(cx, l, h2T, hidT):
    S, ar = cx.S, cx.arena
    ar.mark()
    gcol = ar.alloc("gffn", [KC], F32)
    S.dma("sp", gcol.ap, cx.d_pre_ffn[l], writes=[gcol.buf])
    sg = [ar.alloc(f"sg{j}", [512], F32) for j in range(4)]
    state = {}
    cnt = [0]

    def evac(oc2, tg, bk):
        ps = cx.psum[bk]
        if oc2 % 2 == 0:
            s = sg[cnt[0] % 4]
            cnt[0] += 1
            state[tg] = s
            S.op("act", lambda e: e.activation(out=s.ap, in_=ps.ap, func=AF.Silu), reads=[ps.buf], writes=[s.buf])
        else:
            s = state[tg]
            oc = oc2 // 2
            S.op("dve", lambda e: e.tensor_tensor(out=hidT.ap[:, oc, tg * 512:(tg + 1) * 512], in0=ps.ap, in1=s.ap, op=ALU.mult),
                 reads=[ps.buf, s.buf], writes=[hidT.buf])

    dense(cx, "gu", cx.d_w_gu[l], 2 * NFF, 1, KC, h2T, 1024, evac, scale=gcol, banks=(2, 3, 4, 5, 6, 7), nbuf=3)
    ar.release()


def phase_C(cx, l, x_in, x1_dram, x_out, y_load):
    S, ar = cx.S, cx.arena
    for p in range(2):
        t0 = p * 1024
        ar.mark()
        h2T = ar.alloc("h2T", [KC, 1024], BF16)
        ar.mark()
        yT = ar.alloc("ycatT", [KC, 1024], BF16)
        for kc in range(KC):
            y_load(kc, t0, yT)
        proj_post_residual(cx, "op", cx.d_w_out[l], 1, KC, yT, cx.d_post_mix[l], x_in, x1_dram, t0, h_out=h2T)
        ar.release()
        hidT = ar.alloc("hidT", [NFF, 1024], BF16)
        ffn_hidden(cx, l, h2T, hidT)
        proj_post_residual(cx, "dn", cx.d_w_down[l], 2, 22, hidT, cx.d_post_ffn[l], x1_dram, x_out, t0)
        ar.release()


def ycat_half_cols(h):
    return list(range(512 * h, 512 * h + 512)) + list(range(1024 + 256 * h, 1024 + 256 * h + 256)) + list(range(1536 + 256 * h, 1536 + 256 * h + 256))


def w_out_layout(w, r):
    rows = ycat_half_cols(r) + ycat_half_cols(1 - r)
    return tile_w(np.ascontiguousarray(w[rows, :]), KC)


def w_gu_layout(wg, wu):
    tg, tu = tile_w(wg, KC), tile_w(wu, KC)
    out = np.empty((2 * NFF,) + tg.shape[1:], np.float32)
    out[0::2] = tg
    out[1::2] = tu
    return out


def ln_rstd(cx, dst, src_ps, n_feat, eps_t):
    S = cx.S
    S.op("act", lambda e: e.activation(out=dst.ap, in_=src_ps.ap, func=AF.Ln, scale=1.0 / n_feat, bias=eps_t.ap),
         reads=[src_ps.buf, eps_t.buf], writes=[dst.buf])
    S.op("act", lambda e: e.activation(out=dst.ap, in_=dst.ap, func=AF.Exp, scale=-0.5), reads=[dst.buf], writes=[dst.buf])


def load_bcast_vec(cx, name, dram_vec_ap, n):
    t = cx.arena.alloc(name, [n], F32)
    cx.S.dma("sp", t.ap, dram_vec_ap.partition_broadcast(128), writes=[t.buf])
    return t


def attn_alloc(cx):
    ar = cx.arena
    at = {}
    at["raw"] = {nm: ar.alloc(f"a_{nm}", [S_LEN], BF16) for nm in ("q", "k", "v")}
    at["KR"] = ar.alloc("KR", [S_LEN], BF16)
    at["sets"] = [{"QR": ar.alloc(f"QR{j}", [S_LEN], BF16), "KRm": [ar.alloc(f"KRm{j}_{m}", [S_LEN], BF16) for m in range(2)],
                   "Vt": ar.alloc(f"Vtok{j}", [32, 128], BF16)} for j in range(2)]
    at["t1"] = [ar.alloc(f"rt1_{j}", [512], F32) for j in range(2)]
    at["t2"] = [ar.alloc(f"rt2_{j}", [512], F32) for j in range(2)]
    at["pt"] = [ar.alloc(f"pt{j}", [512], BF16) for j in range(6)]
    at["sacc"] = [ar.alloc(f"sacc{j}", [512], F32) for j in range(2)]
    at["rs"] = [ar.alloc(f"rs{j}", [512], F32) for j in range(2)]
    at["tn"] = [ar.alloc(f"tn{j}", [512], F32) for j in range(2)]
    at["o_t"] = ar.alloc("o_t", [512], F32)
    at["sq_t"] = ar.alloc("sq_t", [512], F32)
    at["rstd_t"] = ar.alloc("rstd_t", [512], F32)
    at["yb"] = [ar.alloc(f"yb{j}", [512], BF16) for j in range(2)]
    return at


def attn_prep(cx, hd, cs, at, st):
    S = cx.S
    raw, KR, QR, KRm, Vt = at["raw"], at["KR"], st["QR"], st["KRm"], st["Vt"]
    for nm, r0 in (("q", R_Q), ("k", R_K), ("v", R_V)):
        t = raw[nm]
        for hf in range(2):
            src, sb = cx.seq_src(r0 + hd * 128, 128, hf * 2048, 2048)
            S.dma("sp", t.ap[:, hf * 2048:(hf + 1) * 2048], src, reads=[sb], writes=[t.buf])
    yield
    ps = cx.psum[3]
    i = 0
    for src, dst in ((raw["q"], QR), (raw["k"], KR)):
        for tg in range(8):
            sl = slice(tg * 512, (tg + 1) * 512)
            a, b2 = at["t1"][i % 2], at["t2"][i % 2]
            i += 1
            S.op("pe", lambda e, src=src, sl=sl: e.matmul(ps.ap, lhsT=cs["rm"].ap, rhs=src.ap[:, sl], start=True, stop=True),
                 reads=[cs["rm"].buf, src.buf], writes=[ps.buf])
            S.op("dve", lambda e, a=a, src=src, sl=sl: e.tensor_tensor(out=a.ap, in0=src.ap[:, sl], in1=cs["cos"].ap[:, sl], op=ALU.mult),
                 reads=[src.buf, cs["cos"].buf], writes=[a.buf])
            S.op("dve", lambda e, b2=b2, sl=sl: e.tensor_tensor(out=b2.ap, in0=ps.ap, in1=cs["sin"].ap[:, sl], op=ALU.mult),
                 reads=[ps.buf, cs["sin"].buf], writes=[b2.buf])
            S.op("pool", lambda e, a=a, b2=b2, dst=dst, sl=sl: e.tensor_tensor(out=dst.ap[:, sl], in0=a.ap, in1=b2.ap, op=ALU.add),
                 reads=[a.buf, b2.buf], writes=[dst.buf])
            if dst is KR:
                for m in range(2):
                    S.op("act", lambda e, m=m, sl=sl: e.activation(out=KRm[m].ap[:, sl], in_=KR.ap[:, sl], func=AF.Copy, scale=cs["hm"][m].ap),
                         reads=[KR.buf, cs["hm"][m].buf], writes=[KRm[m].buf])
            yield
    psb = ps.ap.bitcast(BF16)
    for g in range(4):
        for j in range(8):
            tt = g * 8 + j
            S.op("pe", lambda e, j=j, tt=tt: e.transpose(psb[:, j * 128:(j + 1) * 128], raw["v"].ap[:, tt * 128:(tt + 1) * 128], cs["ident"].ap),
                 reads=[raw["v"].buf, cs["ident"].buf], writes=[ps.buf])
        S.op("act", lambda e, g=g: e.copy(out=Vt.ap[:, g * 8:(g + 1) * 8, :], in_=psb.rearrange("p (a b) -> p a b", a=8)),
             reads=[ps.buf], writes=[Vt.buf])
        yield


def attn_main(cx, hd, cs, at, st, bg=None, bg_per_qg=3):
    S = cx.S
    QR, KRm, Vt = st["QR"], st["KRm"], st["Vt"]
    pt, rs, tn, o_t, sq_t, rstd_t, yb, sacc = at["pt"], at["rs"], at["tn"], at["o_t"], at["sq_t"], at["rstd_t"], at["yb"], at["sacc"]
    STB = (0, 1, 2, 6)
    LA = 3

    def blocks_of(qg):
        return [(kt, m) for kt in range(4 * qg + 4) for m in range(2)]

    def geom(qg, kt):
        j = kt - 4 * qg
        q0 = max(j, 0) * 128
        return j, q0, 512 - q0

    def ST(qg, bi):
        kt, m = blocks_of(qg)[bi]
        j, q0, N = geom(qg, kt)
        ps = cx.psum[STB[bi % 4]]
        S.op("pe", lambda e: e.matmul(ps.ap[:, 0:N], lhsT=KRm[m].ap[:, kt * 128:(kt + 1) * 128], rhs=QR.ap[:, qg * 512 + q0:(qg + 1) * 512], start=True, stop=True),
             reads=[KRm[m].buf, QR.buf], writes=[ps.buf])

    for b0 in range(LA):
        ST(0, b0)
    for qg in range(8):
        blocks = blocks_of(qg)
        last_kt = 4 * qg + 3
        for bi, (kt, m) in enumerate(blocks):
            if bi + LA < len(blocks):
                ST(qg, bi + LA)
            j, q0, N = geom(qg, kt)
            ps = cx.psum[STB[bi % 4]]
            p = pt[bi % 6]
            S.op("act", lambda e, ps=ps, p=p, N=N: e.activation(out=p.ap[:, 0:N], in_=ps.ap[:, 0:N], func=AF.Exp, scale=0.125), reads=[ps.buf], writes=[p.buf])
            if j >= 0:
                S.op("pool", lambda e, p=p: e.tensor_tensor(out=p.ap[:, 0:128], in0=p.ap[:, 0:128], in1=cs["tri"].ap, op=ALU.mult),
                     reads=[p.buf, cs["tri"].buf], writes=[p.buf])
            po = cx.psum[4 + m]
            S.op("pe", lambda e, po=po, p=p, kt=kt, q0=q0, N=N, last_kt=last_kt: e.matmul(po.ap[:, q0:512], lhsT=Vt.ap[:, kt, :], rhs=p.ap[:, 0:N], start=(kt == 0), stop=(kt == last_kt)),
                 reads=[Vt.buf, p.buf], writes=[po.buf])
            if m == 0:
                sa = sacc[0]
                if kt == 0:
                    S.op("dve", lambda e, sa=sa, p=p: e.tensor_copy(out=sa.ap, in_=p.ap), reads=[p.buf], writes=[sa.buf])
                else:
                    S.op("dve", lambda e, sa=sa, p=p, q0=q0, N=N: e.tensor_tensor(out=sa.ap[:, q0:512], in0=sa.ap[:, q0:512], in1=p.ap[:, 0:N], op=ALU.add),
                         reads=[p.buf, sa.buf], writes=[sa.buf])
            else:
                psm = cx.psum[7]
                S.op("pe", lambda e, psm=psm, p=p, kt=kt, q0=q0, N=N, last_kt=last_kt: e.matmul(psm.ap[:, q0:512], lhsT=cs["ones_bf"].ap, rhs=p.ap[:, 0:N], start=(kt == 0), stop=(kt == last_kt)),
                     reads=[cs["ones_bf"].buf, p.buf], writes=[psm.buf])
        S.op("pe", lambda e: e.matmul(cx.psum[6].ap, lhsT=cx.ones.ap, rhs=sacc[0].ap, start=True, stop=True),
             reads=[cx.ones.buf, sacc[0].buf], writes=[cx.psum[6].buf])
        if qg + 1 < 8:
            for b0 in range(LA):
                ST(qg + 1, b0)
        for m in range(2):
            S.op("act", lambda e, m=m: e.activation(out=rs[m].ap, in_=cx.psum[6 + m].ap, func=AF.Ln), reads=[cx.psum[6 + m].buf], writes=[rs[m].buf])
            S.op("act", lambda e, m=m: e.activation(out=rs[m].ap, in_=rs[m].ap, func=AF.Exp, scale=-1.0), reads=[rs[m].buf], writes=[rs[m].buf])
            S.op("dve", lambda e, m=m: e.tensor_tensor(out=tn[m].ap, in0=cx.psum[4 + m].ap, in1=rs[m].ap, op=ALU.mult),
                 reads=[cx.psum[4 + m].buf, rs[m].buf], writes=[tn[m].buf])
        S.op("dve", lambda e: e.scalar_tensor_tensor(out=o_t.ap, in0=tn[1].ap, scalar=cs["neg_lam"].ap, in1=tn[0].ap, op0=ALU.mult, op1=ALU.add),
             reads=[tn[0].buf, tn[1].buf, cs["neg_lam"].buf], writes=[o_t.buf])
        S.op("pool", lambda e: e.tensor_tensor(out=sq_t.ap, in0=o_t.ap, in1=o_t.ap, op=ALU.mult), reads=[o_t.buf], writes=[sq_t.buf])
        pn = cx.psum[3]
        S.op("pe", lambda e: e.matmul(pn.ap, lhsT=cx.ones.ap, rhs=sq_t.ap, start=True, stop=True), reads=[cx.ones.buf, sq_t.buf], writes=[pn.buf])
        ln_rstd(cx, rstd_t, pn, 128, cx.epsc)
        y = yb[qg % 2]
        S.op("dve", lambda e, y=y: e.scalar_tensor_tensor(out=y.ap, in0=o_t.ap, scalar=cs["subw"].ap, in1=rstd_t.ap, op0=ALU.mult, op1=ALU.mult),
             reads=[o_t.buf, cs["subw"].buf, rstd_t.buf], writes=[y.buf])
        dst, db = cx.y_dst(Y_ATT + hd * 128, qg * 512, 512)
        S.dma("sp", dst, y.ap, reads=[y.buf], writes=[db])
        if bg is not None:
            for _ in range(bg_per_qg):
                next(bg, None)
    if bg is not None:
        for _ in bg:
            pass


def attention_all(cx, l, cs, after_attn=None):
    S, ar = cx.S, cx.arena
    ar.mark()
    cos = ar.alloc("cos", [S_LEN], F32)
    sin = ar.alloc("sin", [S_LEN], F32)
    for t, d in ((cos, cx.d_cos), (sin, cx.d_sin)):
        for hf in range(2):
            S.dma("sp", t.ap[:, hf * 2048:(hf + 1) * 2048], d[:, hf * 2048:(hf + 1) * 2048], writes=[t.buf])
    cs["cos"], cs["sin"] = cos, sin
    at = attn_alloc(cx)
    for _ in attn_prep(cx, 0, cs, at, at["sets"][0]):
        pass
    for hd in range(4):
        bg = attn_prep(cx, hd + 1, cs, at, at["sets"][(hd + 1) % 2]) if hd < 3 else None
        attn_main(cx, hd, cs, at, at["sets"][hd % 2], bg)
    ar.release()


S_LEN = S

PM_CW, PM_CB, PM_BA, PM_BX, PM_LAM, PM_SUBLN, PM_GNORM, PM_BG = 0, 8, 10, 12, 14, 16, 17, 18
PM_BD = 32
PM_WG = 544
PM_LV = 672
NPM = 928
CB_RM, CB_ID, CB_TRI, CB_ONES, CB_GM = 0, 128, 256, 384, 512


def mixer_consts(cx, l):
    S, ar = cx.S, cx.arena
    cs = {}
    pm = ar.alloc("pm", [NPM], F32)
    S.dma("sp", pm.ap, cx.d_pm[l], writes=[pm.buf])
    cbf = ar.alloc("cbf", [640], BF16)
    S.dma("sp", cbf.ap, cx.d_cbf, writes=[cbf.buf])
    cs["pm"] = pm
    for nm, off in (("rm", CB_RM), ("ident", CB_ID), ("tri", CB_TRI), ("ones_bf", CB_ONES), ("gmask", CB_GM)):
        cs[nm] = Tile(cbf.ap[:, off:off + 128], cbf.buf)
    sm = ar.alloc("smallc", [16], F32)
    cs["sm"] = sm
    onec = ar.alloc("onec", [1], F32)
    S.op("pool", lambda e: e.memset(onec.ap, 1.0), writes=[onec.buf])
    cs["one"] = onec
    lam_init = 0.8 - 0.6 * math.exp(-0.3 * l)
    tmp = ar.alloc("lamtmp", [128], F32)
    for k in range(2):
        a = pm.ap[:, PM_LV + 128 * k:PM_LV + 128 * k + 64]
        b2 = pm.ap[:, PM_LV + 128 * k + 64:PM_LV + 128 * k + 128]
        S.op("dve", lambda e, a=a, b2=b2, k=k: e.tensor_tensor(out=tmp.ap[:, 64 * k:64 * k + 64], in0=a, in1=b2, op=ALU.mult), reads=[pm.buf], writes=[tmp.buf])
        S.op("dve", lambda e, k=k: e.reduce_sum(out=sm.ap[:, k:k + 1], in_=tmp.ap[:, 64 * k:64 * k + 64], axis=mybir.AxisListType.X), reads=[tmp.buf], writes=[sm.buf])
    S.op("act", lambda e: e.activation(out=sm.ap[:, 0:2], in_=sm.ap[:, 0:2], func=AF.Exp), reads=[sm.buf], writes=[sm.buf])
    S.op("dve", lambda e: e.tensor_tensor(out=sm.ap[:, 2:3], in0=sm.ap[:, 1:2], in1=sm.ap[:, 0:1], op=ALU.subtract), reads=[sm.buf], writes=[sm.buf])
    S.op("dve", lambda e: e.tensor_scalar(out=sm.ap[:, 2:3], in0=sm.ap[:, 2:3], scalar1=-lam_init, scalar2=None, op0=ALU.add), reads=[sm.buf], writes=[sm.buf])
    cs["neg_lam"] = Tile(sm.ap[:, 2:3], sm.buf)
    S.op("dve", lambda e: e.tensor_scalar(out=sm.ap[:, 3:4], in0=pm.ap[:, PM_SUBLN:PM_SUBLN + 1], scalar1=1.0 - lam_init, scalar2=None, op0=ALU.mult), reads=[pm.buf], writes=[sm.buf])
    cs["subw"] = Tile(sm.ap[:, 3:4], sm.buf)
    S.op("act", lambda e: e.activation(out=sm.ap[:, 4:6], in_=pm.ap[:, PM_LAM:PM_LAM + 2], func=AF.Exp, scale=-1.0), reads=[pm.buf], writes=[sm.buf])
    S.op("act", lambda e: e.activation(out=sm.ap[:, 4:6], in_=sm.ap[:, 4:6], func=AF.Ln, bias=onec.ap), reads=[sm.buf, onec.buf], writes=[sm.buf])
    S.op("dve", lambda e: e.tensor_scalar(out=sm.ap[:, 6:8], in0=sm.ap[:, 4:6], scalar1=-16.0, scalar2=None, op0=ALU.mult), reads=[sm.buf], writes=[sm.buf])
    S.op("dve", lambda e: e.tensor_scalar(out=sm.ap[:, 4:6], in0=sm.ap[:, 4:6], scalar1=-8.0, scalar2=None, op0=ALU.mult), reads=[sm.buf], writes=[sm.buf])
    S.op("dve", lambda e: e.tensor_scalar(out=sm.ap[:, 8:9], in0=pm.ap[:, PM_BG:PM_BG + 1], scalar1=-1.0, scalar2=None, op0=ALU.mult), reads=[pm.buf], writes=[sm.buf])
    S.op("pool", lambda e: e.memset(sm.ap[0:64, 9:10], 1.0), writes=[sm.buf])
    S.op("pool", lambda e: e.memset(sm.ap[64:128, 9:10], 0.0), writes=[sm.buf])
    S.op("pool", lambda e: e.memset(sm.ap[0:64, 10:11], 0.0), writes=[sm.buf])
    S.op("pool", lambda e: e.memset(sm.ap[64:128, 10:11], 1.0), writes=[sm.buf])
    cs["hm"] = [Tile(sm.ap[:, 9:10], sm.buf), Tile(sm.ap[:, 10:11], sm.buf)]
    return cs


def lru_chunk(cx, l, c, cs):
    S, ar = cx.S, cx.arena
    pm, sm = cs["pm"], cs["sm"]
    BL = 1024
    xgb = [ar.alloc(f"l{c}_xg{j}", [BL], BF16) for j in range(2)]
    xrb = [ar.alloc(f"l{c}_xr{j}", [BL + 8], BF16) for j in range(2)]
    names = ("w1", "gate", "xc", "r", "i", "a", "m", "u", "hs")
    f = {nm: ar.alloc(f"l{c}_{nm}", [BL], F32) for nm in names}
    hprev = ar.alloc(f"l{c}_hprev", [1], F32)
    yb = [ar.alloc(f"l{c}_yb{j}", [BL], BF16) for j in range(2)]
    bda = pm.ap[:, PM_BD + c * 128:PM_BD + (c + 1) * 128]
    bdx = pm.ap[:, PM_BD + (2 + c) * 128:PM_BD + (3 + c) * 128]
    col = lambda off: pm.ap[:, off:off + 1]
    for blk in range(S_LEN // BL):
        t0 = blk * BL
        xg, xr = xgb[blk % 2], xrb[blk % 2]
        src, sb = cx.seq_src(R_LG + c * 128, 128, t0, BL)
        S.dma("sp", xg.ap, src, reads=[sb], writes=[xg.buf])
        if blk == 0:
            S.op("pool", lambda e, xr=xr: e.memset(xr.ap[:, 0:3], 0.0), writes=[xr.buf])
            src, sb = cx.seq_src(R_LX + c * 128, 128, 0, BL)
            S.dma("sp", xr.ap[:, 3:3 + BL], src, reads=[sb], writes=[xr.buf])
        else:
            src, sb = cx.seq_src(R_LX + c * 128, 128, t0 - 3, BL + 3)
            S.dma("sp", xr.ap[:, 0:3 + BL], src, reads=[sb], writes=[xr.buf])
        w1, gate, xc, r, i_, a, m, u, hs = (f[n] for n in names)
        yield
        S.op("act", lambda e, xg=xg: e.activation(out=w1.ap, in_=xg.ap, func=AF.Square), reads=[xg.buf], writes=[w1.buf])
        S.op("dve", lambda e: e.tensor_scalar(out=w1.ap, in0=w1.ap, scalar1=0.044715, scalar2=1.0, op0=ALU.mult, op1=ALU.add), reads=[w1.buf], writes=[w1.buf])
        S.op("dve", lambda e, xg=xg: e.tensor_tensor(out=w1.ap, in0=w1.ap, in1=xg.ap, op=ALU.mult), reads=[w1.buf, xg.buf], writes=[w1.buf])
        S.op("act", lambda e: e.activation(out=w1.ap, in_=w1.ap, func=AF.Sigmoid, scale=1.5957691216057308), reads=[w1.buf], writes=[w1.buf])
        S.op("pool", lambda e, xg=xg: e.tensor_tensor(out=gate.ap, in0=w1.ap, in1=xg.ap, op=ALU.mult), reads=[w1.buf, xg.buf], writes=[gate.buf])
        yield
        S.op("dve", lambda e, xr=xr: e.tensor_scalar(out=xc.ap, in0=xr.ap[:, 3:3 + BL], scalar1=col(PM_CW + 4 * c + 3), scalar2=col(PM_CB + c), op0=ALU.mult, op1=ALU.add),
             reads=[xr.buf, pm.buf], writes=[xc.buf])
        for j in (2, 1, 0):
            S.op("dve", lambda e, xr=xr, j=j: e.scalar_tensor_tensor(out=xc.ap, in0=xr.ap[:, j:j + BL], scalar=col(PM_CW + 4 * c + j), in1=xc.ap, op0=ALU.mult, op1=ALU.add),
                 reads=[xr.buf, pm.buf, xc.buf], writes=[xc.buf])
        yield
        for tg in range(BL // 512):
            sl = slice(tg * 512, (tg + 1) * 512)
            for (bd, bias_off, dst, bk) in ((bda, PM_BA + c, r, 4 * c + tg), (bdx, PM_BX + c, i_, 4 * c + 2 + tg)):
                ps = cx.psum[bk]
                S.op("pe", lambda e, ps=ps, bd=bd, sl=sl: e.matmul(ps.ap, lhsT=bd, rhs=xc.ap[:, sl], start=True, stop=True), reads=[pm.buf, xc.buf], writes=[ps.buf])
                S.op("act", lambda e, ps=ps, dst=dst, sl=sl, bo=bias_off: e.activation(out=dst.ap[:, sl], in_=ps.ap, func=AF.Sigmoid, bias=col(bo)),
                     reads=[ps.buf, pm.buf], writes=[dst.buf])
        yield
        S.op("act", lambda e: e.activation(out=a.ap, in_=r.ap, func=AF.Exp, scale=sm.ap[:, 4 + c:5 + c]), reads=[r.buf, sm.buf], writes=[a.buf])
        S.op("act", lambda e: e.activation(out=m.ap, in_=r.ap, func=AF.Exp, scale=sm.ap[:, 6 + c:7 + c]), reads=[r.buf, sm.buf], writes=[m.buf])
        S.op("act", lambda e: e.activation(out=m.ap, in_=m.ap, func=AF.Sqrt, scale=-1.0, bias=cs["one"].ap), reads=[m.buf, cs["one"].buf], writes=[m.buf])
        S.op("pool", lambda e: e.tensor_tensor(out=u.ap, in0=m.ap, in1=i_.ap, op=ALU.mult), reads=[m.buf, i_.buf], writes=[u.buf])
        S.op("pool", lambda e: e.tensor_tensor(out=u.ap, in0=u.ap, in1=xc.ap, op=ALU.mult), reads=[u.buf, xc.buf], writes=[u.buf])
        yield
        init = 0.0 if blk == 0 else hprev.ap
        S.op("dve", lambda e, init=init: e.tensor_tensor_scan(out=hs.ap, data0=a.ap, data1=u.ap, initial=init, op0=ALU.mult, op1=ALU.add),
             reads=[a.buf, u.buf, hprev.buf], writes=[hs.buf])
        S.op("pool", lambda e: e.tensor_copy(out=hprev.ap, in_=hs.ap[:, BL - 1:BL]), reads=[hs.buf], writes=[hprev.buf])
        y = yb[blk % 2]
        S.op("pool", lambda e, y=y: e.tensor_tensor(out=y.ap, in0=hs.ap, in1=gate.ap, op=ALU.mult), reads=[hs.buf, gate.buf], writes=[y.buf])
        dst, db = cx.y_dst(Y_LRU + c * 128, t0, BL)
        S.dma("sp", dst, y.ap, reads=[y.buf], writes=[db])
        yield


def gla_heads(cx, l, cs):
    S, ar = cx.S, cx.arena
    pm, sm, hm = cs["pm"], cs["sm"], cs["hm"]
    ar.mark()
    qe = ar.alloc("g_qe", [S_LEN], BF16)
    qeh = [ar.alloc(f"g_qe{h}", [S_LEN], BF16) for h in range(2)]
    keh = [ar.alloc(f"g_ke{h}", [S_LEN], BF16) for h in range(2)]
    dec = ar.alloc("g_dec", [64], F32)
    KDh = [ar.alloc(f"g_KDt{h}", [32, 128], BF16) for h in range(2)]
    Vt = [ar.alloc(f"g_Vt{h}", [32, 128], BF16) for h in range(2)]
    ar.mark()
    kd = ar.alloc("g_kd", [S_LEN], BF16)
    ar.mark()
    gq = ar.alloc("g_q", [S_LEN], BF16)
    gk = ar.alloc("g_k", [S_LEN], BF16)
    for t, r0 in ((gq, R_GQ), (gk, R_GK)):
        for hf in range(2):
            src, sb = cx.seq_src(r0, 128, hf * 2048, 2048)
            S.dma("sp", t.ap[:, hf * 2048:(hf + 1) * 2048], src, reads=[sb], writes=[t.buf])
    lrb = [ar.alloc(f"g_lr{j}", [512], BF16) for j in range(2)]
    lrf = [ar.alloc(f"g_lrf{j}", [512], F32) for j in range(2)]
    Bt = ar.alloc("g_Bt", [S_LEN], F32)
    Bc = ar.alloc("g_Bc", [S_LEN], F32)
    E = ar.alloc("g_E", [S_LEN], F32)
    cm = E
    S.op("pool", lambda e: e.memset(cm.ap, 1.0), writes=[cm.buf])
    S.op("pool", lambda e: e.memset(cm.ap.rearrange("p (n c) -> p n c", c=64)[:, :, 0:1], 0.0), writes=[cm.buf])
    wg = pm.ap[0:16, PM_WG:PM_WG + 128]
    for tg in range(8):
        sl = slice(tg * 512, (tg + 1) * 512)
        a, b2 = lrb[tg % 2], lrf[tg % 2]
        src, sb = cx.seq_src(R_GLR, 16, tg * 512, 512)
        S.dma("sp", a.ap[0:16, :], src, reads=[sb], writes=[a.buf])
        S.op("pool", lambda e, a=a, b2=b2: e.tensor_copy(out=b2.ap[0:16, :], in_=a.ap[0:16, :]), reads=[a.buf], writes=[b2.buf])
        ps = cx.psum[tg % 4]
        S.op("pe", lambda e, ps=ps, b2=b2: e.matmul(ps.ap, lhsT=wg, rhs=b2.ap[0:16, :], start=True, stop=True), reads=[pm.buf, b2.buf], writes=[ps.buf])
        S.op("act", lambda e, ps=ps, sl=sl: e.activation(out=Bt.ap[:, sl], in_=ps.ap, func=AF.Exp, scale=-1.0, bias=sm.ap[:, 8:9]), reads=[ps.buf, sm.buf], writes=[Bt.buf])
    S.op("act", lambda e: e.activation(out=Bt.ap, in_=Bt.ap, func=AF.Ln, bias=cs["one"].ap), reads=[Bt.buf, cs["one"].buf], writes=[Bt.buf])
    S.op("dve", lambda e: e.tensor_tensor_scan(out=Bc.ap, data0=cm.ap, data1=Bt.ap, initial=0.0, op0=ALU.mult, op1=ALU.add), reads=[cm.buf, Bt.buf], writes=[Bc.buf])
    S.op("act", lambda e: e.activation(out=E.ap, in_=Bc.ap, func=AF.Exp, scale=-1.0 / 16.0), reads=[Bc.buf], writes=[E.buf])
    S.op("dve", lambda e: e.scalar_tensor_tensor(out=qe.ap, in0=gq.ap, scalar=0.125, in1=E.ap, op0=ALU.mult, op1=ALU.mult), reads=[gq.buf, E.buf], writes=[qe.buf])
    for h in range(2):
        S.op("act", lambda e, h=h: e.activation(out=qeh[h].ap, in_=qe.ap, func=AF.Copy, scale=hm[h].ap), reads=[qe.buf, hm[h].buf], writes=[qeh[h].buf])
    S.op("act", lambda e: e.activation(out=E.ap, in_=Bc.ap, func=AF.Exp, scale=1.0 / 16.0), reads=[Bc.buf], writes=[E.buf])
    for h in range(2):
        S.op("dve", lambda e, h=h: e.scalar_tensor_tensor(out=keh[h].ap, in0=gk.ap, scalar=hm[h].ap, in1=E.ap, op0=ALU.mult, op1=ALU.mult),
             reads=[gk.buf, hm[h].buf, E.buf], writes=[keh[h].buf])
    Bc3 = Bc.ap.rearrange("p (n c) -> p n c", c=64)
    S.op("dve", lambda e: e.tensor_tensor(out=E.ap.rearrange("p (n c) -> p n c", c=64), in0=Bc3, in1=Bc3[:, :, 63:64].broadcast_to([128, 64, 64]), op=ALU.subtract),
         reads=[Bc.buf], writes=[E.buf])
    S.op("act", lambda e: e.activation(out=E.ap, in_=E.ap, func=AF.Exp, scale=1.0 / 16.0), reads=[E.buf], writes=[E.buf])
    S.op("dve", lambda e: e.tensor_tensor(out=kd.ap, in0=gk.ap, in1=E.ap, op=ALU.mult), reads=[gk.buf, E.buf], writes=[kd.buf])
    S.op("act", lambda e: e.activation(out=dec.ap, in_=Bc3[:, :, 63], func=AF.Exp, scale=-1.0 / 16.0), reads=[Bc.buf], writes=[dec.buf])
    ar.release()
    gv = ar.alloc("g_v", [S_LEN], BF16)
    k = 0
    for (srcT, dsts, row0) in ((kd, KDh, None), (gv, [Vt[0]], R_GV), (gv, [Vt[1]], R_GV + 128)):
        if row0 is not None:
            for hf in range(2):
                src, sb = cx.seq_src(row0, 128, hf * 2048, 2048)
                S.dma("sp", gv.ap[:, hf * 2048:(hf + 1) * 2048], src, reads=[sb], writes=[gv.buf])
        for g in range(4):
            ps = cx.psum[4 + k % 2]
            k += 1
            psb = ps.ap.bitcast(BF16)
            for j in range(8):
                tt = g * 8 + j
                S.op("pe", lambda e, psb=psb, j=j, tt=tt, srcT=srcT: e.transpose(psb[:, j * 128:(j + 1) * 128], srcT.ap[:, tt * 128:(tt + 1) * 128], cs["ident"].ap),
                     reads=[srcT.buf, cs["ident"].buf], writes=[ps.buf])
            pv = psb.rearrange("p (a b) -> p a b", a=8)
            if row0 is None:
                for h in range(2):
                    S.op("dve", lambda e, pv=pv, g=g, h=h: e.tensor_scalar(out=KDh[h].ap[:, g * 8:(g + 1) * 8, :], in0=pv, scalar1=hm[h].ap, scalar2=None, op0=ALU.mult),
                         reads=[ps.buf, hm[h].buf], writes=[KDh[h].buf])
            else:
                S.op("act", lambda e, pv=pv, g=g, d=dsts[0]: e.copy(out=d.ap[:, g * 8:(g + 1) * 8, :], in_=pv), reads=[ps.buf], writes=[dsts[0].buf])
    ar.release()
    if globals().get("GLA_STOP", 9) <= 1.5:
        ar.release()
        return
    KV = ar.alloc("g_KV", [64, 128], F32)
    Sst = ar.alloc("g_Sst", [65, 128], F32)
    Spb = ar.alloc("g_Spb", [64, 128], BF16)
    for g in range(16):
        for h2 in range(2):
            ps = cx.psum[(2 * g + h2) % 4]
            hs_ = slice(h2 * 64, (h2 + 1) * 64)
            for q in range(4):
                n = 4 * g + q
                tt, hf = n // 2, n % 2
                S.op("pe", lambda e, ps=ps, h2=h2, q=q, tt=tt, hf=hf: e.matmul(
                    ps.ap[:, q * 128:(q + 1) * 128], lhsT=KDh[hf].ap[:, tt, :], rhs=Vt[h2].ap[:, tt, :], start=True, stop=True),
                    reads=[KDh[hf].buf, Vt[h2].buf], writes=[ps.buf])
            if h2 == 0:
                S.op("act", lambda e, ps=ps, g=g, hs_=hs_: e.copy(out=KV.ap[hs_, 4 * g:4 * g + 4, :], in_=ps.ap[hs_, :].rearrange("p (a b) -> p a b", a=4)), reads=[ps.buf], writes=[KV.buf])
            else:
                S.op("dve", lambda e, ps=ps, g=g, hs_=hs_: e.tensor_copy(out=KV.ap[hs_, 4 * g:4 * g + 4, :], in_=ps.ap[hs_, :].rearrange("p (a b) -> p a b", a=4)), reads=[ps.buf], writes=[KV.buf])
    if globals().get("GLA_STOP", 9) <= 1.7:
        ar.release()
        return
    S.op("pool", lambda e: e.memset(Sst.ap[:, 0, :], 0.0), writes=[Sst.buf])
    for n in range(63):
        S.op("dve", lambda e, n=n: e.scalar_tensor_tensor(out=Sst.ap[:, n + 1, :], in0=Sst.ap[:, n, :], scalar=dec.ap[:, n:n + 1], in1=KV.ap[:, n, :], op0=ALU.mult, op1=ALU.add),
             reads=[Sst.buf, dec.buf, KV.buf], writes=[Sst.buf])
    S.op("pool", lambda e: e.tensor_copy(out=Spb.ap, in_=Sst.ap[:, 0:64, :]), reads=[Sst.buf], writes=[Spb.buf])
    if globals().get("GLA_STOP", 9) <= 2:
        ar.release()
        return
    at4 = [ar.alloc(f"g_at{j}", [512], BF16) for j in range(2)]
    gob = [ar.alloc(f"g_go{j}", [512], BF16) for j in range(2)]
    o_t = ar.alloc("g_o", [512], F32)
    sq_t = ar.alloc("g_sq", [512], F32)
    rstd_t = ar.alloc("g_rstd", [512], F32)
    sil = ar.alloc("g_sil", [512], F32)
    yb = [ar.alloc(f"g_yb{j}", [512], BF16) for j in range(2)]
    it = 0
    for h2 in range(2):
        for g4 in range(8):
            pa, po, pn = cx.psum[it % 2], cx.psum[2 + it % 2], cx.psum[4 + it % 2]
            at, go, y = at4[it % 2], gob[it % 2], yb[it % 2]
            it += 1
            src, sb = cx.seq_src(R_GO + h2 * 128, 128, g4 * 512, 512)
            S.dma("sp", go.ap, src, reads=[sb], writes=[go.buf])
            for j in range(4):
                tt = 4 * g4 + j
                ts = slice(tt * 128, (tt + 1) * 128)
                S.op("pe", lambda e, pa=pa, j=j, ts=ts, h2=h2: e.matmul(pa.ap[:, j * 128:(j + 1) * 128], lhsT=keh[h2].ap[:, ts], rhs=qe.ap[:, ts], start=True, stop=True),
                     reads=[keh[h2].buf, qe.buf], writes=[pa.buf])
            S.op("dve", lambda e, pa=pa, at=at: e.tensor_tensor(out=at.ap.rearrange("p (a b) -> p a b", a=4), in0=pa.ap.rearrange("p (a b) -> p a b", a=4),
                                                                 in1=cs["gmask"].ap.unsqueeze(1).broadcast_to([128, 4, 128]), op=ALU.mult),
                 reads=[pa.buf, cs["gmask"].buf], writes=[at.buf])
            for j in range(4):
                tt = 4 * g4 + j
                S.op("pe", lambda e, po=po, j=j, tt=tt, at=at, h2=h2: e.matmul(po.ap[:, j * 128:(j + 1) * 128], lhsT=Vt[h2].ap[:, tt, :], rhs=at.ap[:, j * 128:(j + 1) * 128], start=True, stop=False),
                     reads=[Vt[h2].buf, at.buf], writes=[po.buf])
                for hf in range(2):
                    n = 2 * tt + hf
                    S.op("pe", lambda e, po=po, j=j, hf=hf, n=n, h2=h2: e.matmul(po.ap[:, j * 128 + hf * 64:j * 128 + hf * 64 + 64], lhsT=Spb.ap[:, n, :], rhs=qeh[h2].ap[:, n * 64:(n + 1) * 64],
                                                                            start=False, stop=(hf == 1)),
                         reads=[Spb.buf, qeh[h2].buf], writes=[po.buf])
            S.op("act", lambda e, po=po: e.copy(out=o_t.ap, in_=po.ap), reads=[po.buf], writes=[o_t.buf])
            S.op("pool", lambda e: e.tensor_tensor(out=sq_t.ap, in0=o_t.ap, in1=o_t.ap, op=ALU.mult), reads=[o_t.buf], writes=[sq_t.buf])
            S.op("pe", lambda e, pn=pn: e.matmul(pn.ap, lhsT=cx.ones.ap, rhs=sq_t.ap, start=True, stop=True), reads=[cx.ones.buf, sq_t.buf], writes=[pn.buf])
            ln_rstd(cx, rstd_t, pn, 128, cx.epsc)
            S.op("act", lambda e, go=go: e.activation(out=sil.ap, in_=go.ap, func=AF.Silu), reads=[go.buf], writes=[sil.buf])
            S.op("dve", lambda e: e.scalar_tensor_tensor(out=o_t.ap, in0=o_t.ap, scalar=pm.ap[:, PM_GNORM:PM_GNORM + 1], in1=rstd_t.ap, op0=ALU.mult, op1=ALU.mult),
                 reads=[o_t.buf, pm.buf, rstd_t.buf], writes=[o_t.buf])
            S.op("dve", lambda e, y=y: e.tensor_tensor(out=y.ap, in0=o_t.ap, in1=sil.ap, op=ALU.mult), reads=[o_t.buf, sil.buf], writes=[y.buf])
            dst, db = cx.y_dst(Y_GLA + h2 * 128, g4 * 512, 512)
            S.dma("sp", dst, y.ap, reads=[y.buf], writes=[db])
    ar.release()


def phase_B(cx, l, after_attn=None):
    S, ar = cx.S, cx.arena
    ar.mark()
    cs = mixer_consts(cx, l)
    attention_all(cx, l, cs)
    if after_attn is not None:
        after_attn()
    ar.mark()
    alive = [lru_chunk(cx, l, c, cs) for c in range(2)]
    while alive:
        for g in list(alive):
            try:
                next(g)
            except StopIteration:
                alive.remove(g)
    ar.release()
    gla_heads(cx, l, cs)
    ar.release()


def host_consts():
    pos = np.arange(S, dtype=np.float32)
    j = np.arange(128) % 64
    inv = (10000.0 ** (-(2.0 * (j % 32)).astype(np.float32) / 64.0)).astype(np.float32)
    ang = inv[:, None] * pos[None, :]
    cos = np.cos(ang).astype(np.float32)
    sin = np.sin(ang).astype(np.float32)
    cb = np.zeros((128, 640), np.float32)
    for m in range(128):
        jj = m % 64
        base = m - jj
        if jj < 32:
            cb[base + jj + 32, CB_RM + m] = -1.0
        else:
            cb[base + jj - 32, CB_RM + m] = 1.0
    cb[:, CB_ID:CB_ID + 128] = np.eye(128)
    kk = np.arange(128)
    cb[:, CB_TRI:CB_TRI + 128] = (kk[:, None] <= kk[None, :])
    cb[:, CB_ONES:CB_ONES + 128] = 1.0
    cb[:, CB_GM:CB_GM + 128] = (kk[:, None] <= kk[None, :]) & ((kk[:, None] // 64) == (kk[None, :] // 64))
    return cos, sin, cb.astype(ml_dtypes.bfloat16)


def pm_layout(P, l, r):
    pm = np.zeros((128, NPM), np.float32)
    for c in range(2):
        ch = slice(256 * r + 128 * c, 256 * r + 128 * (c + 1))
        pm[:, PM_CW + 4 * c:PM_CW + 4 * c + 4] = P['conv_w'][l][:, ch].T
        pm[:, PM_CB + c] = P['conv_b'][l][ch]
        pm[:, PM_BA + c] = P['b_rgate'][l][ch]
        pm[:, PM_BX + c] = P['b_igate'][l][ch]
        pm[:, PM_LAM + c] = P['lru_lambda'][l][ch]
        for k2, wn in ((0, 'w_rgate'), (2, 'w_igate')):
            for bb in range(2):
                blk = 4 * r + 2 * c + bb
                pm[64 * bb:64 * bb + 64, PM_BD + (k2 + c) * 128 + 64 * bb:PM_BD + (k2 + c) * 128 + 64 * bb + 64] = P[wn][l][blk]
    pm[:, PM_SUBLN] = P['diff_subln'][l]
    pm[:, PM_GNORM] = P['gla_norm'][l]
    pm[:, PM_BG] = P['b_gla_gate'][l][128 * r:128 * r + 128]
    pm[0:16, PM_WG:PM_WG + 128] = P['w_gla_gate_up'][l][:, 128 * r:128 * r + 128]
    for k2, nm in enumerate(('lambda_q1', 'lambda_k1', 'lambda_q2', 'lambda_k2')):
        pm[:, PM_LV + 64 * k2:PM_LV + 64 * k2 + 64] = P[nm][l][None, :]
    return pm


SEND_CH = [(0, 512), (512, 512), (1024, 512), (1536, 512), (2048, 512), (2560, 384)]


def build_program():
    from contextlib import ExitStack
    nc = bass.Bass("TRN2", target_bir_lowering=False)
    with ExitStack() as st:
        cx = make_ctx(nc, st)
        S_ = cx.S
        ext = lambda name, shape, dt: (nc.dram_tensor(name, shape, dt).ap() if globals().get("NOEXT") else nc.dram_tensor(name, shape, dt, kind="ExternalInput").ap())
        xT = ext("xT", [D, T], F32)
        def wl(name, shp):
            if globals().get("NOEXT"):
                return [nc.dram_tensor(f"{name}{l}", shp, F32).ap() for l in range(L)]
            t = ext(name, [L] + shp, F32)
            return [t[l] for l in range(L)]
        cx.d_w_in = wl("w_in", [2 * NHC, 1, 128, KC * 128])
        cx.d_w_out = wl("w_out", [KC, 1, 128, KC * 128])
        cx.d_w_gu = wl("w_gu", [2 * NFF, 1, 128, KC * 128])
        cx.d_w_down = wl("w_down", [KC, 2, 128, 22 * 128])
        gains = ext("gains", [L, 4, 128, KC], F32)
        pm = ext("pm", [L, 128, NPM], F32)
        cx.d_cbf = ext("cbf", [128, 640], BF16)
        cx.d_cos = ext("cos", [128, S], F32)
        cx.d_sin = ext("sin", [128, S], F32)
        out = nc.dram_tensor("out", [D, T], F32, kind="ExternalOutput").ap()
        cx.d_pre_mix = [gains[l, 0] for l in range(L)]
        cx.d_post_mix = [gains[l, 1] for l in range(L)]
        cx.d_pre_ffn = [gains[l, 2] for l in range(L)]
        cx.d_post_ffn = [gains[l, 3] for l in range(L)]
        cx.d_pm = [pm[l] for l in range(L)]
        cx.d_spill = nc.dram_tensor("spill", [D, 1024], F32).ap()
        x1d = nc.dram_tensor("x1d", [D, T], F32).ap()
        xa = nc.dram_tensor("xa", [D, T], F32).ap()
        xb = nc.dram_tensor("xb", [D, T], F32).ap()
        SEQR = 3072
        seq = nc.dram_tensor("seq", [SEQR, S], BF16).ap()
        mine1 = nc.dram_tensor("mine1", [HC, T], BF16).ap()
        send1 = nc.dram_tensor("send1", [SEQR, T], BF16).ap()
        recvall = nc.dram_tensor("recvall", [6, 1024, T], BF16).ap()
        yfull = nc.dram_tensor("yfull", [YH, S], BF16).ap()
        ymine = nc.dram_tensor("ymine", [YH, T], BF16).ap()
        ysend = nc.dram_tensor("ysend", [YH, T], BF16).ap()
        recv2all = nc.dram_tensor("recv2all", [2, 1024, T], BF16).ap()
        yoth = nc.dram_tensor("yoth", [YH, T], BF16).ap()
        seqh = seq.rearrange("r (h t) -> h r t", h=2)
        seqhj = seq.rearrange("(j r) (h t) -> h j r t", r=512, h=2)
        rva = recvall.rearrange("j (s r) t -> s j r t", s=2)
        yfh = yfull.rearrange("r (h t) -> h r t", h=2)
        rv2 = recv2all.rearrange("j (s r) t -> s j r t", s=2)
        pars = {}

        def par(e):
            k = id(e)
            if k not in pars:
                pars[k] = e.partition_id() % 2
            return pars[k]

        cx.seq_src = lambda row0, nrows, tok0, n: (seq[row0:row0 + nrows, tok0:tok0 + n], dbuf(cx, "seq", row0 // 128))
        cx.y_dst = lambda row0, tok0, n: (yfull[row0:row0 + 128, tok0:tok0 + n], dbuf(cx, "yfull", row0 // 128))
        mine_dst = lambda oc, tg: (mine1[oc * 128:(oc + 1) * 128, tg * 512:(tg + 1) * 512], dbuf(cx, "mine1", oc))
        send_dst = lambda oc, tg: (send1[oc * 128:(oc + 1) * 128, tg * 512:(tg + 1) * 512], dbuf(cx, "send1", (oc * 128) // 512))

        def y_load(kc, t0, yt):
            if kc < 8:
                S_.dma("sp", yt.ap[:, kc, :], ymine[kc * 128:(kc + 1) * 128, t0:t0 + 1024], reads=[dbuf(cx, "ymine")], writes=[yt.buf])
            else:
                S_.dma("sp", yt.ap[:, kc, :], yoth[(kc - 8) * 128:(kc - 7) * 128, t0:t0 + 1024], reads=[dbuf(cx, "yoth")], writes=[yt.buf])

        def copy_mine(r0, r1):
            S_.dma("pool", lambda e: seqh[bass.ds(par(e), 1), r0:r1, :].rearrange("h r t -> (h r) t"), mine1[r0:r1, :],
                   reads=[dbuf(cx, "mine1", oc) for oc in range(r0 // 128, r1 // 128)], writes=[dbuf(cx, "seq", oc) for oc in range(r0 // 128, r1 // 128)])

        def gather1(j):
            S_.op("pool", lambda e: e.collective_compute("AllGather", ALU.bypass, replica_groups=RG,
                                                          ins=[send1[512 * j:512 * (j + 1), :].opt()], outs=[recvall[j].opt()]),
                  reads=[dbuf(cx, "send1", j)], writes=[dbuf(cx, "recv1", j)], kind="x")

        def copy_recv(j0, j1):
            S_.dma("pool", lambda e: seqhj[bass.ds(1 - par(e), 1), j0:j1, :, :].rearrange("h j r t -> (h j) r t"),
                   lambda e: rva[bass.ds(1 - par(e), 1), j0:j1, :, :].rearrange("s j r t -> (s j) r t"),
                   reads=[dbuf(cx, "recv1", j) for j in range(j0, j1)], writes=[dbuf(cx, "seq", oc) for oc in range(4 * j0, 4 * j1)])

        def post_store_A(oc):
            if oc >= NHC:
                ocs = oc - NHC
                if ocs % 4 == 3 or ocs == NHC - 1:
                    j = ocs // 4
                    gather1(j)
                    if j == 2:
                        copy_recv(0, 3)
                    elif j == 5:
                        copy_recv(3, 6)
            elif oc == 11:
                copy_mine(0, 1536)
            elif oc == NHC - 1:
                copy_mine(1536, HC)

        def exchange_y(j):
            ybufs = [dbuf(cx, "yfull", k) for k in range(4 * j, 4 * j + 4)]
            rs_ = slice(512 * j, 512 * (j + 1))
            S_.dma("pool", ymine[rs_, :], lambda e: yfh[bass.ds(par(e), 1), rs_, :].rearrange("h r t -> (h r) t"), reads=ybufs, writes=[dbuf(cx, "ymine", j)])
            S_.dma("pool", ysend[rs_, :], lambda e: yfh[bass.ds(1 - par(e), 1), rs_, :].rearrange("h r t -> (h r) t"), reads=ybufs, writes=[dbuf(cx, "ysend", j)])
            S_.op("pool", lambda e: e.collective_compute("AllGather", ALU.bypass, replica_groups=RG,
                                                          ins=[ysend[rs_, :].opt()], outs=[recv2all[j].opt()]),
                  reads=[dbuf(cx, "ysend", j)], writes=[dbuf(cx, "recv2", j)], kind="x")
            S_.dma("pool", yoth[rs_, :], lambda e: rv2[bass.ds(1 - par(e), 1), j, :, :].rearrange("s r t -> (s r) t"),
                   reads=[dbuf(cx, "recv2", j)], writes=[dbuf(cx, "yoth", j)])

        def y_load(kc, t0, yt):
            j = (kc % 8) // 4
            if kc < 8:
                S_.dma("sp", yt.ap[:, kc, :], ymine[kc * 128:(kc + 1) * 128, t0:t0 + 1024], reads=[dbuf(cx, "ymine", j)], writes=[yt.buf])
            else:
                S_.dma("sp", yt.ap[:, kc, :], yoth[(kc - 8) * 128:(kc - 7) * 128, t0:t0 + 1024], reads=[dbuf(cx, "yoth", j)], writes=[yt.buf])

        order_A = list(range(NHC, 2 * NHC)) + list(range(NHC))
        x_cur = xT
        for l in range(L):
            phase_A(cx, l, x_cur, mine_dst, send_dst, oc_order=order_A, post_store=post_store_A)
            phase_B(cx, l, after_attn=lambda: exchange_y(0))
            exchange_y(1)
            x_next = out if l == L - 1 else (xa if l % 2 == 0 else xb)
            phase_C(cx, l, x_cur, x1d, x_next, y_load)
            x_cur = x_next
        cx.S.emit()
        n_ops = cx.S.n_inst
    return nc, n_ops


def make_in_maps(inp):
    P = {k: np.asarray(v, dtype=np.float32) for k, v in inp.items()}
    cos, sin, cbf = host_consts()
    w_in = [np.stack([w_in_layout(P['w_in'][l], r) for l in range(L)]) for r in range(2)]
    w_out = [np.stack([w_out_layout(P['w_out'][l], r) for l in range(L)]) for r in range(2)]
    w_gu = np.stack([w_gu_layout(P['w_ffn_gate'][l], P['w_ffn_up'][l]) for l in range(L)])
    w_dn = np.stack([tile_w(P['w_ffn_down'][l], 22) for l in range(L)])
    gains = np.stack([np.stack([gcol_layout(P[nm][l]) for nm in ('pre_mix_norm', 'post_mix_norm', 'pre_ffn_norm', 'post_ffn_norm')]) for l in range(L)])
    pm = [np.stack([pm_layout(P, l, r) for l in range(L)]) for r in range(2)]
    maps = []
    for c in range(NCORES):
        b, r = c // 2, c % 2
        maps.append({
            "xT": np.ascontiguousarray(P['x'][b, r * T:(r + 1) * T, :].T),
            "w_in": w_in[r], "w_out": w_out[r], "w_gu": w_gu, "w_down": w_dn,
            "gains": gains, "pm": pm[r], "cbf": cbf, "cos": cos, "sin": sin,
        })
    return maps


def kernel_fused(**inputs):
    nc, _ = build_program()
    maps = make_in_maps(inputs)
    res = run_bass_kernel_spmd(nc, maps, core_ids=list(range(NCORES)))
    outp = np.empty((B, S, D), np.float32)
    for c in range(NCORES):
        b, r = c // 2, c % 2
        outp[b, r * T:(r + 1) * T, :] = np.asarray(res.results[c]["out"], dtype=np.float32).T
    return outp


def _prog(builder):
    from contextlib import ExitStack
    nc = bass.Bass("TRN2", target_bir_lowering=False)
    with ExitStack() as st:
        cx = make_ctx(nc, st)
        builder(nc, cx)
        cx.S.emit()
    return nc


def build_A():
    def b(nc, cx):
        xT = nc.dram_tensor("xT", [D, T], F32, kind="ExternalInput").ap()
        w_in = nc.dram_tensor("w_in", [2 * NHC, 1, 128, KC * 128], F32, kind="ExternalInput").ap()
        gpre = nc.dram_tensor("gpre", [128, KC], F32, kind="ExternalInput").ap()
        mine = nc.dram_tensor("mine", [HC, T], BF16, kind="ExternalOutput").ap()
        send = nc.dram_tensor("send", [HC, T], BF16, kind="ExternalOutput").ap()
        cx.d_w_in = [w_in]
        cx.d_pre_mix = [gpre]
        md = lambda oc, tg: (mine[oc * 128:(oc + 1) * 128, tg * 512:(tg + 1) * 512], dbuf(cx, "mine", oc))
        sd = lambda oc, tg: (send[oc * 128:(oc + 1) * 128, tg * 512:(tg + 1) * 512], dbuf(cx, "send", oc))
        phase_A(cx, 0, xT, md, sd)
    return _prog(b)


def build_B(l):
    def b(nc, cx):
        seq = nc.dram_tensor("seq", [HC, S], BF16, kind="ExternalInput").ap()
        pm = nc.dram_tensor("pm", [128, NPM], F32, kind="ExternalInput").ap()
        cx.d_pm = {l: pm}
        cx.d_cbf = nc.dram_tensor("cbf", [128, 640], BF16, kind="ExternalInput").ap()
        cx.d_cos = nc.dram_tensor("cos", [128, S], F32, kind="ExternalInput").ap()
        cx.d_sin = nc.dram_tensor("sin", [128, S], F32, kind="ExternalInput").ap()
        yT = nc.dram_tensor("yT", [YH, S], BF16, kind="ExternalOutput").ap()
        cx.seq_src = lambda row0, nrows, tok0, n: (seq[row0:row0 + nrows, tok0:tok0 + n], dbuf(cx, "seq", 0))
        cx.y_dst = lambda row0, tok0, n: (yT[row0:row0 + 128, tok0:tok0 + n], dbuf(cx, "y", row0))
        phase_B(cx, l)
    return _prog(b)


def build_C():
    def b(nc, cx):
        xT = nc.dram_tensor("xT", [D, T], F32, kind="ExternalInput").ap()
        yT = nc.dram_tensor("yT", [D, T], BF16, kind="ExternalInput").ap()
        cx.d_w_out = [nc.dram_tensor("w_out", [KC, 1, 128, KC * 128], F32, kind="ExternalInput").ap()]
        cx.d_w_gu = [nc.dram_tensor("w_gu", [2 * NFF, 1, 128, KC * 128], F32, kind="ExternalInput").ap()]
        cx.d_w_down = [nc.dram_tensor("w_down", [KC, 2, 128, 22 * 128], F32, kind="ExternalInput").ap()]
        g = nc.dram_tensor("gains", [3, 128, KC], F32, kind="ExternalInput").ap()
        cx.d_post_mix, cx.d_pre_ffn, cx.d_post_ffn = [g[0]], [g[1]], [g[2]]
        cx.d_spill = nc.dram_tensor("spill", [D, 1024], F32).ap()
        x1 = nc.dram_tensor("x1", [D, T], F32).ap()
        x2 = nc.dram_tensor("x2", [D, T], F32, kind="ExternalOutput").ap()

        def y_load(kc, t0, yt):
            cx.S.dma("sp", yt.ap[:, kc, :], yT[kc * 128:(kc + 1) * 128, t0:t0 + 1024], writes=[yt.buf])
        phase_C(cx, 0, xT, x1, x2, y_load)
    return _prog(b)


def kernel_multi(**inputs):
    P = {k: np.asarray(v, dtype=np.float32) for k, v in inputs.items()}
    cos, sin, cbf = host_consts()
    cores = list(range(NCORES))
    xs = [np.ascontiguousarray(P['x'][c // 2, (c % 2) * T:(c % 2 + 1) * T, :].T) for c in cores]
    for l in range(L):
        wl = [w_in_layout(P['w_in'][l], r) for r in range(2)]
        gp = gcol_layout(P['pre_mix_norm'][l])
        res = run_bass_kernel_spmd(build_A(), [{"xT": xs[c], "w_in": wl[c % 2], "gpre": gp} for c in cores], core_ids=cores).results
        seqs = []
        for c in cores:
            r = c % 2
            halves = [None, None]
            halves[r] = res[c]["mine"]
            halves[1 - r] = res[c ^ 1]["send"]
            seqs.append(np.ascontiguousarray(np.concatenate(halves, axis=1)))
        del res
        pml = [pm_layout(P, l, r) for r in range(2)]
        res = run_bass_kernel_spmd(build_B(l), [{"seq": seqs[c], "pm": pml[c % 2], "cbf": cbf, "cos": cos, "sin": sin} for c in cores], core_ids=cores).results
        ys = []
        for c in cores:
            r = c % 2
            ys.append(np.ascontiguousarray(np.concatenate([res[c]["yT"][:, r * T:(r + 1) * T], res[c ^ 1]["yT"][:, r * T:(r + 1) * T]], axis=0)))
        del res, seqs
        wo = [w_out_layout(P['w_out'][l], r) for r in range(2)]
        wgu = w_gu_layout(P['w_ffn_gate'][l], P['w_ffn_up'][l])
        wd = tile_w(P['w_ffn_down'][l], 22)
        g3 = np.stack([gcol_layout(P[nm][l]) for nm in ('post_mix_norm', 'pre_ffn_norm', 'post_ffn_norm')])
        res = run_bass_kernel_spmd(build_C(), [{"xT": xs[c], "yT": ys[c], "w_out": wo[c % 2], "w_gu": wgu, "w_down": wd, "gains": g3} for c in cores], core_ids=cores).results
        xs = [np.asarray(res[c]["x2"]) for c in cores]
        del res, ys
    outp = np.empty((B, S, D), np.float32)
    for c in cores:
        outp[c // 2, (c % 2) * T:(c % 2 + 1) * T, :] = xs[c].T
    return outp


def kernel(**inputs):
    return kernel_fused(**inputs)
```
